# Optimizing a Trainium2 kernel written in Bass

```python
import math
import jax, jax.numpy as jnp
from jax import lax
import numpy as np

D_MODEL = 2048
BATCH = 4
SEQ = 4096
DEPTH = 2

SSD_HEADS = 32
SSD_HEAD_DIM = 64
SSD_WIDTH = SSD_HEADS * SSD_HEAD_DIM
SSD_STATE = 128
SSD_GROUPS = 4
SSD_CONV = 4
SSD_CHUNK = 128
SSD_CONV_DIM = SSD_WIDTH + 2 * SSD_GROUPS * SSD_STATE
DT_MIN = 0.001
DT_MAX = 0.1

SB_HEADS = 16
SB_HEAD_DIM = 128
SB_WIDTH = SB_HEADS * SB_HEAD_DIM
SB_BLOCK = 128

IN_DIM = SSD_WIDTH + SSD_CONV_DIM + SSD_HEADS + 3 * SB_WIDTH
MIX_WIDTH = SSD_WIDTH + SB_WIDTH

POOL_WINDOWS = (2, 4, 8, 16)
POOL_GROUP = D_MODEL // len(POOL_WINDOWS)

D_FF = 4 * D_MODEL
EPS = 1e-6

kernel_name = 'hybrid_ssd_stickbreak_pool_trunk'


def rms_norm(x, g):
    xf = x.astype(jnp.float32)
    y = xf * lax.rsqrt(jnp.mean(xf * xf, axis=-1, keepdims=True) + EPS)
    return (y * g.astype(jnp.float32)).astype(x.dtype)


def causal_depthwise_conv(x, w, b):
    k, c = w.shape
    y = lax.conv_general_dilated(
        x, w[:, None, :].astype(x.dtype), window_strides=(1,),
        padding=[(k - 1, 0)], dimension_numbers=('NWC', 'WIO', 'NWC'),
        feature_group_count=c)
    return y + b.astype(x.dtype)


def ssd_scan(x, dt, a, bmat, cmat):
    f32 = jnp.float32
    bsz, t, h, p = x.shape
    g, n = bmat.shape[2], bmat.shape[3]
    r, l = h // g, SSD_CHUNK
    nc = t // l
    xdt = (x.astype(f32) * dt[..., None]).reshape(bsz, nc, l, g, r, p)
    da = (dt * a).reshape(bsz, nc, l, g, r).transpose(0, 3, 4, 1, 2)
    bc = bmat.astype(f32).reshape(bsz, nc, l, g, n)
    cc = cmat.astype(f32).reshape(bsz, nc, l, g, n)
    a_cs = jnp.cumsum(da, axis=-1)
    causal = jnp.tril(jnp.ones((l, l), dtype=bool))
    seg = a_cs[..., :, None] - a_cs[..., None, :]
    decay = jnp.exp(jnp.where(causal, seg, -jnp.inf))
    cb = jnp.einsum('bclgn,bcsgn->bgcls', cc, bc)
    y_diag = jnp.einsum('bgrcls,bcsgrp->bclgrp', decay * cb[:, :, None], xdt)
    decay_to_end = jnp.exp(a_cs[..., -1:] - a_cs)
    chunk_states = jnp.einsum('bcsgn,bgrcs,bcsgrp->bcgrpn', bc, decay_to_end, xdt)
    chunk_decay = jnp.exp(a_cs[..., -1])

    def step(state, inp):
        s_c, d_c = inp
        return state * d_c[..., None, None] + s_c, state

    init = jnp.zeros((bsz, g, r, p, n), f32)
    _, prev = lax.scan(step, init, (jnp.moveaxis(chunk_states, 1, 0),
                                    jnp.moveaxis(chunk_decay, -1, 0)))
    y_off = jnp.einsum('bclgn,cbgrpn->bclgrp', cc, prev) * \
        jnp.exp(a_cs).transpose(0, 3, 4, 1, 2)[..., None]
    return (y_diag + y_off).reshape(bsz, t, h, p)


def stick_breaking_attention(q, k, v):
    bsz, h, t, d = q.shape
    scale = d ** -0.5
    outs = []
    for i in range(t // SB_BLOCK):
        q0 = i * SB_BLOCK
        end = q0 + SB_BLOCK
        z = jnp.einsum('bhqd,bhkd->bhqk', q[:, :, q0:end], k[:, :, :end]).astype(jnp.float32) * scale
        qpos = q0 + jnp.arange(SB_BLOCK)[:, None]
        kpos = jnp.arange(end)[None, :]
        mask = kpos < qpos
        log_beta = jax.nn.log_sigmoid(z)
        log_keep = jnp.where(mask, jax.nn.log_sigmoid(-z), 0.0)
        between = lax.cumsum(log_keep, axis=3, reverse=True) - log_keep
        w = jnp.where(mask, jnp.exp(log_beta + between), 0.0)
        outs.append(jnp.einsum('bhqk,bhkd->bhqd', w.astype(v.dtype), v[:, :, :end]))
    return jnp.concatenate(outs, axis=2)


def hybrid_mixer(h, w_in, conv_w, conv_b, dt_bias, a_log, d_skip, out_norm, q_norm, k_norm, w_out):
    bsz, t, _ = h.shape
    proj = h @ w_in
    cuts = [SSD_WIDTH, SSD_WIDTH + SSD_CONV_DIM, SSD_WIDTH + SSD_CONV_DIM + SSD_HEADS,
            SSD_WIDTH + SSD_CONV_DIM + SSD_HEADS + SB_WIDTH,
            SSD_WIDTH + SSD_CONV_DIM + SSD_HEADS + 2 * SB_WIDTH]
    z, xbc, dt_raw, q, k, v = jnp.split(proj, cuts, axis=-1)

    xbc = jax.nn.silu(causal_depthwise_conv(xbc, conv_w, conv_b))
    xs, bm, cm = jnp.split(xbc, [SSD_WIDTH, SSD_WIDTH + SSD_GROUPS * SSD_STATE], axis=-1)
    xs = xs.reshape(bsz, t, SSD_HEADS, SSD_HEAD_DIM)
    bm = bm.reshape(bsz, t, SSD_GROUPS, SSD_STATE)
    cm = cm.reshape(bsz, t, SSD_GROUPS, SSD_STATE)
    dt = jax.nn.softplus(dt_raw.astype(jnp.float32) + dt_bias.astype(jnp.float32))
    a = -jnp.exp(a_log.astype(jnp.float32))
    y = ssd_scan(xs, dt, a, bm, cm) + d_skip.astype(jnp.float32)[:, None] * xs.astype(jnp.float32)
    gated = y.reshape(bsz, t, SSD_WIDTH) * jax.nn.silu(z.astype(jnp.float32))
    gsz = SSD_WIDTH // SSD_GROUPS
    y_ssd = rms_norm(gated.reshape(bsz, t, SSD_GROUPS, gsz),
                     out_norm.reshape(SSD_GROUPS, gsz)).reshape(bsz, t, SSD_WIDTH)

    def heads(u):
        return u.reshape(bsz, t, SB_HEADS, SB_HEAD_DIM).transpose(0, 2, 1, 3)
    qh = rms_norm(heads(q), q_norm)
    kh = rms_norm(heads(k), k_norm)
    y_sb = stick_breaking_attention(qh, kh, heads(v))
    y_sb = y_sb.transpose(0, 2, 1, 3).reshape(bsz, t, SB_WIDTH)

    merged = jnp.concatenate([y_ssd.astype(h.dtype), y_sb.astype(h.dtype)], axis=-1)
    return merged @ w_out


def multiscale_pool(h, w, b, scale):
    bsz, t, _ = h.shape
    hf = h.astype(jnp.float32)
    cs = jnp.cumsum(hf, axis=1)
    count = jnp.arange(1, t + 1, dtype=jnp.float32)[:, None]
    diffs = []
    for gi, win in enumerate(POOL_WINDOWS):
        sl = slice(gi * POOL_GROUP, (gi + 1) * POOL_GROUP)
        c = cs[..., sl]
        lagged = jnp.pad(c, ((0, 0), (win, 0), (0, 0)))[:, :t]
        mean = (c - lagged) / jnp.minimum(count, float(win))
        diffs.append(mean - hf[..., sl])
    d = jnp.stack(diffs, axis=2).astype(h.dtype)
    y = jnp.einsum('btgc,gcd->btgd', d, w).reshape(bsz, t, D_MODEL) + b
    return y * scale


def sq_relu_mlp(h, w_up, w_down):
    u = jax.nn.relu(h @ w_up)
    return (u * u) @ w_down


def setup_inputs(seed: int = 0) -> dict:
    key = jax.random.key(seed)
    ks = jax.random.split(key, 20)
    ne = (DEPTH + 1) // 2
    no = DEPTH // 2
    f32 = jnp.float32

    def normal(k, shape, s):
        return jax.random.normal(k, shape, f32) * s

    def gain(k, shape):
        return 1.0 + 0.02 * jax.random.normal(k, shape, f32)

    dt0 = jnp.exp(jax.random.uniform(ks[5], (ne, SSD_HEADS), f32, math.log(DT_MIN), math.log(DT_MAX)))
    return {
        'x': normal(ks[0], (BATCH, SEQ, D_MODEL), 1.0),
        'hyb_norm': gain(ks[1], (ne, D_MODEL)),
        'hyb_w_in': normal(ks[2], (ne, D_MODEL, IN_DIM), D_MODEL ** -0.5),
        'ssd_conv_w': normal(ks[3], (ne, SSD_CONV, SSD_CONV_DIM), SSD_CONV ** -0.5),
        'ssd_conv_b': normal(ks[4], (ne, SSD_CONV_DIM), 0.02),
        'ssd_dt_bias': dt0 + jnp.log(-jnp.expm1(-dt0)),
        'ssd_a_log': jnp.log(jax.random.uniform(ks[6], (ne, SSD_HEADS), f32, 1.0, 16.0)),
        'ssd_d': 1.0 + 0.1 * jax.random.normal(ks[7], (ne, SSD_HEADS), f32),
        'ssd_out_norm': gain(ks[8], (ne, SSD_WIDTH)),
        'sb_q_norm': gain(ks[9], (ne, SB_HEAD_DIM)),
        'sb_k_norm': gain(ks[10], (ne, SB_HEAD_DIM)),
        'hyb_w_out': normal(ks[11], (ne, MIX_WIDTH, D_MODEL), MIX_WIDTH ** -0.5),
        'pool_norm': gain(ks[12], (no, D_MODEL)),
        'pool_w': normal(ks[13], (no, len(POOL_WINDOWS), POOL_GROUP, POOL_GROUP), POOL_GROUP ** -0.5),
        'pool_b': normal(ks[14], (no, D_MODEL), 0.02),
        'pool_scale': gain(ks[15], (no, D_MODEL)),
        'mlp_norm': gain(ks[16], (DEPTH, D_MODEL)),
        'mlp_w_up': normal(ks[17], (DEPTH, D_MODEL, D_FF), D_MODEL ** -0.5),
        'mlp_w_down': normal(ks[18], (DEPTH, D_FF, D_MODEL), D_FF ** -0.5),
    }


def reference(x, hyb_norm, hyb_w_in, ssd_conv_w, ssd_conv_b, ssd_dt_bias, ssd_a_log, ssd_d,
              ssd_out_norm, sb_q_norm, sb_k_norm, hyb_w_out, pool_norm, pool_w, pool_b,
              pool_scale, mlp_norm, mlp_w_up, mlp_w_down):
    for layer in range(DEPTH):
        i = layer // 2
        if layer % 2 == 0:
            mix = hybrid_mixer(rms_norm(x, hyb_norm[i]), hyb_w_in[i], ssd_conv_w[i], ssd_conv_b[i],
                               ssd_dt_bias[i], ssd_a_log[i], ssd_d[i], ssd_out_norm[i],
                               sb_q_norm[i], sb_k_norm[i], hyb_w_out[i])
        else:
            mix = multiscale_pool(rms_norm(x, pool_norm[i]), pool_w[i], pool_b[i], pool_scale[i])
        x = x + mix.astype(x.dtype)
        x = x + sq_relu_mlp(rms_norm(x, mlp_norm[layer]), mlp_w_up[layer], mlp_w_down[layer]).astype(x.dtype)
    return x
```

```python
import contextlib
import numpy as np
import concourse.bass as bass
import concourse.mybir as mybir
from concourse.bass_utils import run_bass_kernel_spmd

F32 = mybir.dt.float32
BF16 = mybir.dt.bfloat16
AF = mybir.ActivationFunctionType
ALU = mybir.AluOpType

D = 2048
SEQ = 4096
NT = 32
HALO = 15
OWN0 = 16
NQT = 17
NEG = -30000.0
CDBG = 99
CSC = list(range(8))
EPS = 1e-6
DFF = 8192
IN_DIM = 11296
C_Z, C_X, C_B, C_C, C_DT, C_Q, C_K, C_V = 0, 2048, 4096, 4608, 5120, 5152, 7200, 9248


class Sched:
    STREAMS = ("pe", "act", "dve", "pool", "sp")

    def __init__(self, nc, es):
        self.nc = nc
        self.csem = {s: es.enter_context(nc.semaphore("c_" + s)) for s in ("pe", "act", "dve", "pool")}
        self.ccount = {s: 0 for s in self.csem}
        self.KQ = 6
        self.dsem = {q: [es.enter_context(nc.semaphore("d_%s%d" % (q, i))) for i in range(self.KQ)]
                     for q in ("sp", "pool", "act")}
        self.dval = {q: [0] * self.KQ for q in self.dsem}
        self.dnext = {q: 0 for q in self.dsem}
        self.reset()

    def reset(self):
        self.ops = []
        self.last_w = {}
        self.readers = {}

    @staticmethod
    def _key(a):
        if isinstance(a, str):
            return a
        if a is None or isinstance(a, (int, float)):
            return None
        sp = str(a.space)
        if "DRAM" in sp:
            return None
        return a.tensor.name

    def add(self, stream, fn, reads, writes, sig=True, dma=False):
        rk = [k for k in (self._key(a) for a in reads) if k is not None]
        wk = [k for k in (self._key(a) for a in writes) if k is not None]
        deps = set()
        for k in rk:
            if k in self.last_w:
                deps.add(self.last_w[k])
        for k in wk:
            if k in self.last_w:
                deps.add(self.last_w[k])
            for r in self.readers.get(k, ()):
                deps.add(r)
        i = len(self.ops)
        for d in deps:
            if not (self.ops[d]["stream"] == "pe" and stream == "pe" and not dma):
                self.ops[d]["sig"] = True
        self.ops.append(dict(stream=stream, fn=fn, deps=deps, sig=sig, dma=dma))
        for k in rk:
            self.readers.setdefault(k, []).append(i)
        for k in wk:
            self.last_w[k] = i
            self.readers[k] = []
        return i

    def mm(self, out, lhsT, rhs, start=True, stop=True):
        self.add("pe", lambda e: e.matmul(out, lhsT=lhsT, rhs=rhs, start=start, stop=stop),
                 [lhsT, rhs], [out], sig=stop)

    def tr(self, out, in_, ident):
        self.add("pe", lambda e: e.transpose(out=out, in_=in_, identity=ident), [in_, ident], [out])

    def act(self, out, in_, func, bias=None, scale=None, accum_out=None):
        kw = {}
        if bias is not None:
            kw["bias"] = bias
        if scale is not None:
            kw["scale"] = scale
        if accum_out is not None:
            kw["accum_out"] = accum_out
        self.add("act", lambda e: e.activation(out=out, in_=in_, func=func, **kw),
                 [in_, bias, scale], [out, accum_out])

    def tt(self, eng, out, in0, in1, op):
        self.add(eng, lambda e: e.tensor_tensor(out=out, in0=in0, in1=in1, op=op), [in0, in1], [out])

    def ts(self, eng, out, in0, s1, s2, op0, op1=None):
        if op1 is None:
            self.add(eng, lambda e: e.tensor_scalar(out=out, in0=in0, scalar1=s1, scalar2=None, op0=op0),
                     [in0, s1], [out])
        else:
            self.add(eng, lambda e: e.tensor_scalar(out=out, in0=in0, scalar1=s1, scalar2=s2, op0=op0, op1=op1),
                     [in0, s1, s2], [out])

    def stt(self, eng, out, in0, scalar, in1, op0, op1):
        self.add(eng, lambda e: e.scalar_tensor_tensor(out=out, in0=in0, scalar=scalar, in1=in1, op0=op0, op1=op1),
                 [in0, scalar, in1], [out])

    def copy(self, eng, out, in_):
        if eng == "act":
            self.add("act", lambda e: e.copy(out=out, in_=in_), [in_], [out])
        else:
            self.add(eng, lambda e: e.tensor_copy(out=out, in_=in_), [in_], [out])

    def memset(self, eng, out, val):
        self.add(eng, lambda e: e.memset(out, val), [], [out])

    def dma(self, q, out, in_, extra_r=(), extra_w=(), slow=False):
        if slow:
            fn = lambda e: e.dma_start(out=out, in_=in_, allow_slow_non_contiguous=True)
        else:
            fn = lambda e: e.dma_start(out=out, in_=in_)
        self.add(q, fn, [in_] + list(extra_r), [out] + list(extra_w), dma=True)

    def run(self, name=None):
        nc = self.nc
        ops = self.ops
        per = {s: [i for i, o in enumerate(ops) if o["stream"] == s] for s in self.STREAMS}
        token = [None] * len(ops)
        for s in ("pe", "act", "dve", "pool"):
            lst = [i for i in per[s] if not ops[i]["dma"]]
            if lst:
                ops[lst[-1]]["sig"] = True
            cnt = self.ccount[s]
            vals = {}
            for i in lst:
                if ops[i]["sig"]:
                    cnt += 1
                    vals[i] = cnt
            nxt = None
            for i in reversed(lst):
                if ops[i]["sig"]:
                    nxt = vals[i]
                token[i] = (self.csem[s], nxt)
            self.ccount[s] = cnt
        prevtok = {}
        for s in ("sp", "pool", "act"):
            for i in per[s]:
                if not ops[i]["dma"]:
                    continue
                k = self.dnext[s]
                self.dnext[s] = (k + 1) % self.KQ
                if self.dval[s][k] > 0:
                    prevtok[i] = (self.dsem[s][k], self.dval[s][k])
                self.dval[s][k] += 16
                token[i] = (self.dsem[s][k], self.dval[s][k])
        final_d = {s: [(self.dsem[s][k], self.dval[s][k]) for k in range(self.KQ) if self.dval[s][k] > 0]
                   for s in self.dsem}
        csem_ids = {id(v): k for k, v in self.csem.items()}

        def emit(stream):
            def body(e):
                waited = {}

                def wait(tok):
                    sem, val = tok
                    if waited.get(id(sem), 0) >= val:
                        return
                    waited[id(sem)] = val
                    e.wait_ge(sem, val)
                for i in per[stream]:
                    o = ops[i]
                    for d in sorted(o["deps"]):
                        od = ops[d]
                        if od["stream"] == "pe" and stream == "pe" and not od["dma"] and not o["dma"]:
                            continue
                        wait(token[d])
                    if i in prevtok:
                        wait(prevtok[i])
                    ins = o["fn"](e)
                    sem, val = token[i]
                    if o["dma"]:
                        ins.then_inc(sem, 16)
                    elif o["sig"]:
                        ins.then_inc(sem, 1)
                if stream in final_d:
                    for tok in final_d[stream]:
                        if any(ops[i]["dma"] for i in per[stream]):
                            wait(tok)
            return body

        with nc.Block() as block:
            if per["pe"]:
                block.tensor(emit("pe"))
            if per["act"]:
                block.scalar(emit("act"))
            if per["dve"]:
                block.vector(emit("dve"))
            if per["pool"]:
                block.gpsimd(emit("pool"))
            if per["sp"]:
                block.sync(emit("sp"))
        self.reset()


class Rot:
    def __init__(self, bufs):
        self.bufs = bufs
        self.i = 0

    def next(self):
        b = self.bufs[self.i % len(self.bufs)]
        self.i += 1
        return b


def bcast_rows(ap_1d, n, parts=128):
    return ap_1d.partition_broadcast(parts)


def build_nc(phases="ABCDEF", taps=()):
    nc = bass.Bass("TRN2", target_bir_lowering=False)
    dt_in = lambda name, shape: nc.dram_tensor(name, list(shape), F32, kind="ExternalInput").ap()

    def scratch(name, shape, dtype):
        if name in taps:
            return nc.dram_tensor(name, list(shape), dtype, kind="ExternalOutput").ap()
        return nc.dram_tensor(name, list(shape), dtype).ap()

    _uc = [0]

    def uniq(name):
        _uc[0] += 1
        return "%s_u%d" % (name, _uc[0])

    x_loc = dt_in("x_loc", [SEQ, D])
    flag = dt_in("flag", [128, 1])
    consts = dt_in("consts", [7, 128, 128])
    mask4 = dt_in("mask4", [4, 128, 512])
    poolP = dt_in("poolP", [2, 4, 2, 128, 128])
    hyb_norm = dt_in("hyb_norm", [D])
    w_in = dt_in("hyb_w_in", [D, IN_DIM])
    conv_w = dt_in("ssd_conv_w", [4, 3072])
    conv_b = dt_in("ssd_conv_b", [3072])
    dt_bias = dt_in("ssd_dt_bias", [32])
    a_log = dt_in("ssd_a_log", [32])
    ssd_d = dt_in("ssd_d", [32])
    out_norm = dt_in("ssd_out_norm", [D])
    q_norm = dt_in("sb_q_norm", [128])
    k_norm = dt_in("sb_k_norm", [128])
    w_out = dt_in("hyb_w_out", [4096, D])
    pool_norm = dt_in("pool_norm", [D])
    pool_w = dt_in("pool_w", [2048, 512])
    pool_b = dt_in("pool_b", [D])
    pool_scale = dt_in("pool_scale", [D])
    mlp_norm = dt_in("mlp_norm", [2, D])
    w_up = dt_in("mlp_w_up", [2, D, DFF])
    w_dn = dt_in("mlp_w_down", [2, DFF, D])
    out = nc.dram_tensor("out", [2048, D], F32, kind="ExternalOutput").ap()

    NFM = 36 + 36
    fm_cols = ([C_X + 128 * i for i in range(16)] + [C_B + 128 * i for i in range(4)] +
               [C_K + 128 * i for i in range(16)] + [C_C + 128 * i for i in range(4)] +
               [C_Q + 128 * i for i in range(16)])
    NFM = len(fm_cols)
    Wfm = scratch("Wfm", [NFM, 128, 16, 128], BF16)
    Wtm = scratch("Wtm", [8, 128, 16, 512], BF16)
    Wdt = scratch("Wdt", [128, 16, 32], BF16)
    Wo = scratch("Wo", [4096, D], BF16)
    Wp = scratch("Wp", [2048, 512], BF16)
    Wup = scratch("Wup", [2, 64, 128, 16, 128], BF16)
    Wdn = scratch("Wdn", [2, DFF, D], BF16)
    xc_s = scratch("xc_s", [3072, SEQ], BF16)
    z_s = scratch("z_s", [SEQ, 2048], F32)
    dt_s = scratch("dt_s", [SEQ, 32], F32)
    qT_s = scratch("qT_s", [16, 128, SEQ], BF16)
    kT_s = scratch("kT_s", [16, 128, SEQ], BF16)
    v_s = scratch("v_s", [SEQ, 2048], BF16)
    mT_s = scratch("mT_s", [4096, NQT * 128], BF16)
    x2_s = scratch("x2_s", [NQT * 128, D], F32)

    with contextlib.ExitStack() as es:
        S = Sched(nc, es)
        sb = lambda name, shape, dtype=F32: es.enter_context(nc.sbuf_tensor(name, list(shape), dtype))
        cf = sb("cf", [128, 7, 128])
        cb = sb("cb", [128, 7, 128], BF16)
        flag_t = sb("flag_t", [128, 1])
        kbias = sb("kbias", [128, 1])
        zero_c = sb("zero_c", [128, 1])
        IDENT, ONES, NEGTRI, NEGMASK, TRIU, NEGSSD, NEGONES = range(7)

        S.dma("sp", cf[:], consts.rearrange("k p j -> p k j"))
        S.dma("sp", flag_t[:], flag)
        S.copy("dve", cb[:], cf[:])
        S.ts("dve", kbias[:], flag_t[:], -1.0, -NEG, ALU.add, ALU.mult)
        S.memset("dve", zero_c[:], 0.0)
        S.run()

        pre_jobs, bg_jobs = [], []
        for bi, c0 in enumerate(fm_cols):
            pre_jobs.append((Wfm[bi], w_in[:, c0:c0 + 128].rearrange("(c p) j -> p c j", p=128), [128, 16, 128]))
        for bi in range(8):
            c0 = (C_Z if bi < 4 else C_V) + 512 * (bi % 4)
            for ch in range(4):
                pre_jobs.append((Wtm[bi, :, 4 * ch:4 * ch + 4, :],
                                 w_in[512 * ch:512 * ch + 512, c0:c0 + 512].rearrange("(c p) j -> p c j", p=128),
                                 [128, 4, 512]))
        pre_jobs.append((Wdt[:], w_in[:, C_DT:C_DT + 32].rearrange("(c p) j -> p c j", p=128), [128, 16, 32]))
        for r in range(32):
            bg_jobs.append((Wo[128 * r:128 * r + 128, :], w_out[128 * r:128 * r + 128, :], [128, 2048]))
        for l in range(2):
            for fb in range(64):
                bg_jobs.append((Wup[l, fb], w_up[l, :, 128 * fb:128 * fb + 128].rearrange("(c p) j -> p c j", p=128),
                                [128, 16, 128]))
            for r in range(64):
                bg_jobs.append((Wdn[l, 128 * r:128 * r + 128, :], w_dn[l, 128 * r:128 * r + 128, :], [128, 2048]))
            if l == 0:
                for r in range(4):
                    bg_jobs.append((Wp[512 * r:512 * r + 512, :].rearrange("(c p) j -> p c j", p=128),
                                    pool_w[512 * r:512 * r + 512, :].rearrange("(c p) j -> p c j", p=128), [128, 4, 512]))

        def conv_job(cv, dst, src, shape):
            t = cv.next()
            n = 1
            for v_ in shape[1:]:
                n *= v_
            view = t[:, 0:n]
            if len(shape) == 3:
                view = view.rearrange("p (a b) -> p a b", a=shape[1])
            S.dma("pool", view, src)
            S.dma("sp", dst, view)

        if "A" in phases:
            with contextlib.ExitStack() as es2:
                cv = Rot([es2.enter_context(nc.sbuf_tensor(uniq("cv%d" % i), [128, 2048], BF16)) for i in range(4)])
                for j in pre_jobs:
                    conv_job(cv, *j)
                if "D" not in phases:
                    for j in bg_jobs:
                        conv_job(cv, *j)
                S.run()

        def norm_transpose(xt_tiles, gT, xnT, scr):
            for i, xt in enumerate(xt_tiles):
                ss = scr["ss"].next()
                S.act(scr["junk"][:], xt, AF.Square, accum_out=ss[:, 0:1])
                S.act(ss[:, 1:2], ss[:, 0:1], AF.Ln, bias=scr["eps"][:], scale=1.0 / D)
                S.act(ss[:, 2:3], ss[:, 1:2], AF.Exp, scale=-0.5)
                xn = scr["xn"].next()
                S.ts("dve", xn[:], xt, ss[:, 2:3], None, ALU.mult)
                for c4 in range(4):
                    pT = scr["pT"].next()
                    for j in range(4):
                        c = 4 * c4 + j
                        S.tr(pT[:, j, :], xn[:, 128 * c:128 * c + 128], cb[:, IDENT, :])
                    eng = "pool" if False else "dve"
                    S.tt(eng, xnT[:, 4 * c4:4 * c4 + 4, 128 * i:128 * i + 128], pT[:],
                         gT[:, 4 * c4:4 * c4 + 4].unsqueeze(2).to_broadcast([128, 4, 128]), ALU.mult)

        if "B" in phases:
            with contextlib.ExitStack() as es2:
                sb2 = lambda name, shape, dtype=F32: es2.enter_context(nc.sbuf_tensor(uniq(name), list(shape), dtype))
                ps2 = lambda name, shape, dtype=F32: es2.enter_context(nc.psum_tensor(uniq(name), list(shape), dtype))
                gT = sb2("gT", [128, 16])
                gq = sb2("gq", [128, 1])
                gk = sb2("gk", [128, 1])
                epsc = sb2("epsc", [128, 1])
                wdt = sb2("wdt", [128, 16, 32], BF16)
                xt = Rot([sb2("xt%d" % i, [128, 2048]) for i in range(3)])
                scr = dict(ss=Rot([sb2("ss%d" % i, [128, 4]) for i in range(4)]),
                           junk=sb2("junk", [128, 2048], BF16), eps=epsc,
                           xn=Rot([sb2("xn%d" % i, [128, 2048], BF16) for i in range(2)]),
                           pT=Rot([ps2("pT%d" % i, [128, 4, 128], BF16) for i in range(2)]))
                xnT = Rot([sb2("xnT%d" % i, [128, 16, 512], BF16) for i in range(2)])
                wblk = Rot([sb2("wblk%d" % i, [128, 16, 128], BF16) for i in range(3)])
                wtm = Rot([sb2("wtm%d" % i, [128, 16, 512], BF16) for i in range(3)])
                ps = Rot([ps2("ps%d" % i, [128, 512]) for i in range(4)])
                pss = Rot([ps2("pss%d" % i, [128, 512]) for i in range(2)])
                of = Rot([sb2("of%d" % i, [128, 512]) for i in range(3)])
                ob = Rot([sb2("ob%d" % i, [128, 512], BF16) for i in range(3)])
                sq = Rot([sb2("sq%d" % i, [128, 512], BF16) for i in range(3)])
                rs = Rot([sb2("rs%d" % i, [128, 512]) for i in range(2)])
                cwTb = sb2("cwTb", [128, 4, 24]); cbTb = sb2("cbTb", [128, 24])
                halo_t = sb2("halo_t", [128, 24, 3])
                xiR = Rot([sb2("xiR%d" % i, [128, 516]) for i in range(3)])
                accR = Rot([sb2("accR%d" % i, [128, 512]) for i in range(3)])
                for k in range(4):
                    S.dma("sp", cwTb[:, k, :], conv_w[k, :].rearrange("(c p) -> p c", p=128), slow=True)
                S.dma("sp", cbTb[:], conv_b.rearrange("(c p) -> p c", p=128), slow=True)
                S.memset("dve", halo_t[:], 0.0)
                S.dma("sp", gT[:], hyb_norm.rearrange("(c p) -> p c", p=128), slow=True)
                S.dma("sp", gq[:], q_norm.rearrange("(p o) -> p o", o=1), slow=True)
                S.dma("sp", gk[:], k_norm.rearrange("(p o) -> p o", o=1), slow=True)
                S.ts("dve", gq[:], gq[:], 128.0 ** -0.5, None, ALU.mult)
                S.memset("dve", epsc[:], EPS)
                S.dma("sp", wdt[:], Wdt[:])
                jobs = []
                pend_tail = []
                STQ = "pool"
                for g in range(8):
                    T0 = 512 * g
                    xT = xnT.next()
                    for i in range(4):
                        t = xt.next()
                        jobs.append((lambda t=t, r0=T0 + 128 * i: S.dma("sp", t[:], x_loc[r0:r0 + 128, :]),
                                     lambda t=t, xT=xT, i=i: norm_transpose([t[:]], gT, xT[:, :, 128 * i:128 * i + 128], scr)))
                    nblk = 36 if g < 3 else NFM

                    def fm_compute(w, bi, xT=xT, T0=T0):
                        p = ps.next()
                        for c in range(16):
                            S.mm(p[:], w[:, c, :], xT[:, c, :], start=(c == 0), stop=(c == 15))
                        if pend_tail:
                            pend_tail.pop(0)()
                        c0 = fm_cols[bi]
                        if c0 >= C_Q:
                            isq = c0 < C_K
                            hd = (c0 - (C_Q if isq else C_K)) // 128
                            s_ = sq.next()
                            S.act(s_[:], p[:], AF.Square)

                            def tail(p=p, s_=s_, isq=isq, hd=hd):
                                p2 = pss.next()
                                S.mm(p2[:], cb[:, ONES, :], s_[:])
                                r = rs.next()
                                S.act(r[:], p2[:], AF.Ln, bias=epsc[:], scale=1.0 / 128)
                                S.act(r[:], r[:], AF.Exp, scale=-0.5)
                                o = ob.next()
                                S.stt("dve", o[:], p[:], (gq if isq else gk)[:, 0:1], r[:], ALU.mult, ALU.mult)
                                S.dma(STQ, (qT_s if isq else kT_s)[hd, :, T0:T0 + 512], o[:])
                            pend_tail.append(tail)
                        else:
                            cc = (c0 - C_X) // 128
                            xi = xiR.next()
                            S.copy("act", xi[:, 4:516], p[:])
                            S.copy("dve", xi[:, 1:4], halo_t[:, cc, :])
                            S.copy("dve", halo_t[:, cc, :], xi[:, 513:516])
                            ac = accR.next()
                            S.ts("dve", ac[:], xi[:, 1:513], cwTb[:, 0, cc:cc + 1], cbTb[:, cc:cc + 1], ALU.mult, ALU.add)
                            for k in range(1, 4):
                                S.stt("dve", ac[:], xi[:, 1 + k:1 + k + 512], cwTb[:, k, cc:cc + 1], ac[:], ALU.mult, ALU.add)

                            def tail(ac=ac, cc=cc):
                                o = ob.next()
                                S.act(o[:], ac[:], AF.Silu)
                                S.dma(STQ, xc_s[128 * cc:128 * cc + 128, T0:T0 + 512], o[:])
                            pend_tail.append(tail)

                    for bi in range(nblk):
                        w = wblk.next()
                        jobs.append((lambda w=w, bi=bi: S.dma("sp", w[:], Wfm[bi]),
                                     lambda w=w, bi=bi, f=fm_compute: f(w, bi)))

                    def tm_compute(w, bi, xT=xT, T0=T0):
                        while pend_tail:
                            pend_tail.pop(0)()
                        for i in range(4):
                            p = ps.next()
                            for c in range(16):
                                S.mm(p[:], xT[:, c, 128 * i:128 * i + 128], w[:, c, :], start=(c == 0), stop=(c == 15))
                            r0 = T0 + 128 * i
                            if bi < 4:
                                o = of.next()
                                S.copy("dve" if i % 2 else "act", o[:], p[:])
                                S.dma(STQ, z_s[r0:r0 + 128, 512 * bi:512 * bi + 512], o[:])
                            else:
                                o = ob.next()
                                S.copy("dve" if i % 2 else "act", o[:], p[:])
                                S.dma(STQ, v_s[r0:r0 + 128, 512 * (bi - 4):512 * (bi - 4) + 512], o[:])

                    for bi in range(8):
                        if bi < 4 and g < 3:
                            continue
                        w = wtm.next()
                        jobs.append((lambda w=w, bi=bi: S.dma("sp", w[:], Wtm[bi]),
                                     lambda w=w, bi=bi, f=tm_compute: f(w, bi)))

                    def dt_compute(xT=xT, T0=T0):
                        for i in range(4):
                            p = ps.next()
                            for c in range(16):
                                S.mm(p[:, 0:32], xT[:, c, 128 * i:128 * i + 128], wdt[:, c, :], start=(c == 0), stop=(c == 15))
                            o = of.next()
                            S.copy("dve", o[:, 0:32], p[:, 0:32])
                            S.dma(STQ, dt_s[T0 + 128 * i:T0 + 128 * i + 128, :], o[:, 0:32])
                    jobs.append((lambda: None, dt_compute))
                DEPTH = 2
                for j in range(min(DEPTH, len(jobs))):
                    jobs[j][0]()
                for j in range(len(jobs)):
                    if j + DEPTH < len(jobs):
                        jobs[j + DEPTH][0]()
                    jobs[j][1]()
                S.run()

        if "C" in phases:
            with contextlib.ExitStack() as es2:
                sb2 = lambda name, shape, dtype=F32: es2.enter_context(nc.sbuf_tensor(uniq(name), list(shape), dtype))
                ps2 = lambda name, shape, dtype=F32: es2.enter_context(nc.psum_tensor(uniq(name), list(shape), dtype))
                cwT = sb2("cwT", [128, 4, 24]); cbT = sb2("cbT", [128, 24])
                dtb = sb2("dtb", [128, 32]); a_b = sb2("a_b", [128, 32]); D_b = sb2("D_b", [128, 32])
                on_b = sb2("on_b", [128, 2048]); epsc = sb2("epsc", [128, 1])
                stT = [sb2("stT%d" % g, [128, 512]) for g in range(4)]
                stB = [sb2("stB%d" % g, [128, 512], BF16) for g in range(4)]
                xcR = Rot([sb2("xc%d" % i, [128, 24, 512], BF16) for i in range(2)])
                xs_tm = sb2("xs_tm", [128, 4, 2048], BF16)
                B_tm = sb2("B_tm", [128, 4, 512], BF16)
                xdt = sb2("xdt", [128, 4, 2048], BF16)
                xdts = sb2("xdts", [128, 4, 2048], BF16)
                dtv = sb2("dtv", [128, 4, 32]); da = sb2("da", [128, 4, 32]); acs = sb2("acs", [128, 4, 32])
                nacs = sb2("nacs", [128, 4, 32]); cdb = sb2("cdb", [128, 4, 32]); dte = sb2("dte", [128, 4, 32])
                ea = sb2("ea", [128, 4, 32]); w1 = sb2("w1", [128, 4, 32])
                cbm = Rot([sb2("cbm%d" % i, [128, 4, 128]) for i in range(2)])
                zt = Rot([sb2("zt%d" % i, [128, 2048]) for i in range(1)])
                sz = Rot([sb2("sz%d" % i, [128, 2048]) for i in range(1)])
                tda = Rot([sb2("tda%d" % i, [128, 128]) for i in range(4)])
                dec = Rot([sb2("dec%d" % i, [128, 128]) for i in range(4)])
                MT = Rot([sb2("MT%d" % i, [128, 128], BF16) for i in range(3)])
                t1 = Rot([sb2("t1_%d" % i, [128, 512]) for i in range(2)])
                t2 = Rot([sb2("t2_%d" % i, [128, 512]) for i in range(2)])
                ssq = Rot([sb2("ssq%d" % i, [128, 4]) for i in range(3)])
                junk = sb2("junkc", [128, 512], BF16)
                yn = Rot([sb2("yn%d" % i, [128, 512], BF16) for i in range(2)])
                yT = Rot([sb2("yT%d" % i, [128, 4, 128], BF16) for i in range(2)])
                cb_ps = ps2("cb_ps", [128, 4, 128])
                y_ps = Rot([ps2("y_ps%d" % i, [128, 512]) for i in range(1)])
                yo_ps = Rot([ps2("yo_ps%d" % i, [128, 512]) for i in range(1)])
                s_ps = Rot([ps2("s_ps%d" % i, [128, 512]) for i in range(1)])
                pT = Rot([ps2("pTc%d" % i, [128, 4, 128], BF16) for i in range(1)])
                at_ps = ps2("at_ps", [128, 2, 128]); acs_ps = at_ps[:, 0, :]; tot_ps = at_ps[:, 1, :]
                seg_ps = Rot([ps2("seg_ps%d" % i, [128, 128]) for i in range(2)])

                for k in range(4):
                    S.dma("sp", cwT[:, k, :], conv_w[k, :].rearrange("(c p) -> p c", p=128), slow=True)
                S.dma("sp", cbT[:], conv_b.rearrange("(c p) -> p c", p=128), slow=True)
                S.dma("sp", dtb[:], dt_bias.partition_broadcast(128))
                S.dma("sp", a_b[:], a_log.partition_broadcast(128))
                S.dma("sp", D_b[:], ssd_d.partition_broadcast(128))
                S.dma("sp", on_b[:], out_norm.partition_broadcast(128))
                S.act(a_b[:], a_b[:], AF.Exp)
                S.ts("dve", a_b[:], a_b[:], -1.0, None, ALU.mult)
                S.memset("dve", epsc[:], EPS)
                for g in range(4):
                    S.memset("dve", stT[g][:], 0.0)
                    S.memset("pool", stB[g][:], 0.0)
                ei = [0]

                def alt():
                    ei[0] += 1
                    return "dve" if ei[0] % 2 else "pool"

                xc_next = xcR.next()
                S.dma("sp", xc_next[:], xc_s[:, 0:512].rearrange("(c p) t -> p c t", p=128))
                for SC in CSC:
                    T0 = 512 * SC
                    xc = xc_next
                    if SC + 1 < 8:
                        xc_next = xcR.next()
                        S.dma("sp", xc_next[:], xc_s[:, T0 + 512:T0 + 1024].rearrange("(c p) t -> p c t", p=128))
                    if CDBG < 2:
                        continue
                    for ch in range(4):
                        for c4 in range(5):
                            p = pT.next()
                            for j in range(4):
                                S.tr(p[:, j, :], xc[:, 4 * c4 + j, 128 * ch:128 * ch + 128], cb[:, IDENT, :])
                            dst = xs_tm[:, ch, 512 * c4:512 * c4 + 512] if c4 < 4 else B_tm[:, ch, :]
                            S.copy("act" if c4 % 2 else "dve", dst, p[:].rearrange("p a b -> p (a b)"))
                    if CDBG < 2.05:
                        continue
                    S.dma("sp", dtv[:], dt_s[T0:T0 + 512, :].rearrange("(c p) h -> p c h", p=128))
                    S.tt("dve", dtv[:], dtv[:], dtb[:].unsqueeze(1).to_broadcast([128, 4, 32]), ALU.add)
                    S.act(dtv[:], dtv[:], AF.Exp)
                    S.act(dtv[:], dtv[:], AF.Ln, bias=1.0)
                    if CDBG == 2.1:
                        continue
                    S.tt("dve", da[:], dtv[:], a_b[:].unsqueeze(1).to_broadcast([128, 4, 32]), ALU.mult)
                    daf = da[:].rearrange("p c h -> p (c h)")
                    S.mm(acs_ps, cf[:, TRIU, :], daf)
                    S.mm(tot_ps, cf[:, ONES, :], daf)
                    if CDBG == 2.2:
                        continue
                    S.copy("dve", acs[:].rearrange("p c h -> p (c h)"), acs_ps)
                    S.ts("dve", nacs[:], acs[:], -1.0, None, ALU.mult)
                    S.copy("dve", w1[:].rearrange("p c h -> p (c h)"), tot_ps)
                    S.act(cdb[:], w1[:], AF.Exp)
                    S.tt("dve", dte[:], w1[:], acs[:], ALU.subtract)
                    S.act(dte[:], dte[:], AF.Exp)
                    S.act(ea[:], acs[:], AF.Exp)
                    S.tt("dve", w1[:], dtv[:], dte[:], ALU.mult)
                    for ch in range(4 if CDBG != 3 else 0):
                        xv = xs_tm[:, ch, :].rearrange("p (h q) -> p h q", q=64)
                        S.tt("dve", xdt[:, ch, :].rearrange("p (h q) -> p h q", q=64), xv,
                             dtv[:, ch, :].unsqueeze(2).to_broadcast([128, 32, 64]), ALU.mult)
                        S.tt("pool", xdts[:, ch, :].rearrange("p (h q) -> p h q", q=64), xv,
                             w1[:, ch, :].unsqueeze(2).to_broadcast([128, 32, 64]), ALU.mult)
                    if CDBG < 4:
                        continue
                    for ch in range(4):
                        lc = 4 * SC + ch
                        cs = slice(128 * ch, 128 * ch + 128)
                        if lc == OWN0:
                            for g in range(4):
                                S.ts("dve", stT[g][:], stT[g][:], flag_t[:, 0:1], None, ALU.mult)
                                S.copy("pool", stB[g][:], stT[g][:])
                        if lc >= HALO:
                            for g in range(4):
                                S.mm(cb_ps[:, g, :], xc[:, 16 + g, cs], xc[:, 20 + g, cs])
                            cm = cbm.next()
                            S.copy("dve", cm[:], cb_ps[:])
                            z_t = zt.next()
                            S.dma("sp", z_t[:], z_s[T0 + 128 * ch:T0 + 128 * ch + 128, :])
                            s_z = sz.next()
                            S.act(s_z[:], z_t[:], AF.Silu)
                            for g in range(4):
                                yp = y_ps.next(); yo = yo_ps.next()
                                S.mm(yo[:], xc[:, 20 + g, cs], stB[g][:])
                                a2 = t2.next()
                                S.tt("pool", a2[:].rearrange("p (h q) -> p h q", q=64),
                                     xs_tm[:, ch, 512 * g:512 * g + 512].rearrange("p (h q) -> p h q", q=64),
                                     D_b[:, 8 * g:8 * g + 8].unsqueeze(2).to_broadcast([128, 8, 64]), ALU.mult)
                                def s1(hh, g=g, ch=ch):
                                    h = 8 * g + hh
                                    td = tda.next()
                                    S.ts("dve", td[:], cf[:, TRIU, :], da[:, ch, h:h + 1], None, ALU.mult)
                                    sg = seg_ps.next()
                                    S.mm(sg[:], cf[:, ONES, :], td[:], start=True, stop=False)
                                    S.mm(sg[:], cf[:, IDENT, :], cf[:, NEGSSD, :], start=False, stop=True)
                                    dc = dec.next()
                                    S.act(dc[:], sg[:], AF.Exp, bias=nacs[:, ch, h:h + 1])
                                    return dc

                                def s2(hh, dc, g=g, ch=ch, yp=yp, cm=cm):
                                    h = 8 * g + hh
                                    m = MT.next()
                                    S.tt("dve", m[:], dc[:], cm[:, g, :], ALU.mult)
                                    S.mm(yp[:, 64 * hh:64 * hh + 64], m[:], xdt[:, ch, 64 * h:64 * h + 64])
                                dcs = {0: s1(0), 1: s1(1)}
                                for hh in range(8):
                                    if hh + 2 < 8:
                                        dcs[hh + 2] = s1(hh + 2)
                                    s2(hh, dcs.pop(hh))
                                a1 = t1.next()
                                S.tt("dve", a1[:].rearrange("p (h q) -> p h q", q=64), yo[:].rearrange("p (h q) -> p h q", q=64),
                                     ea[:, ch, 8 * g:8 * g + 8].unsqueeze(2).to_broadcast([128, 8, 64]), ALU.mult)
                                S.tt("dve", a1[:], a1[:], yp[:], ALU.add)
                                S.tt("dve", a1[:], a1[:], a2[:], ALU.add)
                                S.tt("dve", a1[:], a1[:], s_z[:, 512 * g:512 * g + 512], ALU.mult)
                                sq_ = ssq.next()
                                S.act(junk[:], a1[:], AF.Square, accum_out=sq_[:, 0:1])
                                S.act(sq_[:, 1:2], sq_[:, 0:1], AF.Ln, bias=epsc[:], scale=1.0 / 512)
                                S.act(sq_[:, 2:3], sq_[:, 1:2], AF.Exp, scale=-0.5)
                                y_n = yn.next()
                                S.stt("dve", y_n[:], a1[:], sq_[:, 2:3], on_b[:, 512 * g:512 * g + 512], ALU.mult, ALU.mult)
                                p = pT.next()
                                for j in range(4):
                                    S.tr(p[:, j, :], y_n[:, 128 * j:128 * j + 128], cb[:, IDENT, :])
                                y_t = yT.next()
                                S.copy("act", y_t[:], p[:])
                                col0 = (lc - HALO) * 128
                                S.dma("sp", mT_s[512 * g:512 * g + 512, col0:col0 + 128].rearrange("(j p) t -> p j t", p=128), y_t[:])
                        if lc < NT - 1:
                            for g in range(4):
                                sp_ = s_ps.next()
                                S.mm(sp_[:], B_tm[:, ch, 128 * g:128 * g + 128], xdts[:, ch, 512 * g:512 * g + 512])
                                S.tt("dve", stT[g][:].rearrange("p (h q) -> p h q", q=64), stT[g][:].rearrange("p (h q) -> p h q", q=64),
                                     cdb[:, ch, 8 * g:8 * g + 8].unsqueeze(2).to_broadcast([128, 8, 64]), ALU.mult)
                                S.tt("dve", stT[g][:], stT[g][:], sp_[:], ALU.add)
                                S.copy("act", stB[g][:], stT[g][:])
                S.run()

        if "D" in phases:
            with contextlib.ExitStack() as es2:
                sb2 = lambda name, shape, dtype=F32: es2.enter_context(nc.sbuf_tensor(uniq(name), list(shape), dtype))
                ps2 = lambda name, shape, dtype=F32: es2.enter_context(nc.psum_tensor(uniq(name), list(shape), dtype))
                kT = Rot([sb2("kT%d" % i, [128, SEQ], BF16) for i in range(2)])
                qT = Rot([sb2("qT%d" % i, [128, NQT * 128], BF16) for i in range(2)])
                vh = Rot([sb2("vh%d" % i, [128, 32, 128], BF16) for i in range(2)])
                m4f = sb2("m4f", [128, 4, 512]); m4b = sb2("m4b", [128, 4, 512], BF16)
                ztp = Rot([ps2("ztp%d" % i, [128, 512]) for i in range(4)])
                ypp = Rot([ps2("ypp%d" % i, [128, 512]) for i in range(2)])
                e_t = Rot([sb2("e_t%d" % i, [128, 512]) for i in range(3)])
                sp_t = Rot([sb2("sp_t%d" % i, [128, 512], BF16) for i in range(5)])
                w_t = Rot([sb2("w_t%d" % i, [128, 512], BF16) for i in range(4)])
                spacc = Rot([sb2("spacc%d" % i, [128, 512]) for i in range(2)])
                spab = Rot([sb2("spab%d" % i, [128, 512], BF16) for i in range(3)])
                yo_t = Rot([sb2("yo_t%d" % i, [128, 512], BF16) for i in range(2)])
                S.dma("sp", m4f[:], mask4.rearrange("k p j -> p k j"))
                S.copy("dve", m4b[:], m4f[:])
                cvd = Rot([sb2("cvd%d" % i, [128, 2048], BF16) for i in range(4)])
                bgq = list(bg_jobs) if "A" in phases else []

                def load_head(h):
                    k_ = kT.next(); q_ = qT.next(); v_ = vh.next()
                    S.dma("sp", k_[:], kT_s[h])
                    S.dma("sp", q_[:], qT_s[h, :, HALO * 128:])
                    S.dma("sp", v_[:], v_s[:, 128 * h:128 * h + 128].rearrange("(t p) d -> p t d", p=128))
                    return k_, q_, v_
                nxt_head = load_head(0)
                for h in range(16):
                    k_, q_, v_ = nxt_head
                    if h + 1 < 16:
                        nxt_head = load_head(h + 1)
                    for sbi in range(5):
                        for _ in range(4):
                            if bgq:
                                conv_job(cvd, *bgq.pop(0))
                        if sbi == 0:
                            q0, W, first_tile = 0, 128, HALO
                            kmax = HALO
                        else:
                            q0, W, first_tile = 128 + 512 * (sbi - 1), 512, OWN0 + 4 * (sbi - 1)
                            kmax = first_tile + 3
                        units = list(range(kmax, -1, -1))
                        yp = ypp.next()
                        sa = spacc.next()
                        state = {"sab": None}

                        def stageA(kb):
                            z = ztp.next()
                            diag = kb >= first_tile
                            S.mm(z[:, :W], k_[:, 128 * kb:128 * kb + 128], q_[:, q0:q0 + W], start=True, stop=True)
                            if diag:
                                mk = cb[:, NEGMASK, :] if W == 128 else m4b[:, kb - first_tile, :]
                                S.mm(z[:, :W], cb[:, IDENT, :], mk, start=False, stop=True)
                            bias = kbias if kb < OWN0 else zero_c
                            e = e_t.next()
                            S.act(e[:, :W], z[:, :W], AF.Exp, bias=bias[:])
                            sp = sp_t.next()
                            S.act(sp[:, :W], e[:, :W], AF.Ln, bias=1.0)
                            return z, sp, bias

                        def stageB(ui, kb, z, sp, bias):
                            first = ui == 0
                            last = ui == len(units) - 1
                            S.mm(z[:, :W], cb[:, NEGTRI, :], sp[:, :W], start=False, stop=True)
                            if not first:
                                S.mm(z[:, :W], cb[:, NEGONES, :], state["sab"][:, :W], start=False, stop=True)
                            if not last:
                                if first:
                                    S.copy("dve", sa[:, :W], sp[:, :W])
                                else:
                                    S.tt("dve", sa[:, :W], sa[:, :W], sp[:, :W], ALU.add)
                                nb = spab.next()
                                S.copy("dve", nb[:, :W], sa[:, :W])
                                state["sab"] = nb
                            w = w_t.next()
                            S.act(w[:, :W], z[:, :W], AF.Exp, bias=bias[:])
                            return w

                        def stageC(ui, kb, w):
                            S.mm(yp[:, :W], v_[:, kb, :], w[:, :W], start=(ui == 0), stop=(ui == len(units) - 1))

                        n_u = len(units)
                        pa = {0: stageA(units[0])}
                        if n_u > 1:
                            pa[1] = stageA(units[1])
                        pb = {0: stageB(0, units[0], *pa.pop(0))}
                        for ui in range(n_u):
                            if ui + 2 < n_u:
                                pa[ui + 2] = stageA(units[ui + 2])
                            if ui + 1 < n_u:
                                pb[ui + 1] = stageB(ui + 1, units[ui + 1], *pa.pop(ui + 1))
                            stageC(ui, units[ui], pb.pop(ui))
                        yo = yo_t.next()
                        S.copy("dve", yo[:, :W], yp[:, :W])
                        S.dma("sp", mT_s[2048 + 128 * h:2048 + 128 * h + 128, q0:q0 + W], yo[:, :W])
                while bgq:
                    conv_job(cvd, *bgq.pop(0))
                S.run()

        def mlp_block(l, x_tiles, B):
            nt_ = len(x_tiles)
            T = 128 * nt_
            T1 = min(T, 512)
            xT = B["xnT"]
            for i, xt_ in enumerate(x_tiles):
                norm_transpose([xt_], B["gTm"][l], xT[:, :, 128 * i:128 * i + 128], B["scr"])
            hT = B["hT"]
            for fb in range(64):
                w = B["wblk"].next()
                S.dma("sp", w[:], Wup[l, fb])
                p = B["ps"].next()
                p2 = B["psd"][3 + fb % 2] if T > 512 else None
                for c in range(16):
                    S.mm(p[:, :T1], w[:, c, :], xT[:, c, :T1], start=(c == 0), stop=(c == 15))
                    if p2 is not None:
                        S.mm(p2[:, :T - 512], w[:, c, :], xT[:, c, 512:T], start=(c == 0), stop=(c == 15))
                r = B["rl"].next()
                S.act(r[:, :T1], p[:, :T1], AF.Relu)
                if p2 is not None:
                    S.act(r[:, 512:T], p2[:, :T - 512], AF.Relu)
                S.tt("dve", hT[:, fb, :T], r[:, :T], r[:, :T], ALU.mult)
            for nb in range(4):
                ncol = slice(512 * nb, 512 * nb + 512)
                for kq in range(4):
                    wd = B["wbig"].next()
                    S.dma("sp", wd[:], Wdn[l, 2048 * kq:2048 * kq + 2048, ncol].rearrange("(c p) n -> p c n", p=128))
                    for i in range(nt_):
                        for c in range(16):
                            S.mm(B["psd"][i][:], hT[:, 16 * kq + c, 128 * i:128 * i + 128], wd[:, c, :],
                                 start=(kq == 0 and c == 0), stop=(kq == 3 and c == 15))
                for i, xt_ in enumerate(x_tiles):
                    S.tt("dve", xt_[:, ncol], xt_[:, ncol], B["psd"][i][:], ALU.add)

        def mlp_bufs(es2):
            sb2 = lambda name, shape, dtype=F32: es2.enter_context(nc.sbuf_tensor(uniq(name), list(shape), dtype))
            ps2 = lambda name, shape, dtype=F32: es2.enter_context(nc.psum_tensor(uniq(name), list(shape), dtype))
            B = {}
            B["gTm"] = [sb2("gTm%d" % l, [128, 16]) for l in range(2)]
            epsc = sb2("epsm", [128, 1])
            B["psd"] = [ps2("psd%d" % i, [128, 512]) for i in range(5)]
            B["ps"] = Rot([ps2("psm%d" % i, [128, 512]) for i in range(2)])
            B["scr"] = dict(ss=Rot([sb2("ssm%d" % i, [128, 4]) for i in range(4)]),
                            junk=sb2("junkm", [128, 2048], BF16), eps=epsc,
                            xn=Rot([sb2("xnm%d" % i, [128, 2048], BF16) for i in range(2)]),
                            pT=Rot([ps2("pTm%d" % i, [128, 4, 128], BF16) for i in range(1)]))
            B["xnT"] = sb2("xnTm", [128, 16, 640], BF16)
            B["hT"] = sb2("hTm", [128, 64, 640], BF16)
            B["wblk"] = Rot([sb2("wblkm%d" % i, [128, 16, 128], BF16) for i in range(3)])
            B["wbig"] = Rot([sb2("wbigm%d" % i, [128, 16, 512], BF16) for i in range(2)])
            B["rl"] = Rot([sb2("rlm%d" % i, [128, 640], BF16) for i in range(2)])
            B["xt"] = [sb2("xtm%d" % i, [128, 2048]) for i in range(5)]
            for l in range(2):
                S.dma("sp", B["gTm"][l][:], mlp_norm[l].rearrange("(c p) -> p c", p=128), slow=True)
            S.memset("dve", epsc[:], EPS)
            return B

        if "E" in phases:
            with contextlib.ExitStack() as es2:
                B = mlp_bufs(es2)
                groups = [[0, 1, 2, 3, 4], [5, 6, 7, 8], [9, 10, 11, 12], [13, 14, 15, 16]]
                for grp in groups:
                    nt_ = len(grp)
                    T = 128 * nt_
                    c0 = 128 * grp[0]
                    xts = [B["xt"][i][:] for i in range(nt_)]
                    for i, qt in enumerate(grp):
                        S.dma("sp", B["xt"][i][:], x_loc[(HALO + qt) * 128:(HALO + qt) * 128 + 128, :])
                    mT = B["hT"]
                    S.dma("sp", mT[:, 0:32, :T], mT_s[:, c0:c0 + T].rearrange("(c p) t -> p c t", p=128))
                    for nb in range(4):
                        ncol = slice(512 * nb, 512 * nb + 512)
                        for kh in range(2):
                            wd = B["wbig"].next()
                            S.dma("sp", wd[:], Wo[2048 * kh:2048 * kh + 2048, ncol].rearrange("(c p) n -> p c n", p=128))
                            for i in range(nt_):
                                for c in range(16):
                                    S.mm(B["psd"][i][:], mT[:, 16 * kh + c, 128 * i:128 * i + 128], wd[:, c, :],
                                         start=(kh == 0 and c == 0), stop=(kh == 1 and c == 15))
                        for i in range(nt_):
                            S.tt("dve", xts[i][:, ncol], xts[i][:, ncol], B["psd"][i][:], ALU.add)
                    mlp_block(0, xts, B)
                    for i, qt in enumerate(grp):
                        S.dma("sp", x2_s[qt * 128:qt * 128 + 128, :], xts[i])
                S.run()

        x3_s = scratch("x3_s", [2048, D], F32)
        if "F" in phases:
            with contextlib.ExitStack() as es2:
                sb2 = lambda name, shape, dtype=F32: es2.enter_context(nc.sbuf_tensor(uniq(name), list(shape), dtype))
                ps2 = lambda name, shape, dtype=F32: es2.enter_context(nc.psum_tensor(uniq(name), list(shape), dtype))
                gp_b = sb2("gp_b", [128, 2048]); pb_b = sb2("pb_b", [128, 2048]); psc_b = sb2("psc_b", [128, 2048])
                PP = sb2("PP", [128, 16, 128]); epsc = sb2("epsf", [128, 1])
                Wp_sb = sb2("Wp_sb", [128, 16, 512], BF16)
                xt = Rot([sb2("xtf%d" % i, [128, 2048]) for i in range(4)])
                hn = Rot([sb2("hnf%d" % i, [128, 2048]) for i in range(4)])
                junk = sb2("junkf", [128, 2048], BF16)
                ss = Rot([sb2("ssf%d" % i, [128, 4]) for i in range(3)])
                dT = Rot([sb2("dTf%d" % i, [128, 16, 128], BF16) for i in range(2)])
                tt_ = Rot([sb2("ttf%d" % i, [128, 512]) for i in range(3)])
                d_ps = Rot([ps2("d_ps%d" % i, [128, 4, 128]) for i in range(3)])
                y_ps = Rot([ps2("yf_ps%d" % i, [128, 512]) for i in range(2)])
                S.dma("sp", gp_b[:], pool_norm.partition_broadcast(128))
                S.dma("sp", pb_b[:], pool_b.partition_broadcast(128))
                S.dma("sp", psc_b[:], pool_scale.partition_broadcast(128))
                S.dma("sp", PP[:], poolP.rearrange("f g k p t -> p (f g k) t"))
                S.dma("sp", Wp_sb[:], Wp.rearrange("(c p) n -> p c n", p=128))
                S.memset("dve", epsc[:], EPS)

                def pnorm(row0):
                    x_ = xt.next()
                    S.dma("sp", x_[:], x2_s[row0:row0 + 128, :])
                    s_ = ss.next()
                    S.act(junk[:], x_[:], AF.Square, accum_out=s_[:, 0:1])
                    S.act(s_[:, 1:2], s_[:, 0:1], AF.Ln, bias=epsc[:], scale=1.0 / D)
                    S.act(s_[:, 2:3], s_[:, 1:2], AF.Exp, scale=-0.5)
                    h_ = hn.next()
                    S.stt("dve", h_[:], x_[:], s_[:, 2:3], gp_b[:], ALU.mult, ALU.mult)
                    return x_, h_

                _, hprev = pnorm(0)
                nxt_pn = pnorm(128)
                for i in range(16):
                    x_, hcur = nxt_pn
                    if i + 1 < 16:
                        nxt_pn = pnorm(128 * (i + 2))
                    fi = 0 if i == 0 else 1
                    d_ = dT.next()
                    for g in range(4):
                        dp = d_ps.next()
                        for j in range(4):
                            cs = slice(512 * g + 128 * j, 512 * g + 128 * j + 128)
                            S.mm(dp[:, j, :], hprev[:, cs], PP[:, (fi * 4 + g) * 2 + 0, :], start=True, stop=False)
                            S.mm(dp[:, j, :], hcur[:, cs], PP[:, (fi * 4 + g) * 2 + 1, :], start=False, stop=True)
                        S.copy("act", d_[:, 4 * g:4 * g + 4, :], dp[:])
                    for g in range(4):
                        yp = y_ps.next()
                        for kc in range(4):
                            S.mm(yp[:], d_[:, 4 * g + kc, :], Wp_sb[:, 4 * g + kc, :], start=(kc == 0), stop=(kc == 3))
                        ncol = slice(512 * g, 512 * g + 512)
                        t_ = tt_.next()
                        S.tt("dve", t_[:], yp[:], pb_b[:, ncol], ALU.add)
                        S.tt("dve", t_[:], t_[:], psc_b[:, ncol], ALU.mult)
                        S.tt("dve", x_[:, ncol], x_[:, ncol], t_[:], ALU.add)
                    S.dma("sp", x3_s[128 * i:128 * i + 128, :], x_[:])
                    hprev = hcur
                S.run()
            with contextlib.ExitStack() as es2:
                B = mlp_bufs(es2)
                for gi in range(4):
                    xts = [B["xt"][i][:] for i in range(4)]
                    for i in range(4):
                        S.dma("sp", B["xt"][i][:], x3_s[(4 * gi + i) * 128:(4 * gi + i) * 128 + 128, :])
                    mlp_block(1, xts, B)
                    for i in range(4):
                        S.dma("sp", out[(4 * gi + i) * 128:(4 * gi + i) * 128 + 128, :], xts[i])
                S.run()
    return nc


def make_consts():
    j = np.arange(128)[:, None]
    s = np.arange(128)[None, :]
    c = np.zeros((7, 128, 128), np.float32)
    c[0] = (j == s)
    c[1] = 1.0
    c[2] = -1.0 * (j >= s)
    c[3] = NEG * (j >= s)
    c[4] = 1.0 * (j <= s)
    c[5] = NEG * (j > s)
    c[6] = -1.0
    m4 = np.zeros((4, 128, 512), np.float32)
    for i in range(4):
        m4[i, :, :128 * i] = NEG
        m4[i, :, 128 * i:128 * i + 128] = c[3]
    return c, m4


def make_poolP(s):
    P = np.zeros((2, 4, 2, 128, 128), np.float32)
    tp = np.arange(128)[:, None]
    t = np.arange(128)[None, :]
    for fi in range(2):
        for gi, win in enumerate((2, 4, 8, 16)):
            start_of_seq = (fi == 0 and s == 0)
            cnt = np.minimum(t + 1, win) if start_of_seq else np.full_like(t, win)
            cur = ((tp <= t) & (t - tp < win)) / cnt.astype(np.float32) - (tp == t)
            prev = ((t - (tp - 128)) < win) / cnt.astype(np.float32)
            if start_of_seq:
                prev = np.zeros_like(prev)
            P[fi, gi, 0] = prev
            P[fi, gi, 1] = cur
    return P


_NC_CACHE = {}


def make_in_maps(inputs):
    x = np.asarray(inputs["x"], np.float32)
    consts, m4 = make_consts()
    shared = {
        "consts": consts, "mask4": m4,
        "hyb_norm": inputs["hyb_norm"][0], "hyb_w_in": inputs["hyb_w_in"][0],
        "ssd_conv_w": inputs["ssd_conv_w"][0], "ssd_conv_b": inputs["ssd_conv_b"][0],
        "ssd_dt_bias": inputs["ssd_dt_bias"][0], "ssd_a_log": inputs["ssd_a_log"][0],
        "ssd_d": inputs["ssd_d"][0], "ssd_out_norm": inputs["ssd_out_norm"][0],
        "sb_q_norm": inputs["sb_q_norm"][0], "sb_k_norm": inputs["sb_k_norm"][0],
        "hyb_w_out": inputs["hyb_w_out"][0], "pool_norm": inputs["pool_norm"][0],
        "pool_w": inputs["pool_w"][0].reshape(2048, 512), "pool_b": inputs["pool_b"][0],
        "pool_scale": inputs["pool_scale"][0], "mlp_norm": inputs["mlp_norm"],
        "mlp_w_up": inputs["mlp_w_up"], "mlp_w_down": inputs["mlp_w_down"],
    }
    shared = {k: np.ascontiguousarray(np.asarray(v, np.float32)) for k, v in shared.items()}
    in_maps = []
    for c in range(8):
        b, s = c // 2, c % 2
        xl = np.zeros((SEQ, D), np.float32)
        if s == 1:
            xl[:] = x[b]
        else:
            xl[2048:] = x[b, :2048]
        m = dict(shared)
        m["x_loc"] = xl
        m["flag"] = np.full((128, 1), float(s), np.float32)
        m["poolP"] = make_poolP(s)
        in_maps.append(m)
    return in_maps


def kernel(**inputs):
    if "nc" not in _NC_CACHE:
        _NC_CACHE["nc"] = build_nc()
    nc = _NC_CACHE["nc"]
    in_maps = make_in_maps(inputs)
    res = run_bass_kernel_spmd(nc, in_maps, core_ids=list(range(8)))
    out = np.zeros((4, SEQ, D), np.float32)
    for c in range(8):
        b, s = c // 2, c % 2
        out[b, 2048 * s:2048 * s + 2048] = res.results[c]["out"]
    return out
```

```python
import contextlib
import numpy as np
import concourse.bass as bass
import concourse.mybir as mybir
from concourse.bass_utils import run_bass_kernel_spmd

F32 = mybir.dt.float32
BF16 = mybir.dt.bfloat16
AF = mybir.ActivationFunctionType
ALU = mybir.AluOpType

D = 2048
SEQ = 4096
NT = 32
HALO = 15
OWN0 = 16
NQT = 17
NEG = -30000.0
CDBG = 99
CSC = list(range(8))
EPS = 1e-6
DFF = 8192
IN_DIM = 11296
C_Z, C_X, C_B, C_C, C_DT, C_Q, C_K, C_V = 0, 2048, 4096, 4608, 5120, 5152, 7200, 9248


class Sched:
    STREAMS = ("pe", "act", "dve", "pool", "sp")

    def __init__(self, nc, es):
        self.nc = nc
        self.csem = {s: es.enter_context(nc.semaphore("c_" + s)) for s in ("pe", "act", "dve", "pool")}
        self.ccount = {s: 0 for s in self.csem}
        self.KQ = 6
        self.dsem = {q: [es.enter_context(nc.semaphore("d_%s%d" % (q, i))) for i in range(self.KQ)]
                     for q in ("sp", "pool", "act")}
        self.dval = {q: [0] * self.KQ for q in self.dsem}
        self.dnext = {q: 0 for q in self.dsem}
        self.reset()

    def reset(self):
        self.ops = []
        self.last_w = {}
        self.readers = {}

    @staticmethod
    def _key(a):
        if isinstance(a, str):
            return a
        if a is None or isinstance(a, (int, float)):
            return None
        sp = str(a.space)
        if "DRAM" in sp:
            return None
        return a.tensor.name

    def add(self, stream, fn, reads, writes, sig=True, dma=False):
        rk = [k for k in (self._key(a) for a in reads) if k is not None]
        wk = [k for k in (self._key(a) for a in writes) if k is not None]
        deps = set()
        for k in rk:
            if k in self.last_w:
                deps.add(self.last_w[k])
        for k in wk:
            if k in self.last_w:
                deps.add(self.last_w[k])
            for r in self.readers.get(k, ()):
                deps.add(r)
        i = len(self.ops)
        for d in deps:
            if not (self.ops[d]["stream"] == "pe" and stream == "pe" and not dma):
                self.ops[d]["sig"] = True
        self.ops.append(dict(stream=stream, fn=fn, deps=deps, sig=sig, dma=dma))
        for k in rk:
            self.readers.setdefault(k, []).append(i)
        for k in wk:
            self.last_w[k] = i
            self.readers[k] = []
        return i

    def mm(self, out, lhsT, rhs, start=True, stop=True):
        self.add("pe", lambda e: e.matmul(out, lhsT=lhsT, rhs=rhs, start=start, stop=stop),
                 [lhsT, rhs], [out], sig=stop)

    def tr(self, out, in_, ident):
        self.add("pe", lambda e: e.transpose(out=out, in_=in_, identity=ident), [in_, ident], [out])

    def act(self, out, in_, func, bias=None, scale=None, accum_out=None):
        kw = {}
        if bias is not None:
            kw["bias"] = bias
        if scale is not None:
            kw["scale"] = scale
        if accum_out is not None:
            kw["accum_out"] = accum_out
        self.add("act", lambda e: e.activation(out=out, in_=in_, func=func, **kw),
                 [in_, bias, scale], [out, accum_out])

    def tt(self, eng, out, in0, in1, op):
        self.add(eng, lambda e: e.tensor_tensor(out=out, in0=in0, in1=in1, op=op), [in0, in1], [out])

    def ts(self, eng, out, in0, s1, s2, op0, op1=None):
        if op1 is None:
            self.add(eng, lambda e: e.tensor_scalar(out=out, in0=in0, scalar1=s1, scalar2=None, op0=op0),
                     [in0, s1], [out])
        else:
            self.add(eng, lambda e: e.tensor_scalar(out=out, in0=in0, scalar1=s1, scalar2=s2, op0=op0, op1=op1),
                     [in0, s1, s2], [out])

    def stt(self, eng, out, in0, scalar, in1, op0, op1):
        self.add(eng, lambda e: e.scalar_tensor_tensor(out=out, in0=in0, scalar=scalar, in1=in1, op0=op0, op1=op1),
                 [in0, scalar, in1], [out])

    def copy(self, eng, out, in_):
        if eng == "act":
            self.add("act", lambda e: e.copy(out=out, in_=in_), [in_], [out])
        else:
            self.add(eng, lambda e: e.tensor_copy(out=out, in_=in_), [in_], [out])

    def memset(self, eng, out, val):
        self.add(eng, lambda e: e.memset(out, val), [], [out])

    def dma(self, q, out, in_, extra_r=(), extra_w=(), slow=False):
        if slow:
            fn = lambda e: e.dma_start(out=out, in_=in_, allow_slow_non_contiguous=True)
        else:
            fn = lambda e: e.dma_start(out=out, in_=in_)
        self.add(q, fn, [in_] + list(extra_r), [out] + list(extra_w), dma=True)

    def run(self, name=None):
        nc = self.nc
        ops = self.ops
        per = {s: [i for i, o in enumerate(ops) if o["stream"] == s] for s in self.STREAMS}
        token = [None] * len(ops)
        for s in ("pe", "act", "dve", "pool"):
            lst = [i for i in per[s] if not ops[i]["dma"]]
            if lst:
                ops[lst[-1]]["sig"] = True
            cnt = self.ccount[s]
            vals = {}
            for i in lst:
                if ops[i]["sig"]:
                    cnt += 1
                    vals[i] = cnt
            nxt = None
            for i in reversed(lst):
                if ops[i]["sig"]:
                    nxt = vals[i]
                token[i] = (self.csem[s], nxt)
            self.ccount[s] = cnt
        prevtok = {}
        for s in ("sp", "pool", "act"):
            for i in per[s]:
                if not ops[i]["dma"]:
                    continue
                k = self.dnext[s]
                self.dnext[s] = (k + 1) % self.KQ
                if self.dval[s][k] > 0:
                    prevtok[i] = (self.dsem[s][k], self.dval[s][k])
                self.dval[s][k] += 16
                token[i] = (self.dsem[s][k], self.dval[s][k])
        final_d = {s: [(self.dsem[s][k], self.dval[s][k]) for k in range(self.KQ) if self.dval[s][k] > 0]
                   for s in self.dsem}
        csem_ids = {id(v): k for k, v in self.csem.items()}

        def emit(stream):
            def body(e):
                waited = {}

                def wait(tok):
                    sem, val = tok
                    if waited.get(id(sem), 0) >= val:
                        return
                    waited[id(sem)] = val
                    e.wait_ge(sem, val)
                for i in per[stream]:
                    o = ops[i]
                    for d in sorted(o["deps"]):
                        od = ops[d]
                        if od["stream"] == "pe" and stream == "pe" and not od["dma"] and not o["dma"]:
                            continue
                        wait(token[d])
                    if i in prevtok:
                        wait(prevtok[i])
                    ins = o["fn"](e)
                    sem, val = token[i]
                    if o["dma"]:
                        ins.then_inc(sem, 16)
                    elif o["sig"]:
                        ins.then_inc(sem, 1)
                if stream in final_d:
                    for tok in final_d[stream]:
                        if any(ops[i]["dma"] for i in per[stream]):
                            wait(tok)
            return body

        with nc.Block() as block:
            if per["pe"]:
                block.tensor(emit("pe"))
            if per["act"]:
                block.scalar(emit("act"))
            if per["dve"]:
                block.vector(emit("dve"))
            if per["pool"]:
                block.gpsimd(emit("pool"))
            if per["sp"]:
                block.sync(emit("sp"))
        self.reset()


class Rot:
    def __init__(self, bufs):
        self.bufs = bufs
        self.i = 0

    def next(self):
        b = self.bufs[self.i % len(self.bufs)]
        self.i += 1
        return b


def bcast_rows(ap_1d, n, parts=128):
    return ap_1d.partition_broadcast(parts)


def build_nc(phases="ABCDEF", taps=()):
    nc = bass.Bass("TRN2", target_bir_lowering=False)
    dt_in = lambda name, shape: nc.dram_tensor(name, list(shape), F32, kind="ExternalInput").ap()

    def scratch(name, shape, dtype):
        if name in taps:
            return nc.dram_tensor(name, list(shape), dtype, kind="ExternalOutput").ap()
        return nc.dram_tensor(name, list(shape), dtype).ap()

    _uc = [0]

    def uniq(name):
        _uc[0] += 1
        return "%s_u%d" % (name, _uc[0])

    x_loc = dt_in("x_loc", [SEQ, D])
    flag = dt_in("flag", [128, 1])
    consts = dt_in("consts", [7, 128, 128])
    mask4 = dt_in("mask4", [4, 128, 512])
    poolP = dt_in("poolP", [2, 4, 2, 128, 128])
    hyb_norm = dt_in("hyb_norm", [D])
    w_in = dt_in("hyb_w_in", [D, IN_DIM])
    conv_w = dt_in("ssd_conv_w", [4, 3072])
    conv_b = dt_in("ssd_conv_b", [3072])
    dt_bias = dt_in("ssd_dt_bias", [32])
    a_log = dt_in("ssd_a_log", [32])
    ssd_d = dt_in("ssd_d", [32])
    out_norm = dt_in("ssd_out_norm", [D])
    q_norm = dt_in("sb_q_norm", [128])
    k_norm = dt_in("sb_k_norm", [128])
    w_out = dt_in("hyb_w_out", [4096, D])
    pool_norm = dt_in("pool_norm", [D])
    pool_w = dt_in("pool_w", [2048, 512])
    pool_b = dt_in("pool_b", [D])
    pool_scale = dt_in("pool_scale", [D])
    mlp_norm = dt_in("mlp_norm", [2, D])
    w_up = dt_in("mlp_w_up", [2, D, DFF])
    w_dn = dt_in("mlp_w_down", [2, DFF, D])
    out = nc.dram_tensor("out", [2048, D], F32, kind="ExternalOutput").ap()

    NFM = 36 + 36
    fm_cols = ([C_X + 128 * i for i in range(16)] + [C_B + 128 * i for i in range(4)] +
               [C_K + 128 * i for i in range(16)] + [C_C + 128 * i for i in range(4)] +
               [C_Q + 128 * i for i in range(16)])
    NFM = len(fm_cols)
    Wfm = scratch("Wfm", [NFM, 128, 16, 128], BF16)
    Wtm = scratch("Wtm", [8, 128, 16, 512], BF16)
    Wdt = scratch("Wdt", [128, 16, 32], BF16)
    Wo = scratch("Wo", [4096, D], BF16)
    Wp = scratch("Wp", [2048, 512], BF16)
    Wup = scratch("Wup", [2, 64, 128, 16, 128], BF16)
    Wdn = scratch("Wdn", [2, DFF, D], BF16)
    xc_s = scratch("xc_s", [3072, SEQ], BF16)
    z_s = scratch("z_s", [SEQ, 2048], F32)
    dt_s = scratch("dt_s", [SEQ, 32], F32)
    qT_s = scratch("qT_s", [16, 128, SEQ], BF16)
    kT_s = scratch("kT_s", [16, 128, SEQ], BF16)
    v_s = scratch("v_s", [SEQ, 2048], BF16)
    mT_s = scratch("mT_s", [4096, NQT * 128], BF16)
    x2_s = scratch("x2_s", [NQT * 128, D], F32)

    with contextlib.ExitStack() as es:
        S = Sched(nc, es)
        sb = lambda name, shape, dtype=F32: es.enter_context(nc.sbuf_tensor(name, list(shape), dtype))
        cf = sb("cf", [128, 7, 128])
        cb = sb("cb", [128, 7, 128], BF16)
        flag_t = sb("flag_t", [128, 1])
        kbias = sb("kbias", [128, 1])
        zero_c = sb("zero_c", [128, 1])
        IDENT, ONES, NEGTRI, NEGMASK, TRIU, NEGSSD, NEGONES = range(7)

        S.dma("sp", cf[:], consts.rearrange("k p j -> p k j"))
        S.dma("sp", flag_t[:], flag)
        S.copy("dve", cb[:], cf[:])
        S.ts("dve", kbias[:], flag_t[:], -1.0, -NEG, ALU.add, ALU.mult)
        S.memset("dve", zero_c[:], 0.0)
        S.run()

        pre_jobs, bg_jobs = [], []
        for bi, c0 in enumerate(fm_cols):
            pre_jobs.append((Wfm[bi], w_in[:, c0:c0 + 128].rearrange("(c p) j -> p c j", p=128), [128, 16, 128], "Wfm%d" % bi))
        for bi in [4, 5, 6, 7, 0, 1, 2, 3]:
            c0 = (C_Z if bi < 4 else C_V) + 512 * (bi % 4)
            for ch in range(4):
                pre_jobs.append((Wtm[bi, :, 4 * ch:4 * ch + 4, :],
                                 w_in[512 * ch:512 * ch + 512, c0:c0 + 512].rearrange("(c p) j -> p c j", p=128),
                                 [128, 4, 512], "Wtm%d_%d" % (bi, ch)))
        pre_jobs.append((Wdt[:], w_in[:, C_DT:C_DT + 32].rearrange("(c p) j -> p c j", p=128), [128, 16, 32], "Wdt"))
        for r in range(32):
            bg_jobs.append((Wo[128 * r:128 * r + 128, :], w_out[128 * r:128 * r + 128, :], [128, 2048]))
        for l in range(2):
            for fb in range(64):
                bg_jobs.append((Wup[l, fb], w_up[l, :, 128 * fb:128 * fb + 128].rearrange("(c p) j -> p c j", p=128),
                                [128, 16, 128]))
            for r in range(64):
                bg_jobs.append((Wdn[l, 128 * r:128 * r + 128, :], w_dn[l, 128 * r:128 * r + 128, :], [128, 2048]))
            if l == 0:
                for r in range(4):
                    bg_jobs.append((Wp[512 * r:512 * r + 512, :].rearrange("(c p) j -> p c j", p=128),
                                    pool_w[512 * r:512 * r + 512, :].rearrange("(c p) j -> p c j", p=128), [128, 4, 512]))

        def conv_job(cv, dst, src, shape, key=None):
            t = cv.next()
            n = 1
            for v_ in shape[1:]:
                n *= v_
            view = t[:, 0:n]
            if len(shape) == 3:
                view = view.rearrange("p (a b) -> p a b", a=shape[1])
            S.dma("pool", view, src)
            S.dma("sp", dst, view, extra_w=([key] if key else []))

        MERGE_AB = ("A" in phases) and ("B" in phases)
        if "A" in phases and not MERGE_AB:
            with contextlib.ExitStack() as es2:
                cv = Rot([es2.enter_context(nc.sbuf_tensor(uniq("cv%d" % i), [128, 2048], BF16)) for i in range(4)])
                for j in pre_jobs:
                    conv_job(cv, *j)
                if "D" not in phases:
                    for j in bg_jobs:
                        conv_job(cv, *j)
                S.run()

        def norm_transpose(xt_tiles, gT, xnT, scr):
            for i, xt in enumerate(xt_tiles):
                ss = scr["ss"].next()
                S.act(scr["junk"][:], xt, AF.Square, accum_out=ss[:, 0:1])
                S.act(ss[:, 1:2], ss[:, 0:1], AF.Ln, bias=scr["eps"][:], scale=1.0 / D)
                S.act(ss[:, 2:3], ss[:, 1:2], AF.Exp, scale=-0.5)
                xn = scr["xn"].next()
                S.ts("dve", xn[:], xt, ss[:, 2:3], None, ALU.mult)
                for c4 in range(4):
                    pT = scr["pT"].next()
                    for j in range(4):
                        c = 4 * c4 + j
                        S.tr(pT[:, j, :], xn[:, 128 * c:128 * c + 128], cb[:, IDENT, :])
                    eng = "pool" if False else "dve"
                    S.tt(eng, xnT[:, 4 * c4:4 * c4 + 4, 128 * i:128 * i + 128], pT[:],
                         gT[:, 4 * c4:4 * c4 + 4].unsqueeze(2).to_broadcast([128, 4, 128]), ALU.mult)

        if "B" in phases:
            with contextlib.ExitStack() as es2:
                sb2 = lambda name, shape, dtype=F32: es2.enter_context(nc.sbuf_tensor(uniq(name), list(shape), dtype))
                ps2 = lambda name, shape, dtype=F32: es2.enter_context(nc.psum_tensor(uniq(name), list(shape), dtype))
                gT = sb2("gT", [128, 16])
                gq = sb2("gq", [128, 1])
                gk = sb2("gk", [128, 1])
                epsc = sb2("epsc", [128, 1])
                wdt = sb2("wdt", [128, 16, 32], BF16)
                xt = Rot([sb2("xt%d" % i, [128, 2048]) for i in range(3)])
                scr = dict(ss=Rot([sb2("ss%d" % i, [128, 4]) for i in range(4)]),
                           junk=sb2("junk", [128, 2048], BF16), eps=epsc,
                           xn=Rot([sb2("xn%d" % i, [128, 2048], BF16) for i in range(2)]),
                           pT=Rot([ps2("pT%d" % i, [128, 4, 128], BF16) for i in range(2)]))
                xnT = Rot([sb2("xnT%d" % i, [128, 16, 512], BF16) for i in range(2)])
                wblk = Rot([sb2("wblk%d" % i, [128, 16, 128], BF16) for i in range(3)])
                wtm = Rot([sb2("wtm%d" % i, [128, 16, 512], BF16) for i in range(3)])
                ps = Rot([ps2("ps%d" % i, [128, 512]) for i in range(4)])
                pss = Rot([ps2("pss%d" % i, [128, 512]) for i in range(2)])
                of = Rot([sb2("of%d" % i, [128, 512]) for i in range(3)])
                ob = Rot([sb2("ob%d" % i, [128, 512], BF16) for i in range(3)])
                sq = Rot([sb2("sq%d" % i, [128, 512], BF16) for i in range(3)])
                rs = Rot([sb2("rs%d" % i, [128, 512]) for i in range(2)])
                cwTb = sb2("cwTb", [128, 4, 24]); cbTb = sb2("cbTb", [128, 24])
                halo_t = sb2("halo_t", [128, 24, 3])
                xiR = Rot([sb2("xiR%d" % i, [128, 516]) for i in range(3)])
                accR = Rot([sb2("accR%d" % i, [128, 512]) for i in range(3)])
                for k in range(4):
                    S.dma("sp", cwTb[:, k, :], conv_w[k, :].rearrange("(c p) -> p c", p=128), slow=True)
                S.dma("sp", cbTb[:], conv_b.rearrange("(c p) -> p c", p=128), slow=True)
                S.memset("dve", halo_t[:], 0.0)
                S.dma("sp", gT[:], hyb_norm.rearrange("(c p) -> p c", p=128), slow=True)
                S.dma("sp", gq[:], q_norm.rearrange("(p o) -> p o", o=1), slow=True)
                S.dma("sp", gk[:], k_norm.rearrange("(p o) -> p o", o=1), slow=True)
                S.ts("dve", gq[:], gq[:], 128.0 ** -0.5, None, ALU.mult)
                S.memset("dve", epsc[:], EPS)
                cvb = Rot([sb2("cvb%d" % i, [128, 2048], BF16) for i in range(4)])
                preq = list(pre_jobs) if MERGE_AB else []
                wdt_loaded = [False]
                jobs = []
                pend_tail = []
                STQ = "pool"
                for g in range(8):
                    T0 = 512 * g
                    xT = xnT.next()
                    for i in range(4):
                        t = xt.next()
                        jobs.append((lambda t=t, r0=T0 + 128 * i: S.dma("sp", t[:], x_loc[r0:r0 + 128, :]),
                                     lambda t=t, xT=xT, i=i: norm_transpose([t[:]], gT, xT[:, :, 128 * i:128 * i + 128], scr)))
                    nblk = 36 if g < 3 else NFM

                    def fm_compute(w, bi, xT=xT, T0=T0):
                        p = ps.next()
                        for c in range(16):
                            S.mm(p[:], w[:, c, :], xT[:, c, :], start=(c == 0), stop=(c == 15))
                        if pend_tail:
                            pend_tail.pop(0)()
                        c0 = fm_cols[bi]
                        if c0 >= C_Q:
                            isq = c0 < C_K
                            hd = (c0 - (C_Q if isq else C_K)) // 128
                            s_ = sq.next()
                            S.act(s_[:], p[:], AF.Square)

                            def tail(p=p, s_=s_, isq=isq, hd=hd):
                                p2 = pss.next()
                                S.mm(p2[:], cb[:, ONES, :], s_[:])
                                r = rs.next()
                                S.act(r[:], p2[:], AF.Ln, bias=epsc[:], scale=1.0 / 128)
                                S.act(r[:], r[:], AF.Exp, scale=-0.5)
                                o = ob.next()
                                S.stt("dve", o[:], p[:], (gq if isq else gk)[:, 0:1], r[:], ALU.mult, ALU.mult)
                                S.dma(STQ, (qT_s if isq else kT_s)[hd, :, T0:T0 + 512], o[:])
                            pend_tail.append(tail)
                        else:
                            cc = (c0 - C_X) // 128
                            xi = xiR.next()
                            S.copy("act", xi[:, 4:516], p[:])
                            S.copy("dve", xi[:, 1:4], halo_t[:, cc, :])
                            S.copy("dve", halo_t[:, cc, :], xi[:, 513:516])
                            ac = accR.next()
                            S.ts("dve", ac[:], xi[:, 1:513], cwTb[:, 0, cc:cc + 1], cbTb[:, cc:cc + 1], ALU.mult, ALU.add)
                            for k in range(1, 4):
                                S.stt("dve", ac[:], xi[:, 1 + k:1 + k + 512], cwTb[:, k, cc:cc + 1], ac[:], ALU.mult, ALU.add)

                            def tail(ac=ac, cc=cc):
                                o = ob.next()
                                S.act(o[:], ac[:], AF.Silu)
                                S.dma(STQ, xc_s[128 * cc:128 * cc + 128, T0:T0 + 512], o[:])
                            pend_tail.append(tail)

                    for bi in range(nblk):
                        w = wblk.next()
                        jobs.append((lambda w=w, bi=bi: S.dma("sp", w[:], Wfm[bi], extra_r=["Wfm%d" % bi]),
                                     lambda w=w, bi=bi, f=fm_compute: f(w, bi)))

                    def tm_compute(w, bi, xT=xT, T0=T0):
                        while pend_tail:
                            pend_tail.pop(0)()
                        for i in range(4):
                            p = ps.next()
                            for c in range(16):
                                S.mm(p[:], xT[:, c, 128 * i:128 * i + 128], w[:, c, :], start=(c == 0), stop=(c == 15))
                            r0 = T0 + 128 * i
                            if bi < 4:
                                o = of.next()
                                S.copy("dve" if i % 2 else "act", o[:], p[:])
                                S.dma(STQ, z_s[r0:r0 + 128, 512 * bi:512 * bi + 512], o[:])
                            else:
                                o = ob.next()
                                S.copy("dve" if i % 2 else "act", o[:], p[:])
                                S.dma(STQ, v_s[r0:r0 + 128, 512 * (bi - 4):512 * (bi - 4) + 512], o[:])

                    for bi in range(8):
                        if bi < 4 and g < 3:
                            continue
                        w = wtm.next()
                        jobs.append((lambda w=w, bi=bi: S.dma("sp", w[:], Wtm[bi], extra_r=["Wtm%d_%d" % (bi, ch) for ch in range(4)]),
                                     lambda w=w, bi=bi, f=tm_compute: f(w, bi)))

                    def dt_compute(xT=xT, T0=T0):
                        if not wdt_loaded[0]:
                            wdt_loaded[0] = True
                            S.dma("sp", wdt[:], Wdt[:], extra_r=["Wdt"])
                        for i in range(4):
                            p = ps.next()
                            for c in range(16):
                                S.mm(p[:, 0:32], xT[:, c, 128 * i:128 * i + 128], wdt[:, c, :], start=(c == 0), stop=(c == 15))
                            o = of.next()
                            S.copy("dve", o[:, 0:32], p[:, 0:32])
                            S.dma(STQ, dt_s[T0 + 128 * i:T0 + 128 * i + 128, :], o[:, 0:32])
                    jobs.append((lambda: None, dt_compute))
                DEPTH = 2

                def feed_conv(n):
                    for _ in range(n):
                        if preq:
                            conv_job(cvb, *preq.pop(0))
                feed_conv(10)
                for j in range(min(DEPTH, len(jobs))):
                    jobs[j][0]()
                for j in range(len(jobs)):
                    feed_conv(2 if j < 50 else 100)
                    if j + DEPTH < len(jobs):
                        jobs[j + DEPTH][0]()
                    jobs[j][1]()
                S.run()

        if "C" in phases:
            with contextlib.ExitStack() as es2:
                sb2 = lambda name, shape, dtype=F32: es2.enter_context(nc.sbuf_tensor(uniq(name), list(shape), dtype))
                ps2 = lambda name, shape, dtype=F32: es2.enter_context(nc.psum_tensor(uniq(name), list(shape), dtype))
                cwT = sb2("cwT", [128, 4, 24]); cbT = sb2("cbT", [128, 24])
                dtb = sb2("dtb", [128, 32]); a_b = sb2("a_b", [128, 32]); D_b = sb2("D_b", [128, 32])
                on_b = sb2("on_b", [128, 2048]); epsc = sb2("epsc", [128, 1])
                stT = [sb2("stT%d" % g, [128, 512]) for g in range(4)]
                stB = [sb2("stB%d" % g, [128, 512], BF16) for g in range(4)]
                xcR = Rot([sb2("xc%d" % i, [128, 24, 512], BF16) for i in range(2)])
                xs_tm = sb2("xs_tm", [128, 4, 2048], BF16)
                B_tm = sb2("B_tm", [128, 4, 512], BF16)
                xdt = sb2("xdt", [128, 4, 2048], BF16)
                xdts = sb2("xdts", [128, 4, 2048], BF16)
                dtv = sb2("dtv", [128, 4, 32]); da = sb2("da", [128, 4, 32]); acs = sb2("acs", [128, 4, 32])
                nacs = sb2("nacs", [128, 4, 32]); cdb = sb2("cdb", [128, 4, 32]); dte = sb2("dte", [128, 4, 32])
                ea = sb2("ea", [128, 4, 32]); w1 = sb2("w1", [128, 4, 32])
                cbm = Rot([sb2("cbm%d" % i, [128, 4, 128]) for i in range(2)])
                zt = Rot([sb2("zt%d" % i, [128, 2048]) for i in range(1)])
                sz = Rot([sb2("sz%d" % i, [128, 2048]) for i in range(1)])
                tda = Rot([sb2("tda%d" % i, [128, 128]) for i in range(4)])
                dec = Rot([sb2("dec%d" % i, [128, 128]) for i in range(4)])
                MT = Rot([sb2("MT%d" % i, [128, 128], BF16) for i in range(3)])
                t1 = Rot([sb2("t1_%d" % i, [128, 512]) for i in range(2)])
                t2 = Rot([sb2("t2_%d" % i, [128, 512]) for i in range(2)])
                ssq = Rot([sb2("ssq%d" % i, [128, 4]) for i in range(3)])
                junk = sb2("junkc", [128, 512], BF16)
                yn = Rot([sb2("yn%d" % i, [128, 512], BF16) for i in range(2)])
                yT = Rot([sb2("yT%d" % i, [128, 4, 128], BF16) for i in range(2)])
                cb_ps = ps2("cb_ps", [128, 4, 128])
                y_ps = Rot([ps2("y_ps%d" % i, [128, 512]) for i in range(1)])
                yo_ps = Rot([ps2("yo_ps%d" % i, [128, 512]) for i in range(1)])
                s_ps = Rot([ps2("s_ps%d" % i, [128, 512]) for i in range(1)])
                pT = Rot([ps2("pTc%d" % i, [128, 4, 128], BF16) for i in range(1)])
                at_ps = ps2("at_ps", [128, 2, 128]); acs_ps = at_ps[:, 0, :]; tot_ps = at_ps[:, 1, :]
                seg_ps = Rot([ps2("seg_ps%d" % i, [128, 128]) for i in range(2)])

                for k in range(4):
                    S.dma("sp", cwT[:, k, :], conv_w[k, :].rearrange("(c p) -> p c", p=128), slow=True)
                S.dma("sp", cbT[:], conv_b.rearrange("(c p) -> p c", p=128), slow=True)
                S.dma("sp", dtb[:], dt_bias.partition_broadcast(128))
                S.dma("sp", a_b[:], a_log.partition_broadcast(128))
                S.dma("sp", D_b[:], ssd_d.partition_broadcast(128))
                S.dma("sp", on_b[:], out_norm.partition_broadcast(128))
                S.act(a_b[:], a_b[:], AF.Exp)
                S.ts("dve", a_b[:], a_b[:], -1.0, None, ALU.mult)
                S.memset("dve", epsc[:], EPS)
                for g in range(4):
                    S.memset("dve", stT[g][:], 0.0)
                    S.memset("pool", stB[g][:], 0.0)
                ei = [0]

                def alt():
                    ei[0] += 1
                    return "dve" if ei[0] % 2 else "pool"

                xc_next = xcR.next()
                S.dma("sp", xc_next[:], xc_s[:, 0:512].rearrange("(c p) t -> p c t", p=128))
                for SC in CSC:
                    T0 = 512 * SC
                    xc = xc_next
                    if SC + 1 < 8:
                        xc_next = xcR.next()
                        S.dma("sp", xc_next[:], xc_s[:, T0 + 512:T0 + 1024].rearrange("(c p) t -> p c t", p=128))
                    if CDBG < 2:
                        continue
                    for ch in range(4):
                        for c4 in range(5):
                            p = pT.next()
                            for j in range(4):
                                S.tr(p[:, j, :], xc[:, 4 * c4 + j, 128 * ch:128 * ch + 128], cb[:, IDENT, :])
                            dst = xs_tm[:, ch, 512 * c4:512 * c4 + 512] if c4 < 4 else B_tm[:, ch, :]
                            S.copy("act" if c4 % 2 else "dve", dst, p[:].rearrange("p a b -> p (a b)"))
                    if CDBG < 2.05:
                        continue
                    S.dma("sp", dtv[:], dt_s[T0:T0 + 512, :].rearrange("(c p) h -> p c h", p=128))
                    S.tt("dve", dtv[:], dtv[:], dtb[:].unsqueeze(1).to_broadcast([128, 4, 32]), ALU.add)
                    S.act(dtv[:], dtv[:], AF.Exp)
                    S.act(dtv[:], dtv[:], AF.Ln, bias=1.0)
                    if CDBG == 2.1:
                        continue
                    S.tt("dve", da[:], dtv[:], a_b[:].unsqueeze(1).to_broadcast([128, 4, 32]), ALU.mult)
                    daf = da[:].rearrange("p c h -> p (c h)")
                    S.mm(acs_ps, cf[:, TRIU, :], daf)
                    S.mm(tot_ps, cf[:, ONES, :], daf)
                    if CDBG == 2.2:
                        continue
                    S.copy("dve", acs[:].rearrange("p c h -> p (c h)"), acs_ps)
                    S.ts("dve", nacs[:], acs[:], -1.0, None, ALU.mult)
                    S.copy("dve", w1[:].rearrange("p c h -> p (c h)"), tot_ps)
                    S.act(cdb[:], w1[:], AF.Exp)
                    S.tt("dve", dte[:], w1[:], acs[:], ALU.subtract)
                    S.act(dte[:], dte[:], AF.Exp)
                    S.act(ea[:], acs[:], AF.Exp)
                    S.tt("dve", w1[:], dtv[:], dte[:], ALU.mult)
                    for ch in range(4 if CDBG != 3 else 0):
                        xv = xs_tm[:, ch, :].rearrange("p (h q) -> p h q", q=64)
                        S.tt("dve", xdt[:, ch, :].rearrange("p (h q) -> p h q", q=64), xv,
                             dtv[:, ch, :].unsqueeze(2).to_broadcast([128, 32, 64]), ALU.mult)
                        S.tt("pool", xdts[:, ch, :].rearrange("p (h q) -> p h q", q=64), xv,
                             w1[:, ch, :].unsqueeze(2).to_broadcast([128, 32, 64]), ALU.mult)
                    if CDBG < 4:
                        continue
                    for ch in range(4):
                        lc = 4 * SC + ch
                        cs = slice(128 * ch, 128 * ch + 128)
                        if lc == OWN0:
                            for g in range(4):
                                S.ts("dve", stT[g][:], stT[g][:], flag_t[:, 0:1], None, ALU.mult)
                                S.copy("pool", stB[g][:], stT[g][:])
                        if lc >= HALO:
                            for g in range(4):
                                S.mm(cb_ps[:, g, :], xc[:, 16 + g, cs], xc[:, 20 + g, cs])
                            cm = cbm.next()
                            S.copy("dve", cm[:], cb_ps[:])
                            z_t = zt.next()
                            S.dma("sp", z_t[:], z_s[T0 + 128 * ch:T0 + 128 * ch + 128, :])
                            s_z = sz.next()
                            S.act(s_z[:], z_t[:], AF.Silu)
                            for g in range(4):
                                yp = y_ps.next(); yo = yo_ps.next()
                                S.mm(yo[:], xc[:, 20 + g, cs], stB[g][:])
                                a2 = t2.next()
                                S.tt("pool", a2[:].rearrange("p (h q) -> p h q", q=64),
                                     xs_tm[:, ch, 512 * g:512 * g + 512].rearrange("p (h q) -> p h q", q=64),
                                     D_b[:, 8 * g:8 * g + 8].unsqueeze(2).to_broadcast([128, 8, 64]), ALU.mult)
                                def s1(hh, g=g, ch=ch):
                                    h = 8 * g + hh
                                    td = tda.next()
                                    S.ts("dve", td[:], cf[:, TRIU, :], da[:, ch, h:h + 1], None, ALU.mult)
                                    sg = seg_ps.next()
                                    S.mm(sg[:], cf[:, ONES, :], td[:], start=True, stop=False)
                                    S.mm(sg[:], cf[:, IDENT, :], cf[:, NEGSSD, :], start=False, stop=True)
                                    dc = dec.next()
                                    S.act(dc[:], sg[:], AF.Exp, bias=nacs[:, ch, h:h + 1])
                                    return dc

                                def s2(hh, dc, g=g, ch=ch, yp=yp, cm=cm):
                                    h = 8 * g + hh
                                    m = MT.next()
                                    S.tt("dve", m[:], dc[:], cm[:, g, :], ALU.mult)
                                    S.mm(yp[:, 64 * hh:64 * hh + 64], m[:], xdt[:, ch, 64 * h:64 * h + 64])
                                dcs = {0: s1(0), 1: s1(1)}
                                for hh in range(8):
                                    if hh + 2 < 8:
                                        dcs[hh + 2] = s1(hh + 2)
                                    s2(hh, dcs.pop(hh))
                                a1 = t1.next()
                                S.tt("dve", a1[:].rearrange("p (h q) -> p h q", q=64), yo[:].rearrange("p (h q) -> p h q", q=64),
                                     ea[:, ch, 8 * g:8 * g + 8].unsqueeze(2).to_broadcast([128, 8, 64]), ALU.mult)
                                S.tt("dve", a1[:], a1[:], yp[:], ALU.add)
                                S.tt("dve", a1[:], a1[:], a2[:], ALU.add)
                                S.tt("dve", a1[:], a1[:], s_z[:, 512 * g:512 * g + 512], ALU.mult)
                                sq_ = ssq.next()
                                S.act(junk[:], a1[:], AF.Square, accum_out=sq_[:, 0:1])
                                S.act(sq_[:, 1:2], sq_[:, 0:1], AF.Ln, bias=epsc[:], scale=1.0 / 512)
                                S.act(sq_[:, 2:3], sq_[:, 1:2], AF.Exp, scale=-0.5)
                                y_n = yn.next()
                                S.stt("dve", y_n[:], a1[:], sq_[:, 2:3], on_b[:, 512 * g:512 * g + 512], ALU.mult, ALU.mult)
                                p = pT.next()
                                for j in range(4):
                                    S.tr(p[:, j, :], y_n[:, 128 * j:128 * j + 128], cb[:, IDENT, :])
                                y_t = yT.next()
                                S.copy("act", y_t[:], p[:])
                                col0 = (lc - HALO) * 128
                                S.dma("sp", mT_s[512 * g:512 * g + 512, col0:col0 + 128].rearrange("(j p) t -> p j t", p=128), y_t[:])
                        if lc < NT - 1:
                            for g in range(4):
                                sp_ = s_ps.next()
                                S.mm(sp_[:], B_tm[:, ch, 128 * g:128 * g + 128], xdts[:, ch, 512 * g:512 * g + 512])
                                S.tt("dve", stT[g][:].rearrange("p (h q) -> p h q", q=64), stT[g][:].rearrange("p (h q) -> p h q", q=64),
                                     cdb[:, ch, 8 * g:8 * g + 8].unsqueeze(2).to_broadcast([128, 8, 64]), ALU.mult)
                                S.tt("dve", stT[g][:], stT[g][:], sp_[:], ALU.add)
                                S.copy("act", stB[g][:], stT[g][:])
                S.run()

        if "D" in phases:
            with contextlib.ExitStack() as es2:
                sb2 = lambda name, shape, dtype=F32: es2.enter_context(nc.sbuf_tensor(uniq(name), list(shape), dtype))
                ps2 = lambda name, shape, dtype=F32: es2.enter_context(nc.psum_tensor(uniq(name), list(shape), dtype))
                kT = Rot([sb2("kT%d" % i, [128, SEQ], BF16) for i in range(2)])
                qT = Rot([sb2("qT%d" % i, [128, NQT * 128], BF16) for i in range(2)])
                vh = Rot([sb2("vh%d" % i, [128, 32, 128], BF16) for i in range(2)])
                m4f = sb2("m4f", [128, 4, 512]); m4b = sb2("m4b", [128, 4, 512], BF16)
                ztp = Rot([ps2("ztp%d" % i, [128, 512]) for i in range(4)])
                ypp = Rot([ps2("ypp%d" % i, [128, 512]) for i in range(2)])
                e_t = Rot([sb2("e_t%d" % i, [128, 512]) for i in range(3)])
                sp_t = Rot([sb2("sp_t%d" % i, [128, 512], BF16) for i in range(5)])
                w_t = Rot([sb2("w_t%d" % i, [128, 512], BF16) for i in range(4)])
                spacc = Rot([sb2("spacc%d" % i, [128, 512]) for i in range(2)])
                spab = Rot([sb2("spab%d" % i, [128, 512], BF16) for i in range(3)])
                yo_t = Rot([sb2("yo_t%d" % i, [128, 512], BF16) for i in range(2)])
                S.dma("sp", m4f[:], mask4.rearrange("k p j -> p k j"))
                S.copy("dve", m4b[:], m4f[:])
                cvd = Rot([sb2("cvd%d" % i, [128, 2048], BF16) for i in range(4)])
                bgq = list(bg_jobs) if "A" in phases else []

                def load_head(h):
                    k_ = kT.next(); q_ = qT.next(); v_ = vh.next()
                    S.dma("sp", k_[:], kT_s[h])
                    S.dma("sp", q_[:], qT_s[h, :, HALO * 128:])
                    S.dma("sp", v_[:], v_s[:, 128 * h:128 * h + 128].rearrange("(t p) d -> p t d", p=128))
                    return k_, q_, v_
                nxt_head = load_head(0)
                for h in range(16):
                    k_, q_, v_ = nxt_head
                    if h + 1 < 16:
                        nxt_head = load_head(h + 1)
                    for sbi in range(5):
                        for _ in range(4):
                            if bgq:
                                conv_job(cvd, *bgq.pop(0))
                        if sbi == 0:
                            q0, W, first_tile = 0, 128, HALO
                            kmax = HALO
                        else:
                            q0, W, first_tile = 128 + 512 * (sbi - 1), 512, OWN0 + 4 * (sbi - 1)
                            kmax = first_tile + 3
                        units = list(range(kmax, -1, -1))
                        yp = ypp.next()
                        sa = spacc.next()
                        state = {"sab": None}

                        def stageA(kb):
                            z = ztp.next()
                            diag = kb >= first_tile
                            S.mm(z[:, :W], k_[:, 128 * kb:128 * kb + 128], q_[:, q0:q0 + W], start=True, stop=True)
                            if diag:
                                mk = cb[:, NEGMASK, :] if W == 128 else m4b[:, kb - first_tile, :]
                                S.mm(z[:, :W], cb[:, IDENT, :], mk, start=False, stop=True)
                            bias = kbias if kb < OWN0 else zero_c
                            e = e_t.next()
                            S.act(e[:, :W], z[:, :W], AF.Exp, bias=bias[:])
                            sp = sp_t.next()
                            S.act(sp[:, :W], e[:, :W], AF.Ln, bias=1.0)
                            return z, sp, bias

                        def stageB(ui, kb, z, sp, bias):
                            first = ui == 0
                            last = ui == len(units) - 1
                            S.mm(z[:, :W], cb[:, NEGTRI, :], sp[:, :W], start=False, stop=True)
                            if not first:
                                S.mm(z[:, :W], cb[:, NEGONES, :], state["sab"][:, :W], start=False, stop=True)
                            if not last:
                                if first:
                                    S.copy("dve", sa[:, :W], sp[:, :W])
                                else:
                                    S.tt("dve", sa[:, :W], sa[:, :W], sp[:, :W], ALU.add)
                                nb = spab.next()
                                S.copy("dve", nb[:, :W], sa[:, :W])
                                state["sab"] = nb
                            w = w_t.next()
                            S.act(w[:, :W], z[:, :W], AF.Exp, bias=bias[:])
                            return w

                        def stageC(ui, kb, w):
                            S.mm(yp[:, :W], v_[:, kb, :], w[:, :W], start=(ui == 0), stop=(ui == len(units) - 1))

                        n_u = len(units)
                        pa = {0: stageA(units[0])}
                        if n_u > 1:
                            pa[1] = stageA(units[1])
                        pb = {0: stageB(0, units[0], *pa.pop(0))}
                        for ui in range(n_u):
                            if ui + 2 < n_u:
                                pa[ui + 2] = stageA(units[ui + 2])
                            if ui + 1 < n_u:
                                pb[ui + 1] = stageB(ui + 1, units[ui + 1], *pa.pop(ui + 1))
                            stageC(ui, units[ui], pb.pop(ui))
                        yo = yo_t.next()
                        S.copy("dve", yo[:, :W], yp[:, :W])
                        S.dma("sp", mT_s[2048 + 128 * h:2048 + 128 * h + 128, q0:q0 + W], yo[:, :W])
                while bgq:
                    conv_job(cvd, *bgq.pop(0))
                S.run()

        def mlp_block(l, x_tiles, B):
            nt_ = len(x_tiles)
            T = 128 * nt_
            T1 = min(T, 512)
            xT = B["xnT"]
            for i, xt_ in enumerate(x_tiles):
                norm_transpose([xt_], B["gTm"][l], xT[:, :, 128 * i:128 * i + 128], B["scr"])
            hT = B["hT"]
            for fb in range(64):
                w = B["wblk"].next()
                S.dma("sp", w[:], Wup[l, fb])
                p = B["ps"].next()
                p2 = B["psd"][3 + fb % 2] if T > 512 else None
                for c in range(16):
                    S.mm(p[:, :T1], w[:, c, :], xT[:, c, :T1], start=(c == 0), stop=(c == 15))
                    if p2 is not None:
                        S.mm(p2[:, :T - 512], w[:, c, :], xT[:, c, 512:T], start=(c == 0), stop=(c == 15))
                r = B["rl"].next()
                S.act(r[:, :T1], p[:, :T1], AF.Relu)
                if p2 is not None:
                    S.act(r[:, 512:T], p2[:, :T - 512], AF.Relu)
                S.tt("dve", hT[:, fb, :T], r[:, :T], r[:, :T], ALU.mult)
            for nb in range(4):
                ncol = slice(512 * nb, 512 * nb + 512)
                for kq in range(4):
                    wd = B["wbig"].next()
                    S.dma("sp", wd[:], Wdn[l, 2048 * kq:2048 * kq + 2048, ncol].rearrange("(c p) n -> p c n", p=128))
                    for i in range(nt_):
                        for c in range(16):
                            S.mm(B["psd"][i][:], hT[:, 16 * kq + c, 128 * i:128 * i + 128], wd[:, c, :],
                                 start=(kq == 0 and c == 0), stop=(kq == 3 and c == 15))
                for i, xt_ in enumerate(x_tiles):
                    S.tt("dve", xt_[:, ncol], xt_[:, ncol], B["psd"][i][:], ALU.add)

        def mlp_bufs(es2):
            sb2 = lambda name, shape, dtype=F32: es2.enter_context(nc.sbuf_tensor(uniq(name), list(shape), dtype))
            ps2 = lambda name, shape, dtype=F32: es2.enter_context(nc.psum_tensor(uniq(name), list(shape), dtype))
            B = {}
            B["gTm"] = [sb2("gTm%d" % l, [128, 16]) for l in range(2)]
            epsc = sb2("epsm", [128, 1])
            B["psd"] = [ps2("psd%d" % i, [128, 512]) for i in range(5)]
            B["ps"] = Rot([ps2("psm%d" % i, [128, 512]) for i in range(2)])
            B["scr"] = dict(ss=Rot([sb2("ssm%d" % i, [128, 4]) for i in range(4)]),
                            junk=sb2("junkm", [128, 2048], BF16), eps=epsc,
                            xn=Rot([sb2("xnm%d" % i, [128, 2048], BF16) for i in range(2)]),
                            pT=Rot([ps2("pTm%d" % i, [128, 4, 128], BF16) for i in range(1)]))
            B["xnT"] = sb2("xnTm", [128, 16, 640], BF16)
            B["hT"] = sb2("hTm", [128, 64, 640], BF16)
            B["wblk"] = Rot([sb2("wblkm%d" % i, [128, 16, 128], BF16) for i in range(3)])
            B["wbig"] = Rot([sb2("wbigm%d" % i, [128, 16, 512], BF16) for i in range(2)])
            B["rl"] = Rot([sb2("rlm%d" % i, [128, 640], BF16) for i in range(2)])
            B["xt"] = [sb2("xtm%d" % i, [128, 2048]) for i in range(5)]
            for l in range(2):
                S.dma("sp", B["gTm"][l][:], mlp_norm[l].rearrange("(c p) -> p c", p=128), slow=True)
            S.memset("dve", epsc[:], EPS)
            return B

        if "E" in phases:
            with contextlib.ExitStack() as es2:
                B = mlp_bufs(es2)
                groups = [[0, 1, 2, 3, 4], [5, 6, 7, 8], [9, 10, 11, 12], [13, 14, 15, 16]]
                for grp in groups:
                    nt_ = len(grp)
                    T = 128 * nt_
                    c0 = 128 * grp[0]
                    xts = [B["xt"][i][:] for i in range(nt_)]
                    for i, qt in enumerate(grp):
                        S.dma("sp", B["xt"][i][:], x_loc[(HALO + qt) * 128:(HALO + qt) * 128 + 128, :])
                    mT = B["hT"]
                    S.dma("sp", mT[:, 0:32, :T], mT_s[:, c0:c0 + T].rearrange("(c p) t -> p c t", p=128))
                    for nb in range(4):
                        ncol = slice(512 * nb, 512 * nb + 512)
                        for kh in range(2):
                            wd = B["wbig"].next()
                            S.dma("sp", wd[:], Wo[2048 * kh:2048 * kh + 2048, ncol].rearrange("(c p) n -> p c n", p=128))
                            for i in range(nt_):
                                for c in range(16):
                                    S.mm(B["psd"][i][:], mT[:, 16 * kh + c, 128 * i:128 * i + 128], wd[:, c, :],
                                         start=(kh == 0 and c == 0), stop=(kh == 1 and c == 15))
                        for i in range(nt_):
                            S.tt("dve", xts[i][:, ncol], xts[i][:, ncol], B["psd"][i][:], ALU.add)
                    mlp_block(0, xts, B)
                    for i, qt in enumerate(grp):
                        S.dma("sp", x2_s[qt * 128:qt * 128 + 128, :], xts[i])
                S.run()

        x3_s = scratch("x3_s", [2048, D], F32)
        if "F" in phases:
            with contextlib.ExitStack() as es2:
                sb2 = lambda name, shape, dtype=F32: es2.enter_context(nc.sbuf_tensor(uniq(name), list(shape), dtype))
                ps2 = lambda name, shape, dtype=F32: es2.enter_context(nc.psum_tensor(uniq(name), list(shape), dtype))
                gp_b = sb2("gp_b", [128, 2048]); pb_b = sb2("pb_b", [128, 2048]); psc_b = sb2("psc_b", [128, 2048])
                PP = sb2("PP", [128, 16, 128]); epsc = sb2("epsf", [128, 1])
                Wp_sb = sb2("Wp_sb", [128, 16, 512], BF16)
                xt = Rot([sb2("xtf%d" % i, [128, 2048]) for i in range(4)])
                hn = Rot([sb2("hnf%d" % i, [128, 2048]) for i in range(4)])
                junk = sb2("junkf", [128, 2048], BF16)
                ss = Rot([sb2("ssf%d" % i, [128, 4]) for i in range(3)])
                dT = Rot([sb2("dTf%d" % i, [128, 16, 128], BF16) for i in range(2)])
                tt_ = Rot([sb2("ttf%d" % i, [128, 512]) for i in range(3)])
                d_ps = Rot([ps2("d_ps%d" % i, [128, 4, 128]) for i in range(3)])
                y_ps = Rot([ps2("yf_ps%d" % i, [128, 512]) for i in range(2)])
                S.dma("sp", gp_b[:], pool_norm.partition_broadcast(128))
                S.dma("sp", pb_b[:], pool_b.partition_broadcast(128))
                S.dma("sp", psc_b[:], pool_scale.partition_broadcast(128))
                S.dma("sp", PP[:], poolP.rearrange("f g k p t -> p (f g k) t"))
                S.dma("sp", Wp_sb[:], Wp.rearrange("(c p) n -> p c n", p=128))
                S.memset("dve", epsc[:], EPS)

                def pnorm(row0):
                    x_ = xt.next()
                    S.dma("sp", x_[:], x2_s[row0:row0 + 128, :])
                    s_ = ss.next()
                    S.act(junk[:], x_[:], AF.Square, accum_out=s_[:, 0:1])
                    S.act(s_[:, 1:2], s_[:, 0:1], AF.Ln, bias=epsc[:], scale=1.0 / D)
                    S.act(s_[:, 2:3], s_[:, 1:2], AF.Exp, scale=-0.5)
                    h_ = hn.next()
                    S.stt("dve", h_[:], x_[:], s_[:, 2:3], gp_b[:], ALU.mult, ALU.mult)
                    return x_, h_

                _, hprev = pnorm(0)
                nxt_pn = pnorm(128)
                for i in range(16):
                    x_, hcur = nxt_pn
                    if i + 1 < 16:
                        nxt_pn = pnorm(128 * (i + 2))
                    fi = 0 if i == 0 else 1
                    d_ = dT.next()
                    for g in range(4):
                        dp = d_ps.next()
                        for j in range(4):
                            cs = slice(512 * g + 128 * j, 512 * g + 128 * j + 128)
                            S.mm(dp[:, j, :], hprev[:, cs], PP[:, (fi * 4 + g) * 2 + 0, :], start=True, stop=False)
                            S.mm(dp[:, j, :], hcur[:, cs], PP[:, (fi * 4 + g) * 2 + 1, :], start=False, stop=True)
                        S.copy("act", d_[:, 4 * g:4 * g + 4, :], dp[:])
                    for g in range(4):
                        yp = y_ps.next()
                        for kc in range(4):
                            S.mm(yp[:], d_[:, 4 * g + kc, :], Wp_sb[:, 4 * g + kc, :], start=(kc == 0), stop=(kc == 3))
                        ncol = slice(512 * g, 512 * g + 512)
                        t_ = tt_.next()
                        S.tt("dve", t_[:], yp[:], pb_b[:, ncol], ALU.add)
                        S.tt("dve", t_[:], t_[:], psc_b[:, ncol], ALU.mult)
                        S.tt("dve", x_[:, ncol], x_[:, ncol], t_[:], ALU.add)
                    S.dma("sp", x3_s[128 * i:128 * i + 128, :], x_[:])
                    hprev = hcur
                S.run()
            with contextlib.ExitStack() as es2:
                B = mlp_bufs(es2)
                for gi in range(4):
                    xts = [B["xt"][i][:] for i in range(4)]
                    for i in range(4):
                        S.dma("sp", B["xt"][i][:], x3_s[(4 * gi + i) * 128:(4 * gi + i) * 128 + 128, :])
                    mlp_block(1, xts, B)
                    for i in range(4):
                        S.dma("sp", out[(4 * gi + i) * 128:(4 * gi + i) * 128 + 128, :], xts[i])
                S.run()
    return nc


def make_consts():
    j = np.arange(128)[:, None]
    s = np.arange(128)[None, :]
    c = np.zeros((7, 128, 128), np.float32)
    c[0] = (j == s)
    c[1] = 1.0
    c[2] = -1.0 * (j >= s)
    c[3] = NEG * (j >= s)
    c[4] = 1.0 * (j <= s)
    c[5] = NEG * (j > s)
    c[6] = -1.0
    m4 = np.zeros((4, 128, 512), np.float32)
    for i in range(4):
        m4[i, :, :128 * i] = NEG
        m4[i, :, 128 * i:128 * i + 128] = c[3]
    return c, m4


def make_poolP(s):
    P = np.zeros((2, 4, 2, 128, 128), np.float32)
    tp = np.arange(128)[:, None]
    t = np.arange(128)[None, :]
    for fi in range(2):
        for gi, win in enumerate((2, 4, 8, 16)):
            start_of_seq = (fi == 0 and s == 0)
            cnt = np.minimum(t + 1, win) if start_of_seq else np.full_like(t, win)
            cur = ((tp <= t) & (t - tp < win)) / cnt.astype(np.float32) - (tp == t)
            prev = ((t - (tp - 128)) < win) / cnt.astype(np.float32)
            if start_of_seq:
                prev = np.zeros_like(prev)
            P[fi, gi, 0] = prev
            P[fi, gi, 1] = cur
    return P


_NC_CACHE = {}


def make_in_maps(inputs):
    x = np.asarray(inputs["x"], np.float32)
    consts, m4 = make_consts()
    shared = {
        "consts": consts, "mask4": m4,
        "hyb_norm": inputs["hyb_norm"][0], "hyb_w_in": inputs["hyb_w_in"][0],
        "ssd_conv_w": inputs["ssd_conv_w"][0], "ssd_conv_b": inputs["ssd_conv_b"][0],
        "ssd_dt_bias": inputs["ssd_dt_bias"][0], "ssd_a_log": inputs["ssd_a_log"][0],
        "ssd_d": inputs["ssd_d"][0], "ssd_out_norm": inputs["ssd_out_norm"][0],
        "sb_q_norm": inputs["sb_q_norm"][0], "sb_k_norm": inputs["sb_k_norm"][0],
        "hyb_w_out": inputs["hyb_w_out"][0], "pool_norm": inputs["pool_norm"][0],
        "pool_w": inputs["pool_w"][0].reshape(2048, 512), "pool_b": inputs["pool_b"][0],
        "pool_scale": inputs["pool_scale"][0], "mlp_norm": inputs["mlp_norm"],
        "mlp_w_up": inputs["mlp_w_up"], "mlp_w_down": inputs["mlp_w_down"],
    }
    shared = {k: np.ascontiguousarray(np.asarray(v, np.float32)) for k, v in shared.items()}
    in_maps = []
    for c in range(8):
        b, s = c // 2, c % 2
        xl = np.zeros((SEQ, D), np.float32)
        if s == 1:
            xl[:] = x[b]
        else:
            xl[2048:] = x[b, :2048]
        m = dict(shared)
        m["x_loc"] = xl
        m["flag"] = np.full((128, 1), float(s), np.float32)
        m["poolP"] = make_poolP(s)
        in_maps.append(m)
    return in_maps


def kernel(**inputs):
    if "nc" not in _NC_CACHE:
        _NC_CACHE["nc"] = build_nc()
    nc = _NC_CACHE["nc"]
    in_maps = make_in_maps(inputs)
    res = run_bass_kernel_spmd(nc, in_maps, core_ids=list(range(8)))
    out = np.zeros((4, SEQ, D), np.float32)
    for c in range(8):
        b, s = c // 2, c % 2
        out[b, 2048 * s:2048 * s + 2048] = res.results[c]["out"]
    return out
```

```python
import contextlib
import numpy as np
import concourse.bass as bass
import concourse.mybir as mybir
from concourse.bass_utils import run_bass_kernel_spmd

F32 = mybir.dt.float32
BF16 = mybir.dt.bfloat16
AF = mybir.ActivationFunctionType
ALU = mybir.AluOpType

D = 2048
SEQ = 4096
NT = 32
HALO = 15
OWN0 = 16
NQT = 17
NEG = -30000.0
CDBG = 99
DH = 16
DGROUPS = ([1, 2], [3, 4], [0])
CSC = list(range(8))
EPS = 1e-6
DFF = 8192
IN_DIM = 11296
C_Z, C_X, C_B, C_C, C_DT, C_Q, C_K, C_V = 0, 2048, 4096, 4608, 5120, 5152, 7200, 9248


class Sched:
    STREAMS = ("pe", "act", "dve", "pool", "sp")

    def __init__(self, nc, es):
        self.nc = nc
        self.csem = {s: es.enter_context(nc.semaphore("c_" + s)) for s in ("pe", "act", "dve", "pool")}
        self.ccount = {s: 0 for s in self.csem}
        self.KQ = 6
        self.dsem = {q: [es.enter_context(nc.semaphore("d_%s%d" % (q, i))) for i in range(self.KQ)]
                     for q in ("sp", "pool", "act")}
        self.dval = {q: [0] * self.KQ for q in self.dsem}
        self.dnext = {q: 0 for q in self.dsem}
        self.reset()

    def reset(self):
        self.ops = []
        self.last_w = {}
        self.readers = {}

    @staticmethod
    def _key(a):
        if isinstance(a, str):
            return a
        if a is None or isinstance(a, (int, float)):
            return None
        sp = str(a.space)
        if "DRAM" in sp:
            return None
        return a.tensor.name

    def add(self, stream, fn, reads, writes, sig=True, dma=False):
        rk = [k for k in (self._key(a) for a in reads) if k is not None]
        wk = [k for k in (self._key(a) for a in writes) if k is not None]
        deps = set()
        for k in rk:
            if k in self.last_w:
                deps.add(self.last_w[k])
        for k in wk:
            if k in self.last_w:
                deps.add(self.last_w[k])
            for r in self.readers.get(k, ()):
                deps.add(r)
        i = len(self.ops)
        for d in deps:
            if not (self.ops[d]["stream"] == "pe" and stream == "pe" and not dma):
                self.ops[d]["sig"] = True
        self.ops.append(dict(stream=stream, fn=fn, deps=deps, sig=sig, dma=dma))
        for k in rk:
            self.readers.setdefault(k, []).append(i)
        for k in wk:
            self.last_w[k] = i
            self.readers[k] = []
        return i

    def mm(self, out, lhsT, rhs, start=True, stop=True):
        self.add("pe", lambda e: e.matmul(out, lhsT=lhsT, rhs=rhs, start=start, stop=stop),
                 [lhsT, rhs], [out], sig=stop)

    def tr(self, out, in_, ident):
        self.add("pe", lambda e: e.transpose(out=out, in_=in_, identity=ident), [in_, ident], [out])

    def act(self, out, in_, func, bias=None, scale=None, accum_out=None):
        kw = {}
        if bias is not None:
            kw["bias"] = bias
        if scale is not None:
            kw["scale"] = scale
        if accum_out is not None:
            kw["accum_out"] = accum_out
        self.add("act", lambda e: e.activation(out=out, in_=in_, func=func, **kw),
                 [in_, bias, scale], [out, accum_out])

    def tt(self, eng, out, in0, in1, op):
        self.add(eng, lambda e: e.tensor_tensor(out=out, in0=in0, in1=in1, op=op), [in0, in1], [out])

    def ts(self, eng, out, in0, s1, s2, op0, op1=None):
        if op1 is None:
            self.add(eng, lambda e: e.tensor_scalar(out=out, in0=in0, scalar1=s1, scalar2=None, op0=op0),
                     [in0, s1], [out])
        else:
            self.add(eng, lambda e: e.tensor_scalar(out=out, in0=in0, scalar1=s1, scalar2=s2, op0=op0, op1=op1),
                     [in0, s1, s2], [out])

    def stt(self, eng, out, in0, scalar, in1, op0, op1):
        self.add(eng, lambda e: e.scalar_tensor_tensor(out=out, in0=in0, scalar=scalar, in1=in1, op0=op0, op1=op1),
                 [in0, scalar, in1], [out])

    def copy(self, eng, out, in_):
        if eng == "act":
            self.add("act", lambda e: e.copy(out=out, in_=in_), [in_], [out])
        else:
            self.add(eng, lambda e: e.tensor_copy(out=out, in_=in_), [in_], [out])

    def memset(self, eng, out, val):
        self.add(eng, lambda e: e.memset(out, val), [], [out])

    def dma(self, q, out, in_, extra_r=(), extra_w=(), slow=False):
        if slow:
            fn = lambda e: e.dma_start(out=out, in_=in_, allow_slow_non_contiguous=True)
        else:
            fn = lambda e: e.dma_start(out=out, in_=in_)
        self.add(q, fn, [in_] + list(extra_r), [out] + list(extra_w), dma=True)

    def run(self, name=None):
        nc = self.nc
        ops = self.ops
        per = {s: [i for i, o in enumerate(ops) if o["stream"] == s] for s in self.STREAMS}
        token = [None] * len(ops)
        for s in ("pe", "act", "dve", "pool"):
            lst = [i for i in per[s] if not ops[i]["dma"]]
            if lst:
                ops[lst[-1]]["sig"] = True
            cnt = self.ccount[s]
            vals = {}
            for i in lst:
                if ops[i]["sig"]:
                    cnt += 1
                    vals[i] = cnt
            nxt = None
            for i in reversed(lst):
                if ops[i]["sig"]:
                    nxt = vals[i]
                token[i] = (self.csem[s], nxt)
            self.ccount[s] = cnt
        prevtok = {}
        for s in ("sp", "pool", "act"):
            for i in per[s]:
                if not ops[i]["dma"]:
                    continue
                k = self.dnext[s]
                self.dnext[s] = (k + 1) % self.KQ
                if self.dval[s][k] > 0:
                    prevtok[i] = (self.dsem[s][k], self.dval[s][k])
                self.dval[s][k] += 16
                token[i] = (self.dsem[s][k], self.dval[s][k])
        final_d = {s: [(self.dsem[s][k], self.dval[s][k]) for k in range(self.KQ) if self.dval[s][k] > 0]
                   for s in self.dsem}
        csem_ids = {id(v): k for k, v in self.csem.items()}

        def emit(stream):
            def body(e):
                waited = {}

                def wait(tok):
                    sem, val = tok
                    if waited.get(id(sem), 0) >= val:
                        return
                    waited[id(sem)] = val
                    e.wait_ge(sem, val)
                for i in per[stream]:
                    o = ops[i]
                    for d in sorted(o["deps"]):
                        od = ops[d]
                        if od["stream"] == "pe" and stream == "pe" and not od["dma"] and not o["dma"]:
                            continue
                        wait(token[d])
                    if i in prevtok:
                        wait(prevtok[i])
                    ins = o["fn"](e)
                    sem, val = token[i]
                    if o["dma"]:
                        ins.then_inc(sem, 16)
                    elif o["sig"]:
                        ins.then_inc(sem, 1)
                if stream in final_d:
                    for tok in final_d[stream]:
                        if any(ops[i]["dma"] for i in per[stream]):
                            wait(tok)
            return body

        with nc.Block() as block:
            if per["pe"]:
                block.tensor(emit("pe"))
            if per["act"]:
                block.scalar(emit("act"))
            if per["dve"]:
                block.vector(emit("dve"))
            if per["pool"]:
                block.gpsimd(emit("pool"))
            if per["sp"]:
                block.sync(emit("sp"))
        self.reset()


class Rot:
    def __init__(self, bufs):
        self.bufs = bufs
        self.i = 0

    def next(self):
        b = self.bufs[self.i % len(self.bufs)]
        self.i += 1
        return b


def bcast_rows(ap_1d, n, parts=128):
    return ap_1d.partition_broadcast(parts)


def build_nc(phases="ABCDEF", taps=()):
    nc = bass.Bass("TRN2", target_bir_lowering=False)
    dt_in = lambda name, shape: nc.dram_tensor(name, list(shape), F32, kind="ExternalInput").ap()

    def scratch(name, shape, dtype):
        if name in taps:
            return nc.dram_tensor(name, list(shape), dtype, kind="ExternalOutput").ap()
        return nc.dram_tensor(name, list(shape), dtype).ap()

    _uc = [0]

    def uniq(name):
        _uc[0] += 1
        return "%s_u%d" % (name, _uc[0])

    x_loc = dt_in("x_loc", [SEQ, D])
    flag = dt_in("flag", [128, 1])
    consts = dt_in("consts", [7, 128, 128])
    mask4 = dt_in("mask4", [4, 128, 512])
    poolP = dt_in("poolP", [2, 4, 2, 128, 128])
    hyb_norm = dt_in("hyb_norm", [D])
    w_in = dt_in("hyb_w_in", [D, IN_DIM])
    conv_w = dt_in("ssd_conv_w", [4, 3072])
    conv_b = dt_in("ssd_conv_b", [3072])
    dt_bias = dt_in("ssd_dt_bias", [32])
    a_log = dt_in("ssd_a_log", [32])
    ssd_d = dt_in("ssd_d", [32])
    out_norm = dt_in("ssd_out_norm", [D])
    q_norm = dt_in("sb_q_norm", [128])
    k_norm = dt_in("sb_k_norm", [128])
    w_out = dt_in("hyb_w_out", [4096, D])
    pool_norm = dt_in("pool_norm", [D])
    pool_w = dt_in("pool_w", [2048, 512])
    pool_b = dt_in("pool_b", [D])
    pool_scale = dt_in("pool_scale", [D])
    mlp_norm = dt_in("mlp_norm", [2, D])
    w_up = dt_in("mlp_w_up", [2, D, DFF])
    w_dn = dt_in("mlp_w_down", [2, DFF, D])
    out = nc.dram_tensor("out", [2048, D], F32, kind="ExternalOutput").ap()

    NFM = 36 + 36
    fm_cols = ([C_X + 128 * i for i in range(16)] + [C_B + 128 * i for i in range(4)] +
               [C_K + 128 * i for i in range(16)] + [C_C + 128 * i for i in range(4)] +
               [C_Q + 128 * i for i in range(16)])
    NFM = len(fm_cols)
    Wfm = scratch("Wfm", [NFM, 128, 16, 128], BF16)
    Wtm = scratch("Wtm", [8, 128, 16, 512], BF16)
    Wdt = scratch("Wdt", [128, 16, 32], BF16)
    Wo = scratch("Wo", [4096, D], BF16)
    Wp = scratch("Wp", [2048, 512], BF16)
    Wup = scratch("Wup", [2, 64, 128, 16, 128], BF16)
    Wdn = scratch("Wdn", [2, DFF, D], BF16)
    xc_s = scratch("xc_s", [3072, SEQ], BF16)
    z_s = scratch("z_s", [SEQ, 2048], F32)
    dt_s = scratch("dt_s", [SEQ, 32], F32)
    qT_s = scratch("qT_s", [16, 128, SEQ], BF16)
    kT_s = scratch("kT_s", [16, 128, SEQ], BF16)
    v_s = scratch("v_s", [SEQ, 2048], BF16)
    mT_s = scratch("mT_s", [4096, NQT * 128], BF16)
    x2_s = scratch("x2_s", [NQT * 128, D], F32)

    with contextlib.ExitStack() as es:
        S = Sched(nc, es)
        sb = lambda name, shape, dtype=F32: es.enter_context(nc.sbuf_tensor(name, list(shape), dtype))
        cf = sb("cf", [128, 7, 128])
        cb = sb("cb", [128, 7, 128], BF16)
        flag_t = sb("flag_t", [128, 1])
        kbias = sb("kbias", [128, 1])
        zero_c = sb("zero_c", [128, 1])
        IDENT, ONES, NEGTRI, NEGMASK, TRIU, NEGSSD, NEGONES = range(7)

        S.dma("sp", cf[:], consts.rearrange("k p j -> p k j"))
        S.dma("sp", flag_t[:], flag)
        S.copy("dve", cb[:], cf[:])
        S.ts("dve", kbias[:], flag_t[:], -1.0, -NEG, ALU.add, ALU.mult)
        S.memset("dve", zero_c[:], 0.0)
        S.run()

        pre_jobs, bg_jobs = [], []
        for bi, c0 in enumerate(fm_cols):
            pre_jobs.append((Wfm[bi], w_in[:, c0:c0 + 128].rearrange("(c p) j -> p c j", p=128), [128, 16, 128]))
        for bi in range(8):
            c0 = (C_Z if bi < 4 else C_V) + 512 * (bi % 4)
            for ch in range(4):
                pre_jobs.append((Wtm[bi, :, 4 * ch:4 * ch + 4, :],
                                 w_in[512 * ch:512 * ch + 512, c0:c0 + 512].rearrange("(c p) j -> p c j", p=128),
                                 [128, 4, 512]))
        pre_jobs.append((Wdt[:], w_in[:, C_DT:C_DT + 32].rearrange("(c p) j -> p c j", p=128), [128, 16, 32]))
        for r in range(32):
            bg_jobs.append((Wo[128 * r:128 * r + 128, :], w_out[128 * r:128 * r + 128, :], [128, 2048]))
        for l in range(2):
            for fb in range(64):
                bg_jobs.append((Wup[l, fb], w_up[l, :, 128 * fb:128 * fb + 128].rearrange("(c p) j -> p c j", p=128),
                                [128, 16, 128]))
            for r in range(64):
                bg_jobs.append((Wdn[l, 128 * r:128 * r + 128, :], w_dn[l, 128 * r:128 * r + 128, :], [128, 2048]))
            if l == 0:
                for r in range(4):
                    bg_jobs.append((Wp[512 * r:512 * r + 512, :].rearrange("(c p) j -> p c j", p=128),
                                    pool_w[512 * r:512 * r + 512, :].rearrange("(c p) j -> p c j", p=128), [128, 4, 512]))

        def conv_job(cv, dst, src, shape):
            t = cv.next()
            n = 1
            for v_ in shape[1:]:
                n *= v_
            view = t[:, 0:n]
            if len(shape) == 3:
                view = view.rearrange("p (a b) -> p a b", a=shape[1])
            S.dma("pool", view, src)
            S.dma("sp", dst, view)

        if "A" in phases:
            with contextlib.ExitStack() as es2:
                cv = Rot([es2.enter_context(nc.sbuf_tensor(uniq("cv%d" % i), [128, 2048], BF16)) for i in range(4)])
                for j in pre_jobs:
                    conv_job(cv, *j)
                if "D" not in phases:
                    for j in bg_jobs:
                        conv_job(cv, *j)
                S.run()

        def norm_transpose(xt_tiles, gT, xnT, scr):
            for i, xt in enumerate(xt_tiles):
                ss = scr["ss"].next()
                S.act(scr["junk"][:], xt, AF.Square, accum_out=ss[:, 0:1])
                S.act(ss[:, 1:2], ss[:, 0:1], AF.Ln, bias=scr["eps"][:], scale=1.0 / D)
                S.act(ss[:, 2:3], ss[:, 1:2], AF.Exp, scale=-0.5)
                xn = scr["xn"].next()
                S.ts("dve", xn[:], xt, ss[:, 2:3], None, ALU.mult)
                for c4 in range(4):
                    pT = scr["pT"].next()
                    for j in range(4):
                        c = 4 * c4 + j
                        S.tr(pT[:, j, :], xn[:, 128 * c:128 * c + 128], cb[:, IDENT, :])
                    eng = "pool" if False else "dve"
                    S.tt(eng, xnT[:, 4 * c4:4 * c4 + 4, 128 * i:128 * i + 128], pT[:],
                         gT[:, 4 * c4:4 * c4 + 4].unsqueeze(2).to_broadcast([128, 4, 128]), ALU.mult)

        if "B" in phases:
            with contextlib.ExitStack() as es2:
                sb2 = lambda name, shape, dtype=F32: es2.enter_context(nc.sbuf_tensor(uniq(name), list(shape), dtype))
                ps2 = lambda name, shape, dtype=F32: es2.enter_context(nc.psum_tensor(uniq(name), list(shape), dtype))
                gT = sb2("gT", [128, 16])
                gq = sb2("gq", [128, 1])
                gk = sb2("gk", [128, 1])
                epsc = sb2("epsc", [128, 1])
                wdt = sb2("wdt", [128, 16, 32], BF16)
                xt = Rot([sb2("xt%d" % i, [128, 2048]) for i in range(3)])
                scr = dict(ss=Rot([sb2("ss%d" % i, [128, 4]) for i in range(4)]),
                           junk=sb2("junk", [128, 2048], BF16), eps=epsc,
                           xn=Rot([sb2("xn%d" % i, [128, 2048], BF16) for i in range(2)]),
                           pT=Rot([ps2("pT%d" % i, [128, 4, 128], BF16) for i in range(2)]))
                xnT = Rot([sb2("xnT%d" % i, [128, 16, 512], BF16) for i in range(2)])
                wblk = Rot([sb2("wblk%d" % i, [128, 16, 128], BF16) for i in range(3)])
                wtm = Rot([sb2("wtm%d" % i, [128, 16, 512], BF16) for i in range(3)])
                ps = Rot([ps2("ps%d" % i, [128, 512]) for i in range(4)])
                pss = Rot([ps2("pss%d" % i, [128, 512]) for i in range(2)])
                of = Rot([sb2("of%d" % i, [128, 512]) for i in range(3)])
                ob = Rot([sb2("ob%d" % i, [128, 512], BF16) for i in range(3)])
                sq = Rot([sb2("sq%d" % i, [128, 512], BF16) for i in range(3)])
                rs = Rot([sb2("rs%d" % i, [128, 512]) for i in range(2)])
                cwTb = sb2("cwTb", [128, 4, 24]); cbTb = sb2("cbTb", [128, 24])
                halo_t = sb2("halo_t", [128, 24, 3])
                xiR = Rot([sb2("xiR%d" % i, [128, 516]) for i in range(3)])
                accR = Rot([sb2("accR%d" % i, [128, 512]) for i in range(3)])
                for k in range(4):
                    S.dma("sp", cwTb[:, k, :], conv_w[k, :].rearrange("(c p) -> p c", p=128), slow=True)
                S.dma("sp", cbTb[:], conv_b.rearrange("(c p) -> p c", p=128), slow=True)
                S.memset("dve", halo_t[:], 0.0)
                S.dma("sp", gT[:], hyb_norm.rearrange("(c p) -> p c", p=128), slow=True)
                S.dma("sp", gq[:], q_norm.rearrange("(p o) -> p o", o=1), slow=True)
                S.dma("sp", gk[:], k_norm.rearrange("(p o) -> p o", o=1), slow=True)
                S.ts("dve", gq[:], gq[:], 128.0 ** -0.5, None, ALU.mult)
                S.memset("dve", epsc[:], EPS)
                S.dma("sp", wdt[:], Wdt[:])
                jobs = []
                pend_tail = []
                STQ = "pool"
                for g in range(8):
                    T0 = 512 * g
                    xT = xnT.next()
                    for i in range(4):
                        t = xt.next()
                        jobs.append((lambda t=t, r0=T0 + 128 * i: S.dma("sp", t[:], x_loc[r0:r0 + 128, :]),
                                     lambda t=t, xT=xT, i=i: norm_transpose([t[:]], gT, xT[:, :, 128 * i:128 * i + 128], scr)))
                    nblk = 36 if g < 3 else NFM

                    def fm_compute(w, bi, xT=xT, T0=T0):
                        p = ps.next()
                        for c in range(16):
                            S.mm(p[:], w[:, c, :], xT[:, c, :], start=(c == 0), stop=(c == 15))
                        if pend_tail:
                            pend_tail.pop(0)()
                        c0 = fm_cols[bi]
                        if c0 >= C_Q:
                            isq = c0 < C_K
                            hd = (c0 - (C_Q if isq else C_K)) // 128
                            s_ = sq.next()
                            S.act(s_[:], p[:], AF.Square)

                            def tail(p=p, s_=s_, isq=isq, hd=hd):
                                p2 = pss.next()
                                S.mm(p2[:], cb[:, ONES, :], s_[:])
                                r = rs.next()
                                S.act(r[:], p2[:], AF.Ln, bias=epsc[:], scale=1.0 / 128)
                                S.act(r[:], r[:], AF.Exp, scale=-0.5)
                                o = ob.next()
                                S.stt("dve", o[:], p[:], (gq if isq else gk)[:, 0:1], r[:], ALU.mult, ALU.mult)
                                S.dma(STQ, (qT_s if isq else kT_s)[hd, :, T0:T0 + 512], o[:])
                            pend_tail.append(tail)
                        else:
                            cc = (c0 - C_X) // 128
                            xi = xiR.next()
                            S.copy("act", xi[:, 4:516], p[:])
                            S.copy("dve", xi[:, 1:4], halo_t[:, cc, :])
                            S.copy("dve", halo_t[:, cc, :], xi[:, 513:516])
                            ac = accR.next()
                            S.ts("dve", ac[:], xi[:, 1:513], cwTb[:, 0, cc:cc + 1], cbTb[:, cc:cc + 1], ALU.mult, ALU.add)
                            for k in range(1, 4):
                                S.stt("dve", ac[:], xi[:, 1 + k:1 + k + 512], cwTb[:, k, cc:cc + 1], ac[:], ALU.mult, ALU.add)

                            def tail(ac=ac, cc=cc):
                                o = ob.next()
                                S.act(o[:], ac[:], AF.Silu)
                                S.dma(STQ, xc_s[128 * cc:128 * cc + 128, T0:T0 + 512], o[:])
                            pend_tail.append(tail)

                    for bi in range(nblk):
                        w = wblk.next()
                        jobs.append((lambda w=w, bi=bi: S.dma("sp", w[:], Wfm[bi]),
                                     lambda w=w, bi=bi, f=fm_compute: f(w, bi)))

                    def tm_compute(w, bi, xT=xT, T0=T0):
                        while pend_tail:
                            pend_tail.pop(0)()
                        for i in range(4):
                            p = ps.next()
                            for c in range(16):
                                S.mm(p[:], xT[:, c, 128 * i:128 * i + 128], w[:, c, :], start=(c == 0), stop=(c == 15))
                            r0 = T0 + 128 * i
                            if bi < 4:
                                o = of.next()
                                S.copy("dve" if i % 2 else "act", o[:], p[:])
                                S.dma(STQ, z_s[r0:r0 + 128, 512 * bi:512 * bi + 512], o[:])
                            else:
                                o = ob.next()
                                S.copy("dve" if i % 2 else "act", o[:], p[:])
                                S.dma(STQ, v_s[r0:r0 + 128, 512 * (bi - 4):512 * (bi - 4) + 512], o[:])

                    for bi in range(8):
                        if bi < 4 and g < 3:
                            continue
                        w = wtm.next()
                        jobs.append((lambda w=w, bi=bi: S.dma("sp", w[:], Wtm[bi]),
                                     lambda w=w, bi=bi, f=tm_compute: f(w, bi)))

                    def dt_compute(xT=xT, T0=T0):
                        for i in range(4):
                            p = ps.next()
                            for c in range(16):
                                S.mm(p[:, 0:32], xT[:, c, 128 * i:128 * i + 128], wdt[:, c, :], start=(c == 0), stop=(c == 15))
                            o = of.next()
                            S.copy("dve", o[:, 0:32], p[:, 0:32])
                            S.dma(STQ, dt_s[T0 + 128 * i:T0 + 128 * i + 128, :], o[:, 0:32])
                    jobs.append((lambda: None, dt_compute))
                DEPTH = 2
                for j in range(min(DEPTH, len(jobs))):
                    jobs[j][0]()
                for j in range(len(jobs)):
                    if j + DEPTH < len(jobs):
                        jobs[j + DEPTH][0]()
                    jobs[j][1]()
                S.run()

        if "C" in phases:
            with contextlib.ExitStack() as es2:
                sb2 = lambda name, shape, dtype=F32: es2.enter_context(nc.sbuf_tensor(uniq(name), list(shape), dtype))
                ps2 = lambda name, shape, dtype=F32: es2.enter_context(nc.psum_tensor(uniq(name), list(shape), dtype))
                cwT = sb2("cwT", [128, 4, 24]); cbT = sb2("cbT", [128, 24])
                dtb = sb2("dtb", [128, 32]); a_b = sb2("a_b", [128, 32]); D_b = sb2("D_b", [128, 32])
                on_b = sb2("on_b", [128, 2048]); epsc = sb2("epsc", [128, 1])
                stT = [sb2("stT%d" % g, [128, 512]) for g in range(4)]
                stB = [sb2("stB%d" % g, [128, 512], BF16) for g in range(4)]
                xcR = Rot([sb2("xc%d" % i, [128, 24, 512], BF16) for i in range(2)])
                xs_tm = sb2("xs_tm", [128, 4, 2048], BF16)
                B_tm = sb2("B_tm", [128, 4, 512], BF16)
                xdt = sb2("xdt", [128, 4, 2048], BF16)
                xdts = sb2("xdts", [128, 4, 2048], BF16)
                dtv = sb2("dtv", [128, 4, 32]); da = sb2("da", [128, 4, 32]); acs = sb2("acs", [128, 4, 32])
                nacs = sb2("nacs", [128, 4, 32]); cdb = sb2("cdb", [128, 4, 32]); dte = sb2("dte", [128, 4, 32])
                ea = sb2("ea", [128, 4, 32]); w1 = sb2("w1", [128, 4, 32])
                cbm = Rot([sb2("cbm%d" % i, [128, 4, 128]) for i in range(2)])
                zt = Rot([sb2("zt%d" % i, [128, 2048]) for i in range(1)])
                sz = Rot([sb2("sz%d" % i, [128, 2048]) for i in range(1)])
                tda = Rot([sb2("tda%d" % i, [128, 128]) for i in range(4)])
                dec = Rot([sb2("dec%d" % i, [128, 128]) for i in range(4)])
                MT = Rot([sb2("MT%d" % i, [128, 128], BF16) for i in range(3)])
                t1 = Rot([sb2("t1_%d" % i, [128, 512]) for i in range(2)])
                t2 = Rot([sb2("t2_%d" % i, [128, 512]) for i in range(2)])
                ssq = Rot([sb2("ssq%d" % i, [128, 4]) for i in range(3)])
                junk = sb2("junkc", [128, 512], BF16)
                yn = Rot([sb2("yn%d" % i, [128, 512], BF16) for i in range(2)])
                yT = Rot([sb2("yT%d" % i, [128, 4, 128], BF16) for i in range(2)])
                cb_ps = ps2("cb_ps", [128, 4, 128])
                y_ps = Rot([ps2("y_ps%d" % i, [128, 512]) for i in range(1)])
                yo_ps = Rot([ps2("yo_ps%d" % i, [128, 512]) for i in range(1)])
                s_ps = Rot([ps2("s_ps%d" % i, [128, 512]) for i in range(1)])
                pT = Rot([ps2("pTc%d" % i, [128, 4, 128], BF16) for i in range(1)])
                at_ps = ps2("at_ps", [128, 2, 128]); acs_ps = at_ps[:, 0, :]; tot_ps = at_ps[:, 1, :]
                seg_ps = Rot([ps2("seg_ps%d" % i, [128, 128]) for i in range(2)])

                for k in range(4):
                    S.dma("sp", cwT[:, k, :], conv_w[k, :].rearrange("(c p) -> p c", p=128), slow=True)
                S.dma("sp", cbT[:], conv_b.rearrange("(c p) -> p c", p=128), slow=True)
                S.dma("sp", dtb[:], dt_bias.partition_broadcast(128))
                S.dma("sp", a_b[:], a_log.partition_broadcast(128))
                S.dma("sp", D_b[:], ssd_d.partition_broadcast(128))
                S.dma("sp", on_b[:], out_norm.partition_broadcast(128))
                S.act(a_b[:], a_b[:], AF.Exp)
                S.ts("dve", a_b[:], a_b[:], -1.0, None, ALU.mult)
                S.memset("dve", epsc[:], EPS)
                for g in range(4):
                    S.memset("dve", stT[g][:], 0.0)
                    S.memset("pool", stB[g][:], 0.0)
                ei = [0]

                def alt():
                    ei[0] += 1
                    return "dve" if ei[0] % 2 else "pool"

                xc_next = xcR.next()
                S.dma("sp", xc_next[:], xc_s[:, 0:512].rearrange("(c p) t -> p c t", p=128))
                for SC in CSC:
                    T0 = 512 * SC
                    xc = xc_next
                    if SC + 1 < 8:
                        xc_next = xcR.next()
                        S.dma("sp", xc_next[:], xc_s[:, T0 + 512:T0 + 1024].rearrange("(c p) t -> p c t", p=128))
                    if CDBG < 2:
                        continue
                    for ch in range(4):
                        for c4 in range(5):
                            p = pT.next()
                            for j in range(4):
                                S.tr(p[:, j, :], xc[:, 4 * c4 + j, 128 * ch:128 * ch + 128], cb[:, IDENT, :])
                            dst = xs_tm[:, ch, 512 * c4:512 * c4 + 512] if c4 < 4 else B_tm[:, ch, :]
                            S.copy("act" if c4 % 2 else "dve", dst, p[:].rearrange("p a b -> p (a b)"))
                    if CDBG < 2.05:
                        continue
                    S.dma("sp", dtv[:], dt_s[T0:T0 + 512, :].rearrange("(c p) h -> p c h", p=128))
                    S.tt("dve", dtv[:], dtv[:], dtb[:].unsqueeze(1).to_broadcast([128, 4, 32]), ALU.add)
                    S.act(dtv[:], dtv[:], AF.Exp)
                    S.act(dtv[:], dtv[:], AF.Ln, bias=1.0)
                    if CDBG == 2.1:
                        continue
                    S.tt("dve", da[:], dtv[:], a_b[:].unsqueeze(1).to_broadcast([128, 4, 32]), ALU.mult)
                    daf = da[:].rearrange("p c h -> p (c h)")
                    S.mm(acs_ps, cf[:, TRIU, :], daf)
                    S.mm(tot_ps, cf[:, ONES, :], daf)
                    if CDBG == 2.2:
                        continue
                    S.copy("dve", acs[:].rearrange("p c h -> p (c h)"), acs_ps)
                    S.ts("dve", nacs[:], acs[:], -1.0, None, ALU.mult)
                    S.copy("dve", w1[:].rearrange("p c h -> p (c h)"), tot_ps)
                    S.act(cdb[:], w1[:], AF.Exp)
                    S.tt("dve", dte[:], w1[:], acs[:], ALU.subtract)
                    S.act(dte[:], dte[:], AF.Exp)
                    S.act(ea[:], acs[:], AF.Exp)
                    S.tt("dve", w1[:], dtv[:], dte[:], ALU.mult)
                    for ch in range(4 if CDBG != 3 else 0):
                        xv = xs_tm[:, ch, :].rearrange("p (h q) -> p h q", q=64)
                        S.tt("dve", xdt[:, ch, :].rearrange("p (h q) -> p h q", q=64), xv,
                             dtv[:, ch, :].unsqueeze(2).to_broadcast([128, 32, 64]), ALU.mult)
                        S.tt("pool", xdts[:, ch, :].rearrange("p (h q) -> p h q", q=64), xv,
                             w1[:, ch, :].unsqueeze(2).to_broadcast([128, 32, 64]), ALU.mult)
                    if CDBG < 4:
                        continue
                    for ch in range(4):
                        lc = 4 * SC + ch
                        cs = slice(128 * ch, 128 * ch + 128)
                        if lc == OWN0:
                            for g in range(4):
                                S.ts("dve", stT[g][:], stT[g][:], flag_t[:, 0:1], None, ALU.mult)
                                S.copy("pool", stB[g][:], stT[g][:])
                        if lc >= HALO:
                            for g in range(4):
                                S.mm(cb_ps[:, g, :], xc[:, 16 + g, cs], xc[:, 20 + g, cs])
                            cm = cbm.next()
                            S.copy("dve", cm[:], cb_ps[:])
                            z_t = zt.next()
                            S.dma("sp", z_t[:], z_s[T0 + 128 * ch:T0 + 128 * ch + 128, :])
                            s_z = sz.next()
                            S.act(s_z[:], z_t[:], AF.Silu)
                            for g in range(4):
                                yp = y_ps.next(); yo = yo_ps.next()
                                S.mm(yo[:], xc[:, 20 + g, cs], stB[g][:])
                                a2 = t2.next()
                                S.tt("pool", a2[:].rearrange("p (h q) -> p h q", q=64),
                                     xs_tm[:, ch, 512 * g:512 * g + 512].rearrange("p (h q) -> p h q", q=64),
                                     D_b[:, 8 * g:8 * g + 8].unsqueeze(2).to_broadcast([128, 8, 64]), ALU.mult)
                                def s1(hh, g=g, ch=ch):
                                    h = 8 * g + hh
                                    td = tda.next()
                                    S.ts("dve", td[:], cf[:, TRIU, :], da[:, ch, h:h + 1], None, ALU.mult)
                                    sg = seg_ps.next()
                                    S.mm(sg[:], cf[:, ONES, :], td[:], start=True, stop=False)
                                    S.mm(sg[:], cf[:, IDENT, :], cf[:, NEGSSD, :], start=False, stop=True)
                                    dc = dec.next()
                                    S.act(dc[:], sg[:], AF.Exp, bias=nacs[:, ch, h:h + 1])
                                    return dc

                                def s2(hh, dc, g=g, ch=ch, yp=yp, cm=cm):
                                    h = 8 * g + hh
                                    m = MT.next()
                                    S.tt("dve", m[:], dc[:], cm[:, g, :], ALU.mult)
                                    S.mm(yp[:, 64 * hh:64 * hh + 64], m[:], xdt[:, ch, 64 * h:64 * h + 64])
                                dcs = {0: s1(0), 1: s1(1)}
                                for hh in range(8):
                                    if hh + 2 < 8:
                                        dcs[hh + 2] = s1(hh + 2)
                                    s2(hh, dcs.pop(hh))
                                a1 = t1.next()
                                S.tt("dve", a1[:].rearrange("p (h q) -> p h q", q=64), yo[:].rearrange("p (h q) -> p h q", q=64),
                                     ea[:, ch, 8 * g:8 * g + 8].unsqueeze(2).to_broadcast([128, 8, 64]), ALU.mult)
                                S.tt("dve", a1[:], a1[:], yp[:], ALU.add)
                                S.tt("dve", a1[:], a1[:], a2[:], ALU.add)
                                S.tt("dve", a1[:], a1[:], s_z[:, 512 * g:512 * g + 512], ALU.mult)
                                sq_ = ssq.next()
                                S.act(junk[:], a1[:], AF.Square, accum_out=sq_[:, 0:1])
                                S.act(sq_[:, 1:2], sq_[:, 0:1], AF.Ln, bias=epsc[:], scale=1.0 / 512)
                                S.act(sq_[:, 2:3], sq_[:, 1:2], AF.Exp, scale=-0.5)
                                y_n = yn.next()
                                S.stt("dve", y_n[:], a1[:], sq_[:, 2:3], on_b[:, 512 * g:512 * g + 512], ALU.mult, ALU.mult)
                                p = pT.next()
                                for j in range(4):
                                    S.tr(p[:, j, :], y_n[:, 128 * j:128 * j + 128], cb[:, IDENT, :])
                                y_t = yT.next()
                                S.copy("act", y_t[:], p[:])
                                col0 = (lc - HALO) * 128
                                S.dma("sp", mT_s[512 * g:512 * g + 512, col0:col0 + 128].rearrange("(j p) t -> p j t", p=128), y_t[:])
                        if lc < NT - 1:
                            for g in range(4):
                                sp_ = s_ps.next()
                                S.mm(sp_[:], B_tm[:, ch, 128 * g:128 * g + 128], xdts[:, ch, 512 * g:512 * g + 512])
                                S.tt("dve", stT[g][:].rearrange("p (h q) -> p h q", q=64), stT[g][:].rearrange("p (h q) -> p h q", q=64),
                                     cdb[:, ch, 8 * g:8 * g + 8].unsqueeze(2).to_broadcast([128, 8, 64]), ALU.mult)
                                S.tt("dve", stT[g][:], stT[g][:], sp_[:], ALU.add)
                                S.copy("act", stB[g][:], stT[g][:])
                S.run()

        if "D" in phases:
            with contextlib.ExitStack() as es2:
                sb2 = lambda name, shape, dtype=F32: es2.enter_context(nc.sbuf_tensor(uniq(name), list(shape), dtype))
                ps2 = lambda name, shape, dtype=F32: es2.enter_context(nc.psum_tensor(uniq(name), list(shape), dtype))
                kT = Rot([sb2("kT%d" % i, [128, SEQ], BF16) for i in range(2)])
                qT = Rot([sb2("qT%d" % i, [128, NQT * 128], BF16) for i in range(2)])
                vh = Rot([sb2("vh%d" % i, [128, 32, 128], BF16) for i in range(2)])
                m4f = sb2("m4f", [128, 4, 512]); m4b = sb2("m4b", [128, 4, 512], BF16)
                WW = 1024
                ztp = Rot([ps2("ztp%d" % i, [128, WW]) for i in range(3)])
                ypp = Rot([ps2("ypp%d" % i, [128, WW]) for i in range(1)])
                e_t = Rot([sb2("e_t%d" % i, [128, WW]) for i in range(3)])
                sp_t = Rot([sb2("sp_t%d" % i, [128, WW], BF16) for i in range(5)])
                w_t = Rot([sb2("w_t%d" % i, [128, WW], BF16) for i in range(4)])
                spacc = Rot([sb2("spacc%d" % i, [128, WW]) for i in range(2)])
                spab = Rot([sb2("spab%d" % i, [128, WW], BF16) for i in range(3)])
                yo_t = Rot([sb2("yo_t%d" % i, [128, WW], BF16) for i in range(2)])
                S.dma("sp", m4f[:], mask4.rearrange("k p j -> p k j"))
                S.copy("dve", m4b[:], m4f[:])
                cvd = Rot([sb2("cvd%d" % i, [128, 2048], BF16) for i in range(4)])
                bgq = list(bg_jobs) if "A" in phases else []

                def load_head(h):
                    k_ = kT.next(); q_ = qT.next(); v_ = vh.next()
                    S.dma("sp", k_[:], kT_s[h])
                    S.dma("sp", q_[:], qT_s[h, :, HALO * 128:])
                    S.dma("sp", v_[:], v_s[:, 128 * h:128 * h + 128].rearrange("(t p) d -> p t d", p=128))
                    return k_, q_, v_

                def sb_info(sbi):
                    if sbi == 0:
                        return 0, 128, HALO, HALO
                    ft = OWN0 + 4 * (sbi - 1)
                    return 128 + 512 * (sbi - 1), 512, ft, ft + 3

                def attn_group(h, k_, q_, v_, sbis):
                    segs = []
                    for si, sbi in enumerate(sbis):
                        q0, W, ft, kmax = sb_info(sbi)
                        segs.append(dict(q0=q0, W=W, ft=ft, kmax=kmax, off=512 * si))
                    kmax_all = max(g_["kmax"] for g_ in segs)
                    units = list(range(kmax_all, -1, -1))
                    yp = ypp.next()
                    sa = spacc.next()
                    state = {"sab": None}

                    def active(kb):
                        return [g_ for g_ in segs if kb <= g_["kmax"]]

                    def span(act):
                        lo = min(g_["off"] for g_ in act)
                        hi = max(g_["off"] + g_["W"] for g_ in act)
                        return slice(lo, hi)

                    def stageA(kb):
                        z = ztp.next()
                        act = active(kb)
                        for g_ in act:
                            cs = slice(g_["off"], g_["off"] + g_["W"])
                            S.mm(z[:, cs], k_[:, 128 * kb:128 * kb + 128], q_[:, g_["q0"]:g_["q0"] + g_["W"]], start=True, stop=False)
                            if kb >= g_["ft"]:
                                mk = cb[:, NEGMASK, :] if g_["W"] == 128 else m4b[:, kb - g_["ft"], :]
                                S.mm(z[:, cs], cb[:, IDENT, :], mk, start=False, stop=False)
                        bias = kbias if kb < OWN0 else zero_c
                        sl = span(act)
                        e = e_t.next()
                        S.act(e[:, sl], z[:, sl], AF.Exp, bias=bias[:])
                        sp = sp_t.next()
                        S.act(sp[:, sl], e[:, sl], AF.Ln, bias=1.0)
                        return z, sp, bias

                    def stageB(kb, z, sp, bias):
                        act = active(kb)
                        sl = span(act)
                        last = kb == 0
                        for g_ in act:
                            cs = slice(g_["off"], g_["off"] + g_["W"])
                            S.mm(z[:, cs], cb[:, NEGTRI, :], sp[:, cs], start=False, stop=(kb == g_["kmax"]))
                            if kb != g_["kmax"]:
                                S.mm(z[:, cs], cb[:, NEGONES, :], state["sab"][:, cs], start=False, stop=True)
                        if not last:
                            firsts = [g_ for g_ in act if kb == g_["kmax"]]
                            rest = [g_ for g_ in act if kb != g_["kmax"]]
                            for g_ in firsts:
                                cs = slice(g_["off"], g_["off"] + g_["W"])
                                S.copy("dve", sa[:, cs], sp[:, cs])
                            if rest:
                                rs_ = span(rest)
                                S.tt("dve", sa[:, rs_], sa[:, rs_], sp[:, rs_], ALU.add)
                            nb = spab.next()
                            S.copy("dve", nb[:, sl], sa[:, sl])
                            state["sab"] = nb
                        w = w_t.next()
                        S.act(w[:, sl], z[:, sl], AF.Exp, bias=bias[:])
                        return w

                    def stageC(kb, w):
                        for g_ in active(kb):
                            cs = slice(g_["off"], g_["off"] + g_["W"])
                            S.mm(yp[:, cs], v_[:, kb, :], w[:, cs], start=(kb == g_["kmax"]), stop=(kb == 0))

                    n_u = len(units)
                    pa = {0: stageA(units[0])}
                    if n_u > 1:
                        pa[1] = stageA(units[1])
                    pb = {0: stageB(units[0], *pa.pop(0))}
                    for ui in range(n_u):
                        if ui + 2 < n_u:
                            pa[ui + 2] = stageA(units[ui + 2])
                        if ui + 1 < n_u:
                            pb[ui + 1] = stageB(units[ui + 1], *pa.pop(ui + 1))
                        stageC(units[ui], pb.pop(ui))
                    yo = yo_t.next()
                    for g_ in segs:
                        cs = slice(g_["off"], g_["off"] + g_["W"])
                        S.copy("dve", yo[:, cs], yp[:, cs])
                        S.dma("sp", mT_s[2048 + 128 * h:2048 + 128 * h + 128, g_["q0"]:g_["q0"] + g_["W"]], yo[:, cs])

                nxt_head = load_head(0)
                for h in range(DH):
                    k_, q_, v_ = nxt_head
                    if h + 1 < DH:
                        nxt_head = load_head(h + 1)
                    for sbis in DGROUPS:
                        for _ in range(7):
                            if bgq:
                                conv_job(cvd, *bgq.pop(0))
                        attn_group(h, k_, q_, v_, sbis)
                while bgq:
                    conv_job(cvd, *bgq.pop(0))
                S.run()

        def mlp_block(l, x_tiles, B):
            nt_ = len(x_tiles)
            T = 128 * nt_
            T1 = min(T, 512)
            xT = B["xnT"]
            for i, xt_ in enumerate(x_tiles):
                norm_transpose([xt_], B["gTm"][l], xT[:, :, 128 * i:128 * i + 128], B["scr"])
            hT = B["hT"]
            for fb in range(64):
                w = B["wblk"].next()
                S.dma("sp", w[:], Wup[l, fb])
                p = B["ps"].next()
                p2 = B["psd"][3 + fb % 2] if T > 512 else None
                for c in range(16):
                    S.mm(p[:, :T1], w[:, c, :], xT[:, c, :T1], start=(c == 0), stop=(c == 15))
                    if p2 is not None:
                        S.mm(p2[:, :T - 512], w[:, c, :], xT[:, c, 512:T], start=(c == 0), stop=(c == 15))
                r = B["rl"].next()
                S.act(r[:, :T1], p[:, :T1], AF.Relu)
                if p2 is not None:
                    S.act(r[:, 512:T], p2[:, :T - 512], AF.Relu)
                S.tt("dve", hT[:, fb, :T], r[:, :T], r[:, :T], ALU.mult)
            for nb in range(4):
                ncol = slice(512 * nb, 512 * nb + 512)
                for kq in range(4):
                    wd = B["wbig"].next()
                    S.dma("sp", wd[:], Wdn[l, 2048 * kq:2048 * kq + 2048, ncol].rearrange("(c p) n -> p c n", p=128))
                    for i in range(nt_):
                        for c in range(16):
                            S.mm(B["psd"][i][:], hT[:, 16 * kq + c, 128 * i:128 * i + 128], wd[:, c, :],
                                 start=(kq == 0 and c == 0), stop=(kq == 3 and c == 15))
                for i, xt_ in enumerate(x_tiles):
                    S.tt("dve", xt_[:, ncol], xt_[:, ncol], B["psd"][i][:], ALU.add)

        def mlp_bufs(es2):
            sb2 = lambda name, shape, dtype=F32: es2.enter_context(nc.sbuf_tensor(uniq(name), list(shape), dtype))
            ps2 = lambda name, shape, dtype=F32: es2.enter_context(nc.psum_tensor(uniq(name), list(shape), dtype))
            B = {}
            B["gTm"] = [sb2("gTm%d" % l, [128, 16]) for l in range(2)]
            epsc = sb2("epsm", [128, 1])
            B["psd"] = [ps2("psd%d" % i, [128, 512]) for i in range(5)]
            B["ps"] = Rot([ps2("psm%d" % i, [128, 512]) for i in range(2)])
            B["scr"] = dict(ss=Rot([sb2("ssm%d" % i, [128, 4]) for i in range(4)]),
                            junk=sb2("junkm", [128, 2048], BF16), eps=epsc,
                            xn=Rot([sb2("xnm%d" % i, [128, 2048], BF16) for i in range(2)]),
                            pT=Rot([ps2("pTm%d" % i, [128, 4, 128], BF16) for i in range(1)]))
            B["xnT"] = sb2("xnTm", [128, 16, 640], BF16)
            B["hT"] = sb2("hTm", [128, 64, 640], BF16)
            B["wblk"] = Rot([sb2("wblkm%d" % i, [128, 16, 128], BF16) for i in range(3)])
            B["wbig"] = Rot([sb2("wbigm%d" % i, [128, 16, 512], BF16) for i in range(2)])
            B["rl"] = Rot([sb2("rlm%d" % i, [128, 640], BF16) for i in range(2)])
            B["xt"] = [sb2("xtm%d" % i, [128, 2048]) for i in range(5)]
            for l in range(2):
                S.dma("sp", B["gTm"][l][:], mlp_norm[l].rearrange("(c p) -> p c", p=128), slow=True)
            S.memset("dve", epsc[:], EPS)
            return B

        if "E" in phases:
            with contextlib.ExitStack() as es2:
                B = mlp_bufs(es2)
                groups = [[0, 1, 2, 3, 4], [5, 6, 7, 8], [9, 10, 11, 12], [13, 14, 15, 16]]
                for grp in groups:
                    nt_ = len(grp)
                    T = 128 * nt_
                    c0 = 128 * grp[0]
                    xts = [B["xt"][i][:] for i in range(nt_)]
                    for i, qt in enumerate(grp):
                        S.dma("sp", B["xt"][i][:], x_loc[(HALO + qt) * 128:(HALO + qt) * 128 + 128, :])
                    mT = B["hT"]
                    S.dma("sp", mT[:, 0:32, :T], mT_s[:, c0:c0 + T].rearrange("(c p) t -> p c t", p=128))
                    for nb in range(4):
                        ncol = slice(512 * nb, 512 * nb + 512)
                        for kh in range(2):
                            wd = B["wbig"].next()
                            S.dma("sp", wd[:], Wo[2048 * kh:2048 * kh + 2048, ncol].rearrange("(c p) n -> p c n", p=128))
                            for i in range(nt_):
                                for c in range(16):
                                    S.mm(B["psd"][i][:], mT[:, 16 * kh + c, 128 * i:128 * i + 128], wd[:, c, :],
                                         start=(kh == 0 and c == 0), stop=(kh == 1 and c == 15))
                        for i in range(nt_):
                            S.tt("dve", xts[i][:, ncol], xts[i][:, ncol], B["psd"][i][:], ALU.add)
                    mlp_block(0, xts, B)
                    for i, qt in enumerate(grp):
                        S.dma("sp", x2_s[qt * 128:qt * 128 + 128, :], xts[i])
                S.run()

        x3_s = scratch("x3_s", [2048, D], F32)
        if "F" in phases:
            with contextlib.ExitStack() as es2:
                sb2 = lambda name, shape, dtype=F32: es2.enter_context(nc.sbuf_tensor(uniq(name), list(shape), dtype))
                ps2 = lambda name, shape, dtype=F32: es2.enter_context(nc.psum_tensor(uniq(name), list(shape), dtype))
                gp_b = sb2("gp_b", [128, 2048]); pb_b = sb2("pb_b", [128, 2048]); psc_b = sb2("psc_b", [128, 2048])
                PP = sb2("PP", [128, 16, 128]); epsc = sb2("epsf", [128, 1])
                Wp_sb = sb2("Wp_sb", [128, 16, 512], BF16)
                xt = Rot([sb2("xtf%d" % i, [128, 2048]) for i in range(4)])
                hn = Rot([sb2("hnf%d" % i, [128, 2048]) for i in range(4)])
                junk = sb2("junkf", [128, 2048], BF16)
                ss = Rot([sb2("ssf%d" % i, [128, 4]) for i in range(3)])
                dT = Rot([sb2("dTf%d" % i, [128, 16, 128], BF16) for i in range(2)])
                tt_ = Rot([sb2("ttf%d" % i, [128, 512]) for i in range(3)])
                d_ps = Rot([ps2("d_ps%d" % i, [128, 4, 128]) for i in range(3)])
                y_ps = Rot([ps2("yf_ps%d" % i, [128, 512]) for i in range(2)])
                S.dma("sp", gp_b[:], pool_norm.partition_broadcast(128))
                S.dma("sp", pb_b[:], pool_b.partition_broadcast(128))
                S.dma("sp", psc_b[:], pool_scale.partition_broadcast(128))
                S.dma("sp", PP[:], poolP.rearrange("f g k p t -> p (f g k) t"))
                S.dma("sp", Wp_sb[:], Wp.rearrange("(c p) n -> p c n", p=128))
                S.memset("dve", epsc[:], EPS)

                def pnorm(row0):
                    x_ = xt.next()
                    S.dma("sp", x_[:], x2_s[row0:row0 + 128, :])
                    s_ = ss.next()
                    S.act(junk[:], x_[:], AF.Square, accum_out=s_[:, 0:1])
                    S.act(s_[:, 1:2], s_[:, 0:1], AF.Ln, bias=epsc[:], scale=1.0 / D)
                    S.act(s_[:, 2:3], s_[:, 1:2], AF.Exp, scale=-0.5)
                    h_ = hn.next()
                    S.stt("dve", h_[:], x_[:], s_[:, 2:3], gp_b[:], ALU.mult, ALU.mult)
                    return x_, h_

                _, hprev = pnorm(0)
                nxt_pn = pnorm(128)
                for i in range(16):
                    x_, hcur = nxt_pn
                    if i + 1 < 16:
                        nxt_pn = pnorm(128 * (i + 2))
                    fi = 0 if i == 0 else 1
                    d_ = dT.next()
                    for g in range(4):
                        dp = d_ps.next()
                        for j in range(4):
                            cs = slice(512 * g + 128 * j, 512 * g + 128 * j + 128)
                            S.mm(dp[:, j, :], hprev[:, cs], PP[:, (fi * 4 + g) * 2 + 0, :], start=True, stop=False)
                            S.mm(dp[:, j, :], hcur[:, cs], PP[:, (fi * 4 + g) * 2 + 1, :], start=False, stop=True)
                        S.copy("act", d_[:, 4 * g:4 * g + 4, :], dp[:])
                    for g in range(4):
                        yp = y_ps.next()
                        for kc in range(4):
                            S.mm(yp[:], d_[:, 4 * g + kc, :], Wp_sb[:, 4 * g + kc, :], start=(kc == 0), stop=(kc == 3))
                        ncol = slice(512 * g, 512 * g + 512)
                        t_ = tt_.next()
                        S.tt("dve", t_[:], yp[:], pb_b[:, ncol], ALU.add)
                        S.tt("dve", t_[:], t_[:], psc_b[:, ncol], ALU.mult)
                        S.tt("dve", x_[:, ncol], x_[:, ncol], t_[:], ALU.add)
                    S.dma("sp", x3_s[128 * i:128 * i + 128, :], x_[:])
                    hprev = hcur
                S.run()
            with contextlib.ExitStack() as es2:
                B = mlp_bufs(es2)
                for gi in range(4):
                    xts = [B["xt"][i][:] for i in range(4)]
                    for i in range(4):
                        S.dma("sp", B["xt"][i][:], x3_s[(4 * gi + i) * 128:(4 * gi + i) * 128 + 128, :])
                    mlp_block(1, xts, B)
                    for i in range(4):
                        S.dma("sp", out[(4 * gi + i) * 128:(4 * gi + i) * 128 + 128, :], xts[i])
                S.run()
    return nc


def make_consts():
    j = np.arange(128)[:, None]
    s = np.arange(128)[None, :]
    c = np.zeros((7, 128, 128), np.float32)
    c[0] = (j == s)
    c[1] = 1.0
    c[2] = -1.0 * (j >= s)
    c[3] = NEG * (j >= s)
    c[4] = 1.0 * (j <= s)
    c[5] = NEG * (j > s)
    c[6] = -1.0
    m4 = np.zeros((4, 128, 512), np.float32)
    for i in range(4):
        m4[i, :, :128 * i] = NEG
        m4[i, :, 128 * i:128 * i + 128] = c[3]
    return c, m4


def make_poolP(s):
    P = np.zeros((2, 4, 2, 128, 128), np.float32)
    tp = np.arange(128)[:, None]
    t = np.arange(128)[None, :]
    for fi in range(2):
        for gi, win in enumerate((2, 4, 8, 16)):
            start_of_seq = (fi == 0 and s == 0)
            cnt = np.minimum(t + 1, win) if start_of_seq else np.full_like(t, win)
            cur = ((tp <= t) & (t - tp < win)) / cnt.astype(np.float32) - (tp == t)
            prev = ((t - (tp - 128)) < win) / cnt.astype(np.float32)
            if start_of_seq:
                prev = np.zeros_like(prev)
            P[fi, gi, 0] = prev
            P[fi, gi, 1] = cur
    return P


_NC_CACHE = {}


def make_in_maps(inputs):
    x = np.asarray(inputs["x"], np.float32)
    consts, m4 = make_consts()
    shared = {
        "consts": consts, "mask4": m4,
        "hyb_norm": inputs["hyb_norm"][0], "hyb_w_in": inputs["hyb_w_in"][0],
        "ssd_conv_w": inputs["ssd_conv_w"][0], "ssd_conv_b": inputs["ssd_conv_b"][0],
        "ssd_dt_bias": inputs["ssd_dt_bias"][0], "ssd_a_log": inputs["ssd_a_log"][0],
        "ssd_d": inputs["ssd_d"][0], "ssd_out_norm": inputs["ssd_out_norm"][0],
        "sb_q_norm": inputs["sb_q_norm"][0], "sb_k_norm": inputs["sb_k_norm"][0],
        "hyb_w_out": inputs["hyb_w_out"][0], "pool_norm": inputs["pool_norm"][0],
        "pool_w": inputs["pool_w"][0].reshape(2048, 512), "pool_b": inputs["pool_b"][0],
        "pool_scale": inputs["pool_scale"][0], "mlp_norm": inputs["mlp_norm"],
        "mlp_w_up": inputs["mlp_w_up"], "mlp_w_down": inputs["mlp_w_down"],
    }
    shared = {k: np.ascontiguousarray(np.asarray(v, np.float32)) for k, v in shared.items()}
    in_maps = []
    for c in range(8):
        b, s = c // 2, c % 2
        xl = np.zeros((SEQ, D), np.float32)
        if s == 1:
            xl[:] = x[b]
        else:
            xl[2048:] = x[b, :2048]
        m = dict(shared)
        m["x_loc"] = xl
        m["flag"] = np.full((128, 1), float(s), np.float32)
        m["poolP"] = make_poolP(s)
        in_maps.append(m)
    return in_maps


def kernel(**inputs):
    if "nc" not in _NC_CACHE:
        _NC_CACHE["nc"] = build_nc()
    nc = _NC_CACHE["nc"]
    in_maps = make_in_maps(inputs)
    res = run_bass_kernel_spmd(nc, in_maps, core_ids=list(range(8)))
    out = np.zeros((4, SEQ, D), np.float32)
    for c in range(8):
        b, s = c // 2, c % 2
        out[b, 2048 * s:2048 * s + 2048] = res.results[c]["out"]
    return out
```

```python
import contextlib
import numpy as np
import concourse.bass as bass
import concourse.mybir as mybir
from concourse.bass_utils import run_bass_kernel_spmd

F32 = mybir.dt.float32
BF16 = mybir.dt.bfloat16
AF = mybir.ActivationFunctionType
ALU = mybir.AluOpType

D = 2048
SEQ = 4096
NT = 32
HALO = 15
OWN0 = 16
NQT = 17
NEG = -30000.0
CDBG = 99
DH = 16
DGROUPS = ([1, 2], [3, 4], [0])
CSC = list(range(8))
EPS = 1e-6
DFF = 8192
IN_DIM = 11296
C_Z, C_X, C_B, C_C, C_DT, C_Q, C_K, C_V = 0, 2048, 4096, 4608, 5120, 5152, 7200, 9248


class Sched:
    STREAMS = ("pe", "act", "dve", "pool", "sp")

    def __init__(self, nc, es):
        self.nc = nc
        self.csem = {s: es.enter_context(nc.semaphore("c_" + s)) for s in ("pe", "act", "dve", "pool")}
        self.ccount = {s: 0 for s in self.csem}
        self.KQ = 6
        self.dsem = {q: [es.enter_context(nc.semaphore("d_%s%d" % (q, i))) for i in range(self.KQ)]
                     for q in ("sp", "pool", "act")}
        self.dval = {q: [0] * self.KQ for q in self.dsem}
        self.dnext = {q: 0 for q in self.dsem}
        self.reset()

    def reset(self):
        self.ops = []
        self.last_w = {}
        self.readers = {}

    @staticmethod
    def _key(a):
        if isinstance(a, str):
            return a
        if a is None or isinstance(a, (int, float)):
            return None
        sp = str(a.space)
        if "DRAM" in sp:
            return None
        return a.tensor.name

    def add(self, stream, fn, reads, writes, sig=True, dma=False):
        rk = [k for k in (self._key(a) for a in reads) if k is not None]
        wk = [k for k in (self._key(a) for a in writes) if k is not None]
        deps = set()
        for k in rk:
            if k in self.last_w:
                deps.add(self.last_w[k])
        for k in wk:
            if k in self.last_w:
                deps.add(self.last_w[k])
            for r in self.readers.get(k, ()):
                deps.add(r)
        i = len(self.ops)
        for d in deps:
            if not (self.ops[d]["stream"] == "pe" and stream == "pe" and not dma):
                self.ops[d]["sig"] = True
        self.ops.append(dict(stream=stream, fn=fn, deps=deps, sig=sig, dma=dma))
        for k in rk:
            self.readers.setdefault(k, []).append(i)
        for k in wk:
            self.last_w[k] = i
            self.readers[k] = []
        return i

    def mm(self, out, lhsT, rhs, start=True, stop=True):
        self.add("pe", lambda e: e.matmul(out, lhsT=lhsT, rhs=rhs, start=start, stop=stop),
                 [lhsT, rhs], [out], sig=stop)

    def tr(self, out, in_, ident):
        self.add("pe", lambda e: e.transpose(out=out, in_=in_, identity=ident), [in_, ident], [out])

    def act(self, out, in_, func, bias=None, scale=None, accum_out=None):
        kw = {}
        if bias is not None:
            kw["bias"] = bias
        if scale is not None:
            kw["scale"] = scale
        if accum_out is not None:
            kw["accum_out"] = accum_out
        self.add("act", lambda e: e.activation(out=out, in_=in_, func=func, **kw),
                 [in_, bias, scale], [out, accum_out])

    def tt(self, eng, out, in0, in1, op):
        self.add(eng, lambda e: e.tensor_tensor(out=out, in0=in0, in1=in1, op=op), [in0, in1], [out])

    def ts(self, eng, out, in0, s1, s2, op0, op1=None):
        if op1 is None:
            self.add(eng, lambda e: e.tensor_scalar(out=out, in0=in0, scalar1=s1, scalar2=None, op0=op0),
                     [in0, s1], [out])
        else:
            self.add(eng, lambda e: e.tensor_scalar(out=out, in0=in0, scalar1=s1, scalar2=s2, op0=op0, op1=op1),
                     [in0, s1, s2], [out])

    def stt(self, eng, out, in0, scalar, in1, op0, op1):
        self.add(eng, lambda e: e.scalar_tensor_tensor(out=out, in0=in0, scalar=scalar, in1=in1, op0=op0, op1=op1),
                 [in0, scalar, in1], [out])

    def copy(self, eng, out, in_):
        if eng == "act":
            self.add("act", lambda e: e.copy(out=out, in_=in_), [in_], [out])
        else:
            self.add(eng, lambda e: e.tensor_copy(out=out, in_=in_), [in_], [out])

    def memset(self, eng, out, val):
        self.add(eng, lambda e: e.memset(out, val), [], [out])

    def dma(self, q, out, in_, extra_r=(), extra_w=(), slow=False):
        if slow:
            fn = lambda e: e.dma_start(out=out, in_=in_, allow_slow_non_contiguous=True)
        else:
            fn = lambda e: e.dma_start(out=out, in_=in_)
        self.add(q, fn, [in_] + list(extra_r), [out] + list(extra_w), dma=True)

    def run(self, name=None):
        nc = self.nc
        ops = self.ops
        per = {s: [i for i, o in enumerate(ops) if o["stream"] == s] for s in self.STREAMS}
        token = [None] * len(ops)
        for s in ("pe", "act", "dve", "pool"):
            lst = [i for i in per[s] if not ops[i]["dma"]]
            if lst:
                ops[lst[-1]]["sig"] = True
            cnt = self.ccount[s]
            vals = {}
            for i in lst:
                if ops[i]["sig"]:
                    cnt += 1
                    vals[i] = cnt
            nxt = None
            for i in reversed(lst):
                if ops[i]["sig"]:
                    nxt = vals[i]
                token[i] = (self.csem[s], nxt)
            self.ccount[s] = cnt
        prevtok = {}
        for s in ("sp", "pool", "act"):
            for i in per[s]:
                if not ops[i]["dma"]:
                    continue
                k = self.dnext[s]
                self.dnext[s] = (k + 1) % self.KQ
                if self.dval[s][k] > 0:
                    prevtok[i] = (self.dsem[s][k], self.dval[s][k])
                self.dval[s][k] += 16
                token[i] = (self.dsem[s][k], self.dval[s][k])
        final_d = {s: [(self.dsem[s][k], self.dval[s][k]) for k in range(self.KQ) if self.dval[s][k] > 0]
                   for s in self.dsem}
        csem_ids = {id(v): k for k, v in self.csem.items()}

        def emit(stream):
            def body(e):
                waited = {}

                def wait(tok):
                    sem, val = tok
                    if waited.get(id(sem), 0) >= val:
                        return
                    waited[id(sem)] = val
                    e.wait_ge(sem, val)
                for i in per[stream]:
                    o = ops[i]
                    for d in sorted(o["deps"]):
                        od = ops[d]
                        if od["stream"] == "pe" and stream == "pe" and not od["dma"] and not o["dma"]:
                            continue
                        wait(token[d])
                    if i in prevtok:
                        wait(prevtok[i])
                    ins = o["fn"](e)
                    sem, val = token[i]
                    if o["dma"]:
                        ins.then_inc(sem, 16)
                    elif o["sig"]:
                        ins.then_inc(sem, 1)
                if stream in final_d:
                    for tok in final_d[stream]:
                        if any(ops[i]["dma"] for i in per[stream]):
                            wait(tok)
            return body

        with nc.Block() as block:
            if per["pe"]:
                block.tensor(emit("pe"))
            if per["act"]:
                block.scalar(emit("act"))
            if per["dve"]:
                block.vector(emit("dve"))
            if per["pool"]:
                block.gpsimd(emit("pool"))
            if per["sp"]:
                block.sync(emit("sp"))
        self.reset()


class Rot:
    def __init__(self, bufs):
        self.bufs = bufs
        self.i = 0

    def next(self):
        b = self.bufs[self.i % len(self.bufs)]
        self.i += 1
        return b


def bcast_rows(ap_1d, n, parts=128):
    return ap_1d.partition_broadcast(parts)


def build_nc(phases="ABCDEF", taps=()):
    nc = bass.Bass("TRN2", target_bir_lowering=False)
    dt_in = lambda name, shape: nc.dram_tensor(name, list(shape), F32, kind="ExternalInput").ap()

    def scratch(name, shape, dtype):
        if name in taps:
            return nc.dram_tensor(name, list(shape), dtype, kind="ExternalOutput").ap()
        return nc.dram_tensor(name, list(shape), dtype).ap()

    _uc = [0]

    def uniq(name):
        _uc[0] += 1
        return "%s_u%d" % (name, _uc[0])

    x_loc = dt_in("x_loc", [SEQ, D])
    flag = dt_in("flag", [128, 1])
    consts = dt_in("consts", [7, 128, 128])
    mask4 = dt_in("mask4", [4, 128, 512])
    poolP = dt_in("poolP", [2, 4, 2, 128, 128])
    hyb_norm = dt_in("hyb_norm", [D])
    w_in = dt_in("hyb_w_in", [D, IN_DIM])
    conv_w = dt_in("ssd_conv_w", [4, 3072])
    conv_b = dt_in("ssd_conv_b", [3072])
    dt_bias = dt_in("ssd_dt_bias", [32])
    a_log = dt_in("ssd_a_log", [32])
    ssd_d = dt_in("ssd_d", [32])
    out_norm = dt_in("ssd_out_norm", [D])
    q_norm = dt_in("sb_q_norm", [128])
    k_norm = dt_in("sb_k_norm", [128])
    w_out = dt_in("hyb_w_out", [4096, D])
    pool_norm = dt_in("pool_norm", [D])
    pool_w = dt_in("pool_w", [2048, 512])
    pool_b = dt_in("pool_b", [D])
    pool_scale = dt_in("pool_scale", [D])
    mlp_norm = dt_in("mlp_norm", [2, D])
    w_up = dt_in("mlp_w_up", [2, D, DFF])
    w_dn = dt_in("mlp_w_down", [2, DFF, D])
    out = nc.dram_tensor("out", [2048, D], F32, kind="ExternalOutput").ap()

    NFM = 36 + 36
    fm_cols = ([C_X + 128 * i for i in range(16)] + [C_B + 128 * i for i in range(4)] +
               [C_K + 128 * i for i in range(16)] + [C_C + 128 * i for i in range(4)] +
               [C_Q + 128 * i for i in range(16)])
    NFM = len(fm_cols)
    Wfm = scratch("Wfm", [NFM, 128, 16, 128], BF16)
    Wtm = scratch("Wtm", [8, 128, 16, 512], BF16)
    Wdt = scratch("Wdt", [128, 16, 32], BF16)
    Wo = scratch("Wo", [4096, D], BF16)
    Wp = scratch("Wp", [2048, 512], BF16)
    Wup = scratch("Wup", [2, 64, 128, 16, 128], BF16)
    Wdn = scratch("Wdn", [2, DFF, D], BF16)
    xc_s = scratch("xc_s", [3072, SEQ], BF16)
    z_s = scratch("z_s", [SEQ, 2048], F32)
    dt_s = scratch("dt_s", [SEQ, 32], F32)
    qT_s = scratch("qT_s", [16, 128, SEQ], BF16)
    kT_s = scratch("kT_s", [16, 128, SEQ], BF16)
    v_s = scratch("v_s", [SEQ, 2048], BF16)
    mT_s = scratch("mT_s", [4096, NQT * 128], BF16)
    x2_s = scratch("x2_s", [NQT * 128, D], F32)

    with contextlib.ExitStack() as es:
        S = Sched(nc, es)
        sb = lambda name, shape, dtype=F32: es.enter_context(nc.sbuf_tensor(name, list(shape), dtype))
        cf = sb("cf", [128, 7, 128])
        cb = sb("cb", [128, 7, 128], BF16)
        flag_t = sb("flag_t", [128, 1])
        kbias = sb("kbias", [128, 1])
        zero_c = sb("zero_c", [128, 1])
        IDENT, ONES, NEGTRI, NEGMASK, TRIU, NEGSSD, NEGONES = range(7)

        S.dma("sp", cf[:], consts.rearrange("k p j -> p k j"))
        S.dma("sp", flag_t[:], flag)
        S.copy("dve", cb[:], cf[:])
        S.ts("dve", kbias[:], flag_t[:], -1.0, -NEG, ALU.add, ALU.mult)
        S.memset("dve", zero_c[:], 0.0)
        S.run()

        pre_jobs, bg_jobs = [], []
        for bi, c0 in enumerate(fm_cols):
            pre_jobs.append((Wfm[bi], w_in[:, c0:c0 + 128].rearrange("(c p) j -> p c j", p=128), [128, 16, 128]))
        for bi in range(8):
            c0 = (C_Z if bi < 4 else C_V) + 512 * (bi % 4)
            for ch in range(4):
                pre_jobs.append((Wtm[bi, :, 4 * ch:4 * ch + 4, :],
                                 w_in[512 * ch:512 * ch + 512, c0:c0 + 512].rearrange("(c p) j -> p c j", p=128),
                                 [128, 4, 512]))
        pre_jobs.append((Wdt[:], w_in[:, C_DT:C_DT + 32].rearrange("(c p) j -> p c j", p=128), [128, 16, 32]))
        for r in range(32):
            bg_jobs.append((Wo[128 * r:128 * r + 128, :], w_out[128 * r:128 * r + 128, :], [128, 2048]))
        for l in range(2):
            for fb in range(64):
                bg_jobs.append((Wup[l, fb], w_up[l, :, 128 * fb:128 * fb + 128].rearrange("(c p) j -> p c j", p=128),
                                [128, 16, 128]))
            for r in range(64):
                bg_jobs.append((Wdn[l, 128 * r:128 * r + 128, :], w_dn[l, 128 * r:128 * r + 128, :], [128, 2048]))
            if l == 0:
                for r in range(4):
                    bg_jobs.append((Wp[512 * r:512 * r + 512, :].rearrange("(c p) j -> p c j", p=128),
                                    pool_w[512 * r:512 * r + 512, :].rearrange("(c p) j -> p c j", p=128), [128, 4, 512]))

        def conv_job(cv, dst, src, shape):
            t = cv.next()
            n = 1
            for v_ in shape[1:]:
                n *= v_
            view = t[:, 0:n]
            if len(shape) == 3:
                view = view.rearrange("p (a b) -> p a b", a=shape[1])
            S.dma("pool", view, src)
            S.dma("sp", dst, view)

        if "A" in phases:
            with contextlib.ExitStack() as es2:
                cv = Rot([es2.enter_context(nc.sbuf_tensor(uniq("cv%d" % i), [128, 2048], BF16)) for i in range(4)])
                for j in pre_jobs:
                    conv_job(cv, *j)
                if "D" not in phases:
                    for j in bg_jobs:
                        conv_job(cv, *j)
                S.run()

        def norm_transpose(xt_tiles, gT, xnT, scr):
            for i, xt in enumerate(xt_tiles):
                ss = scr["ss"].next()
                S.act(scr["junk"][:], xt, AF.Square, accum_out=ss[:, 0:1])
                S.act(ss[:, 1:2], ss[:, 0:1], AF.Ln, bias=scr["eps"][:], scale=1.0 / D)
                S.act(ss[:, 2:3], ss[:, 1:2], AF.Exp, scale=-0.5)
                xn = scr["xn"].next()
                S.ts("dve", xn[:], xt, ss[:, 2:3], None, ALU.mult)
                for c4 in range(4):
                    pT = scr["pT"].next()
                    for j in range(4):
                        c = 4 * c4 + j
                        S.tr(pT[:, j, :], xn[:, 128 * c:128 * c + 128], cb[:, IDENT, :])
                    eng = "pool" if False else "dve"
                    S.tt(eng, xnT[:, 4 * c4:4 * c4 + 4, 128 * i:128 * i + 128], pT[:],
                         gT[:, 4 * c4:4 * c4 + 4].unsqueeze(2).to_broadcast([128, 4, 128]), ALU.mult)

        if "B" in phases:
            with contextlib.ExitStack() as es2:
                sb2 = lambda name, shape, dtype=F32: es2.enter_context(nc.sbuf_tensor(uniq(name), list(shape), dtype))
                ps2 = lambda name, shape, dtype=F32: es2.enter_context(nc.psum_tensor(uniq(name), list(shape), dtype))
                gT = sb2("gT", [128, 16])
                gq = sb2("gq", [128, 1])
                gk = sb2("gk", [128, 1])
                epsc = sb2("epsc", [128, 1])
                wdt = sb2("wdt", [128, 16, 32], BF16)
                xt = Rot([sb2("xt%d" % i, [128, 2048]) for i in range(3)])
                scr = dict(ss=Rot([sb2("ss%d" % i, [128, 4]) for i in range(4)]),
                           junk=sb2("junk", [128, 2048], BF16), eps=epsc,
                           xn=Rot([sb2("xn%d" % i, [128, 2048], BF16) for i in range(2)]),
                           pT=Rot([ps2("pT%d" % i, [128, 4, 128], BF16) for i in range(2)]))
                xnT = Rot([sb2("xnT%d" % i, [128, 16, 512], BF16) for i in range(2)])
                wblk = Rot([sb2("wblk%d" % i, [128, 16, 128], BF16) for i in range(3)])
                wtm = Rot([sb2("wtm%d" % i, [128, 16, 512], BF16) for i in range(3)])
                ps = Rot([ps2("ps%d" % i, [128, 512]) for i in range(4)])
                pss = Rot([ps2("pss%d" % i, [128, 512]) for i in range(2)])
                of = Rot([sb2("of%d" % i, [128, 512]) for i in range(3)])
                ob = Rot([sb2("ob%d" % i, [128, 512], BF16) for i in range(3)])
                sq = Rot([sb2("sq%d" % i, [128, 512], BF16) for i in range(3)])
                rs = Rot([sb2("rs%d" % i, [128, 512]) for i in range(2)])
                cwTb = sb2("cwTb", [128, 4, 24]); cbTb = sb2("cbTb", [128, 24])
                halo_t = sb2("halo_t", [128, 24, 3])
                xiR = Rot([sb2("xiR%d" % i, [128, 516]) for i in range(3)])
                accR = Rot([sb2("accR%d" % i, [128, 512]) for i in range(3)])
                for k in range(4):
                    S.dma("sp", cwTb[:, k, :], conv_w[k, :].rearrange("(c p) -> p c", p=128), slow=True)
                S.dma("sp", cbTb[:], conv_b.rearrange("(c p) -> p c", p=128), slow=True)
                S.memset("dve", halo_t[:], 0.0)
                S.dma("sp", gT[:], hyb_norm.rearrange("(c p) -> p c", p=128), slow=True)
                S.dma("sp", gq[:], q_norm.rearrange("(p o) -> p o", o=1), slow=True)
                S.dma("sp", gk[:], k_norm.rearrange("(p o) -> p o", o=1), slow=True)
                S.ts("dve", gq[:], gq[:], 128.0 ** -0.5, None, ALU.mult)
                S.memset("dve", epsc[:], EPS)
                S.dma("sp", wdt[:], Wdt[:])
                jobs = []
                gjobs = []
                pend_tail = []
                STQ = "pool"
                for g in range(8):
                    T0 = 512 * g
                    xT = xnT.next()
                    jobs = []
                    gjobs.append(jobs)
                    for i in range(4):
                        t = xt.next()
                        jobs.append((lambda t=t, r0=T0 + 128 * i: S.dma("sp", t[:], x_loc[r0:r0 + 128, :]),
                                     lambda t=t, xT=xT, i=i: norm_transpose([t[:]], gT, xT[:, :, 128 * i:128 * i + 128], scr)))
                    nblk = 36 if g < 3 else NFM

                    def fm_compute(w, bi, xT=xT, T0=T0):
                        p = ps.next()
                        for c in range(16):
                            S.mm(p[:], w[:, c, :], xT[:, c, :], start=(c == 0), stop=(c == 15))
                        if pend_tail:
                            pend_tail.pop(0)()
                        c0 = fm_cols[bi]
                        if c0 >= C_Q:
                            isq = c0 < C_K
                            hd = (c0 - (C_Q if isq else C_K)) // 128
                            s_ = sq.next()
                            S.act(s_[:], p[:], AF.Square)

                            def tail(p=p, s_=s_, isq=isq, hd=hd):
                                p2 = pss.next()
                                S.mm(p2[:], cb[:, ONES, :], s_[:])
                                r = rs.next()
                                S.act(r[:], p2[:], AF.Ln, bias=epsc[:], scale=1.0 / 128)
                                S.act(r[:], r[:], AF.Exp, scale=-0.5)
                                o = ob.next()
                                S.stt("dve", o[:], p[:], (gq if isq else gk)[:, 0:1], r[:], ALU.mult, ALU.mult)
                                S.dma(STQ, (qT_s if isq else kT_s)[hd, :, T0:T0 + 512], o[:])
                            pend_tail.append(tail)
                        else:
                            cc = (c0 - C_X) // 128
                            xi = xiR.next()
                            S.copy("act", xi[:, 4:516], p[:])
                            S.copy("dve", xi[:, 1:4], halo_t[:, cc, :])
                            S.copy("dve", halo_t[:, cc, :], xi[:, 513:516])
                            ac = accR.next()
                            S.ts("dve", ac[:], xi[:, 1:513], cwTb[:, 0, cc:cc + 1], cbTb[:, cc:cc + 1], ALU.mult, ALU.add)
                            for k in range(1, 4):
                                S.stt("dve", ac[:], xi[:, 1 + k:1 + k + 512], cwTb[:, k, cc:cc + 1], ac[:], ALU.mult, ALU.add)

                            def tail(ac=ac, cc=cc):
                                o = ob.next()
                                S.act(o[:], ac[:], AF.Silu)
                                S.dma(STQ, xc_s[128 * cc:128 * cc + 128, T0:T0 + 512], o[:])
                            pend_tail.append(tail)

                    for bi in range(nblk):
                        w = wblk.next()
                        jobs.append((lambda w=w, bi=bi: S.dma("sp", w[:], Wfm[bi]),
                                     lambda w=w, bi=bi, f=fm_compute: f(w, bi)))

                    def tm_compute(w, bi, xT=xT, T0=T0):
                        while pend_tail:
                            pend_tail.pop(0)()
                        for i in range(4):
                            p = ps.next()
                            for c in range(16):
                                S.mm(p[:], xT[:, c, 128 * i:128 * i + 128], w[:, c, :], start=(c == 0), stop=(c == 15))
                            r0 = T0 + 128 * i
                            if bi < 4:
                                o = of.next()
                                S.copy("dve" if i % 2 else "act", o[:], p[:])
                                S.dma(STQ, z_s[r0:r0 + 128, 512 * bi:512 * bi + 512], o[:])
                            else:
                                o = ob.next()
                                S.copy("dve" if i % 2 else "act", o[:], p[:])
                                S.dma(STQ, v_s[r0:r0 + 128, 512 * (bi - 4):512 * (bi - 4) + 512], o[:])

                    for bi in range(8):
                        if bi < 4 and g < 3:
                            continue
                        w = wtm.next()
                        jobs.append((lambda w=w, bi=bi: S.dma("sp", w[:], Wtm[bi]),
                                     lambda w=w, bi=bi, f=tm_compute: f(w, bi)))

                    def dt_compute(xT=xT, T0=T0):
                        for i in range(4):
                            p = ps.next()
                            for c in range(16):
                                S.mm(p[:, 0:32], xT[:, c, 128 * i:128 * i + 128], wdt[:, c, :], start=(c == 0), stop=(c == 15))
                            o = of.next()
                            S.copy("dve", o[:, 0:32], p[:, 0:32])
                            S.dma(STQ, dt_s[T0 + 128 * i:T0 + 128 * i + 128, :], o[:, 0:32])
                    jobs.append((lambda: None, dt_compute))
                jobs = list(gjobs[0])
                for g in range(1, 8):
                    nj = gjobs[g][:4]
                    rest = gjobs[g][4:]
                    tail = jobs[-5:]
                    jobs = jobs[:-5]
                    for a in range(5):
                        jobs.append(tail[a])
                        if a < 4:
                            jobs.append(nj[a])
                    jobs.extend(rest)
                DEPTH = 2
                for j in range(min(DEPTH, len(jobs))):
                    jobs[j][0]()
                for j in range(len(jobs)):
                    if j + DEPTH < len(jobs):
                        jobs[j + DEPTH][0]()
                    jobs[j][1]()
                S.run()

        if "C" in phases:
            with contextlib.ExitStack() as es2:
                sb2 = lambda name, shape, dtype=F32: es2.enter_context(nc.sbuf_tensor(uniq(name), list(shape), dtype))
                ps2 = lambda name, shape, dtype=F32: es2.enter_context(nc.psum_tensor(uniq(name), list(shape), dtype))
                cwT = sb2("cwT", [128, 4, 24]); cbT = sb2("cbT", [128, 24])
                dtb = sb2("dtb", [128, 32]); a_b = sb2("a_b", [128, 32]); D_b = sb2("D_b", [128, 32])
                on_b = sb2("on_b", [128, 2048]); epsc = sb2("epsc", [128, 1])
                stT = [sb2("stT%d" % g, [128, 512]) for g in range(4)]
                stB = [sb2("stB%d" % g, [128, 512], BF16) for g in range(4)]
                xcR = Rot([sb2("xc%d" % i, [128, 24, 512], BF16) for i in range(2)])
                xs_tm = sb2("xs_tm", [128, 4, 2048], BF16)
                B_tm = sb2("B_tm", [128, 4, 512], BF16)
                xdt = sb2("xdt", [128, 4, 2048], BF16)
                xdts = sb2("xdts", [128, 4, 2048], BF16)
                dtv = sb2("dtv", [128, 4, 32]); da = sb2("da", [128, 4, 32]); acs = sb2("acs", [128, 4, 32])
                nacs = sb2("nacs", [128, 4, 32]); cdb = sb2("cdb", [128, 4, 32]); dte = sb2("dte", [128, 4, 32])
                ea = sb2("ea", [128, 4, 32]); w1 = sb2("w1", [128, 4, 32])
                cbm = Rot([sb2("cbm%d" % i, [128, 4, 128]) for i in range(2)])
                zt = Rot([sb2("zt%d" % i, [128, 2048]) for i in range(1)])
                sz = Rot([sb2("sz%d" % i, [128, 2048]) for i in range(1)])
                tda = Rot([sb2("tda%d" % i, [128, 128]) for i in range(4)])
                dec = Rot([sb2("dec%d" % i, [128, 128]) for i in range(4)])
                MT = Rot([sb2("MT%d" % i, [128, 128], BF16) for i in range(3)])
                t1 = Rot([sb2("t1_%d" % i, [128, 512]) for i in range(2)])
                t2 = Rot([sb2("t2_%d" % i, [128, 512]) for i in range(2)])
                ssq = Rot([sb2("ssq%d" % i, [128, 4]) for i in range(3)])
                junk = sb2("junkc", [128, 512], BF16)
                yn = Rot([sb2("yn%d" % i, [128, 512], BF16) for i in range(2)])
                yT = Rot([sb2("yT%d" % i, [128, 4, 128], BF16) for i in range(2)])
                cb_ps = ps2("cb_ps", [128, 4, 128])
                y_ps = Rot([ps2("y_ps%d" % i, [128, 512]) for i in range(1)])
                yo_ps = Rot([ps2("yo_ps%d" % i, [128, 512]) for i in range(1)])
                s_ps = Rot([ps2("s_ps%d" % i, [128, 512]) for i in range(1)])
                pT = Rot([ps2("pTc%d" % i, [128, 4, 128], BF16) for i in range(1)])
                at_ps = ps2("at_ps", [128, 2, 128]); acs_ps = at_ps[:, 0, :]; tot_ps = at_ps[:, 1, :]
                seg_ps = Rot([ps2("seg_ps%d" % i, [128, 128]) for i in range(2)])

                for k in range(4):
                    S.dma("sp", cwT[:, k, :], conv_w[k, :].rearrange("(c p) -> p c", p=128), slow=True)
                S.dma("sp", cbT[:], conv_b.rearrange("(c p) -> p c", p=128), slow=True)
                S.dma("sp", dtb[:], dt_bias.partition_broadcast(128))
                S.dma("sp", a_b[:], a_log.partition_broadcast(128))
                S.dma("sp", D_b[:], ssd_d.partition_broadcast(128))
                S.dma("sp", on_b[:], out_norm.partition_broadcast(128))
                S.act(a_b[:], a_b[:], AF.Exp)
                S.ts("dve", a_b[:], a_b[:], -1.0, None, ALU.mult)
                S.memset("dve", epsc[:], EPS)
                for g in range(4):
                    S.memset("dve", stT[g][:], 0.0)
                    S.memset("pool", stB[g][:], 0.0)
                ei = [0]

                def alt():
                    ei[0] += 1
                    return "dve" if ei[0] % 2 else "pool"

                xc_next = xcR.next()
                S.dma("sp", xc_next[:], xc_s[:, 0:512].rearrange("(c p) t -> p c t", p=128))
                for SC in CSC:
                    T0 = 512 * SC
                    xc = xc_next
                    if SC + 1 < 8:
                        xc_next = xcR.next()
                        S.dma("sp", xc_next[:], xc_s[:, T0 + 512:T0 + 1024].rearrange("(c p) t -> p c t", p=128))
                    if CDBG < 2:
                        continue
                    for ch in range(4):
                        for c4 in range(5):
                            p = pT.next()
                            for j in range(4):
                                S.tr(p[:, j, :], xc[:, 4 * c4 + j, 128 * ch:128 * ch + 128], cb[:, IDENT, :])
                            dst = xs_tm[:, ch, 512 * c4:512 * c4 + 512] if c4 < 4 else B_tm[:, ch, :]
                            S.copy("act" if c4 % 2 else "dve", dst, p[:].rearrange("p a b -> p (a b)"))
                    if CDBG < 2.05:
                        continue
                    S.dma("sp", dtv[:], dt_s[T0:T0 + 512, :].rearrange("(c p) h -> p c h", p=128))
                    S.tt("dve", dtv[:], dtv[:], dtb[:].unsqueeze(1).to_broadcast([128, 4, 32]), ALU.add)
                    S.act(dtv[:], dtv[:], AF.Exp)
                    S.act(dtv[:], dtv[:], AF.Ln, bias=1.0)
                    if CDBG == 2.1:
                        continue
                    S.tt("dve", da[:], dtv[:], a_b[:].unsqueeze(1).to_broadcast([128, 4, 32]), ALU.mult)
                    daf = da[:].rearrange("p c h -> p (c h)")
                    S.mm(acs_ps, cf[:, TRIU, :], daf)
                    S.mm(tot_ps, cf[:, ONES, :], daf)
                    if CDBG == 2.2:
                        continue
                    S.copy("dve", acs[:].rearrange("p c h -> p (c h)"), acs_ps)
                    S.ts("dve", nacs[:], acs[:], -1.0, None, ALU.mult)
                    S.copy("dve", w1[:].rearrange("p c h -> p (c h)"), tot_ps)
                    S.act(cdb[:], w1[:], AF.Exp)
                    S.tt("dve", dte[:], w1[:], acs[:], ALU.subtract)
                    S.act(dte[:], dte[:], AF.Exp)
                    S.act(ea[:], acs[:], AF.Exp)
                    S.tt("dve", w1[:], dtv[:], dte[:], ALU.mult)
                    for ch in range(4 if CDBG != 3 else 0):
                        xv = xs_tm[:, ch, :].rearrange("p (h q) -> p h q", q=64)
                        S.tt("dve", xdt[:, ch, :].rearrange("p (h q) -> p h q", q=64), xv,
                             dtv[:, ch, :].unsqueeze(2).to_broadcast([128, 32, 64]), ALU.mult)
                        S.tt("pool", xdts[:, ch, :].rearrange("p (h q) -> p h q", q=64), xv,
                             w1[:, ch, :].unsqueeze(2).to_broadcast([128, 32, 64]), ALU.mult)
                    if CDBG < 4:
                        continue
                    for ch in range(4):
                        lc = 4 * SC + ch
                        cs = slice(128 * ch, 128 * ch + 128)
                        if lc == OWN0:
                            for g in range(4):
                                S.ts("dve", stT[g][:], stT[g][:], flag_t[:, 0:1], None, ALU.mult)
                                S.copy("pool", stB[g][:], stT[g][:])
                        if lc >= HALO:
                            for g in range(4):
                                S.mm(cb_ps[:, g, :], xc[:, 16 + g, cs], xc[:, 20 + g, cs])
                            cm = cbm.next()
                            S.copy("dve", cm[:], cb_ps[:])
                            z_t = zt.next()
                            S.dma("sp", z_t[:], z_s[T0 + 128 * ch:T0 + 128 * ch + 128, :])
                            s_z = sz.next()
                            S.act(s_z[:], z_t[:], AF.Silu)
                            for g in range(4):
                                yp = y_ps.next(); yo = yo_ps.next()
                                S.mm(yo[:], xc[:, 20 + g, cs], stB[g][:])
                                a2 = t2.next()
                                S.tt("pool", a2[:].rearrange("p (h q) -> p h q", q=64),
                                     xs_tm[:, ch, 512 * g:512 * g + 512].rearrange("p (h q) -> p h q", q=64),
                                     D_b[:, 8 * g:8 * g + 8].unsqueeze(2).to_broadcast([128, 8, 64]), ALU.mult)
                                def s1(hh, g=g, ch=ch):
                                    h = 8 * g + hh
                                    td = tda.next()
                                    S.ts("dve", td[:], cf[:, TRIU, :], da[:, ch, h:h + 1], None, ALU.mult)
                                    sg = seg_ps.next()
                                    S.mm(sg[:], cf[:, ONES, :], td[:], start=True, stop=False)
                                    S.mm(sg[:], cf[:, IDENT, :], cf[:, NEGSSD, :], start=False, stop=True)
                                    dc = dec.next()
                                    S.act(dc[:], sg[:], AF.Exp, bias=nacs[:, ch, h:h + 1])
                                    return dc

                                def s2(hh, dc, g=g, ch=ch, yp=yp, cm=cm):
                                    h = 8 * g + hh
                                    m = MT.next()
                                    S.tt("dve", m[:], dc[:], cm[:, g, :], ALU.mult)
                                    S.mm(yp[:, 64 * hh:64 * hh + 64], m[:], xdt[:, ch, 64 * h:64 * h + 64])
                                dcs = {0: s1(0), 1: s1(1)}
                                for hh in range(8):
                                    if hh + 2 < 8:
                                        dcs[hh + 2] = s1(hh + 2)
                                    s2(hh, dcs.pop(hh))
                                a1 = t1.next()
                                S.tt("dve", a1[:].rearrange("p (h q) -> p h q", q=64), yo[:].rearrange("p (h q) -> p h q", q=64),
                                     ea[:, ch, 8 * g:8 * g + 8].unsqueeze(2).to_broadcast([128, 8, 64]), ALU.mult)
                                S.tt("dve", a1[:], a1[:], yp[:], ALU.add)
                                S.tt("dve", a1[:], a1[:], a2[:], ALU.add)
                                S.tt("dve", a1[:], a1[:], s_z[:, 512 * g:512 * g + 512], ALU.mult)
                                sq_ = ssq.next()
                                S.act(junk[:], a1[:], AF.Square, accum_out=sq_[:, 0:1])
                                S.act(sq_[:, 1:2], sq_[:, 0:1], AF.Ln, bias=epsc[:], scale=1.0 / 512)
                                S.act(sq_[:, 2:3], sq_[:, 1:2], AF.Exp, scale=-0.5)
                                y_n = yn.next()
                                S.stt("dve", y_n[:], a1[:], sq_[:, 2:3], on_b[:, 512 * g:512 * g + 512], ALU.mult, ALU.mult)
                                p = pT.next()
                                for j in range(4):
                                    S.tr(p[:, j, :], y_n[:, 128 * j:128 * j + 128], cb[:, IDENT, :])
                                y_t = yT.next()
                                S.copy("act", y_t[:], p[:])
                                col0 = (lc - HALO) * 128
                                S.dma("sp", mT_s[512 * g:512 * g + 512, col0:col0 + 128].rearrange("(j p) t -> p j t", p=128), y_t[:])
                        if lc < NT - 1:
                            for g in range(4):
                                sp_ = s_ps.next()
                                S.mm(sp_[:], B_tm[:, ch, 128 * g:128 * g + 128], xdts[:, ch, 512 * g:512 * g + 512])
                                S.tt("dve", stT[g][:].rearrange("p (h q) -> p h q", q=64), stT[g][:].rearrange("p (h q) -> p h q", q=64),
                                     cdb[:, ch, 8 * g:8 * g + 8].unsqueeze(2).to_broadcast([128, 8, 64]), ALU.mult)
                                S.tt("dve", stT[g][:], stT[g][:], sp_[:], ALU.add)
                                S.copy("act", stB[g][:], stT[g][:])
                S.run()

        if "D" in phases:
            with contextlib.ExitStack() as es2:
                sb2 = lambda name, shape, dtype=F32: es2.enter_context(nc.sbuf_tensor(uniq(name), list(shape), dtype))
                ps2 = lambda name, shape, dtype=F32: es2.enter_context(nc.psum_tensor(uniq(name), list(shape), dtype))
                kT = Rot([sb2("kT%d" % i, [128, SEQ], BF16) for i in range(2)])
                qT = Rot([sb2("qT%d" % i, [128, NQT * 128], BF16) for i in range(2)])
                vh = Rot([sb2("vh%d" % i, [128, 32, 128], BF16) for i in range(2)])
                m4f = sb2("m4f", [128, 4, 512]); m4b = sb2("m4b", [128, 4, 512], BF16)
                WW = 1024
                ztp = Rot([ps2("ztp%d" % i, [128, WW]) for i in range(3)])
                ypp = Rot([ps2("ypp%d" % i, [128, WW]) for i in range(1)])
                e_t = Rot([sb2("e_t%d" % i, [128, WW]) for i in range(3)])
                sp_t = Rot([sb2("sp_t%d" % i, [128, WW], BF16) for i in range(5)])
                w_t = Rot([sb2("w_t%d" % i, [128, WW], BF16) for i in range(4)])
                spacc = Rot([sb2("spacc%d" % i, [128, WW]) for i in range(2)])
                spab = Rot([sb2("spab%d" % i, [128, WW], BF16) for i in range(3)])
                yo_t = Rot([sb2("yo_t%d" % i, [128, WW], BF16) for i in range(2)])
                S.dma("sp", m4f[:], mask4.rearrange("k p j -> p k j"))
                S.copy("dve", m4b[:], m4f[:])
                cvd = Rot([sb2("cvd%d" % i, [128, 2048], BF16) for i in range(4)])
                bgq = list(bg_jobs) if "A" in phases else []

                def load_head(h):
                    k_ = kT.next(); q_ = qT.next(); v_ = vh.next()
                    S.dma("sp", k_[:], kT_s[h])
                    S.dma("sp", q_[:], qT_s[h, :, HALO * 128:])
                    S.dma("sp", v_[:], v_s[:, 128 * h:128 * h + 128].rearrange("(t p) d -> p t d", p=128))
                    return k_, q_, v_

                def sb_info(sbi):
                    if sbi == 0:
                        return 0, 128, HALO, HALO
                    ft = OWN0 + 4 * (sbi - 1)
                    return 128 + 512 * (sbi - 1), 512, ft, ft + 3

                def attn_group(h, k_, q_, v_, sbis):
                    segs = []
                    for si, sbi in enumerate(sbis):
                        q0, W, ft, kmax = sb_info(sbi)
                        segs.append(dict(q0=q0, W=W, ft=ft, kmax=kmax, off=512 * si))
                    kmax_all = max(g_["kmax"] for g_ in segs)
                    units = list(range(kmax_all, -1, -1))
                    yp = ypp.next()
                    sa = spacc.next()
                    state = {"sab": None}

                    def active(kb):
                        return [g_ for g_ in segs if kb <= g_["kmax"]]

                    def span(act):
                        lo = min(g_["off"] for g_ in act)
                        hi = max(g_["off"] + g_["W"] for g_ in act)
                        return slice(lo, hi)

                    def stageA(kb):
                        z = ztp.next()
                        act = active(kb)
                        for g_ in act:
                            cs = slice(g_["off"], g_["off"] + g_["W"])
                            S.mm(z[:, cs], k_[:, 128 * kb:128 * kb + 128], q_[:, g_["q0"]:g_["q0"] + g_["W"]], start=True, stop=False)
                            if kb >= g_["ft"]:
                                mk = cb[:, NEGMASK, :] if g_["W"] == 128 else m4b[:, kb - g_["ft"], :]
                                S.mm(z[:, cs], cb[:, IDENT, :], mk, start=False, stop=False)
                        bias = kbias if kb < OWN0 else zero_c
                        sl = span(act)
                        e = e_t.next()
                        S.act(e[:, sl], z[:, sl], AF.Exp, bias=bias[:])
                        sp = sp_t.next()
                        S.act(sp[:, sl], e[:, sl], AF.Ln, bias=1.0)
                        return z, sp, bias

                    def stageB(kb, z, sp, bias):
                        act = active(kb)
                        sl = span(act)
                        last = kb == 0
                        for g_ in act:
                            cs = slice(g_["off"], g_["off"] + g_["W"])
                            S.mm(z[:, cs], cb[:, NEGTRI, :], sp[:, cs], start=False, stop=(kb == g_["kmax"]))
                            if kb != g_["kmax"]:
                                S.mm(z[:, cs], cb[:, NEGONES, :], state["sab"][:, cs], start=False, stop=True)
                        if not last:
                            firsts = [g_ for g_ in act if kb == g_["kmax"]]
                            rest = [g_ for g_ in act if kb != g_["kmax"]]
                            for g_ in firsts:
                                cs = slice(g_["off"], g_["off"] + g_["W"])
                                S.copy("dve", sa[:, cs], sp[:, cs])
                            if rest:
                                rs_ = span(rest)
                                S.tt("dve", sa[:, rs_], sa[:, rs_], sp[:, rs_], ALU.add)
                            nb = spab.next()
                            S.copy("dve", nb[:, sl], sa[:, sl])
                            state["sab"] = nb
                        w = w_t.next()
                        S.act(w[:, sl], z[:, sl], AF.Exp, bias=bias[:])
                        return w

                    def stageC(kb, w):
                        for g_ in active(kb):
                            cs = slice(g_["off"], g_["off"] + g_["W"])
                            S.mm(yp[:, cs], v_[:, kb, :], w[:, cs], start=(kb == g_["kmax"]), stop=(kb == 0))

                    n_u = len(units)
                    pa = {0: stageA(units[0])}
                    if n_u > 1:
                        pa[1] = stageA(units[1])
                    pb = {0: stageB(units[0], *pa.pop(0))}
                    for ui in range(n_u):
                        if ui + 2 < n_u:
                            pa[ui + 2] = stageA(units[ui + 2])
                        if ui + 1 < n_u:
                            pb[ui + 1] = stageB(units[ui + 1], *pa.pop(ui + 1))
                        stageC(units[ui], pb.pop(ui))
                    yo = yo_t.next()
                    for g_ in segs:
                        cs = slice(g_["off"], g_["off"] + g_["W"])
                        S.copy("dve", yo[:, cs], yp[:, cs])
                        S.dma("sp", mT_s[2048 + 128 * h:2048 + 128 * h + 128, g_["q0"]:g_["q0"] + g_["W"]], yo[:, cs])

                nxt_head = load_head(0)
                for h in range(DH):
                    k_, q_, v_ = nxt_head
                    if h + 1 < DH:
                        nxt_head = load_head(h + 1)
                    for sbis in DGROUPS:
                        for _ in range(7):
                            if bgq:
                                conv_job(cvd, *bgq.pop(0))
                        attn_group(h, k_, q_, v_, sbis)
                while bgq:
                    conv_job(cvd, *bgq.pop(0))
                S.run()

        def mlp_block(l, x_tiles, B):
            nt_ = len(x_tiles)
            T = 128 * nt_
            T1 = min(T, 512)
            xT = B["xnT"]
            for i, xt_ in enumerate(x_tiles):
                norm_transpose([xt_], B["gTm"][l], xT[:, :, 128 * i:128 * i + 128], B["scr"])
            hT = B["hT"]
            for fb in range(64):
                w = B["wblk"].next()
                S.dma("sp", w[:], Wup[l, fb])
                p = B["ps"].next()
                p2 = B["psd"][3 + fb % 2] if T > 512 else None
                for c in range(16):
                    S.mm(p[:, :T1], w[:, c, :], xT[:, c, :T1], start=(c == 0), stop=(c == 15))
                    if p2 is not None:
                        S.mm(p2[:, :T - 512], w[:, c, :], xT[:, c, 512:T], start=(c == 0), stop=(c == 15))
                r = B["rl"].next()
                S.act(r[:, :T1], p[:, :T1], AF.Relu)
                if p2 is not None:
                    S.act(r[:, 512:T], p2[:, :T - 512], AF.Relu)
                S.tt("dve", hT[:, fb, :T], r[:, :T], r[:, :T], ALU.mult)
            for nb in range(4):
                ncol = slice(512 * nb, 512 * nb + 512)
                for kq in range(4):
                    wd = B["wbig"].next()
                    S.dma("sp", wd[:], Wdn[l, 2048 * kq:2048 * kq + 2048, ncol].rearrange("(c p) n -> p c n", p=128))
                    for i in range(nt_):
                        for c in range(16):
                            S.mm(B["psd"][i][:], hT[:, 16 * kq + c, 128 * i:128 * i + 128], wd[:, c, :],
                                 start=(kq == 0 and c == 0), stop=(kq == 3 and c == 15))
                for i, xt_ in enumerate(x_tiles):
                    S.tt("dve", xt_[:, ncol], xt_[:, ncol], B["psd"][i][:], ALU.add)

        def mlp_bufs(es2):
            sb2 = lambda name, shape, dtype=F32: es2.enter_context(nc.sbuf_tensor(uniq(name), list(shape), dtype))
            ps2 = lambda name, shape, dtype=F32: es2.enter_context(nc.psum_tensor(uniq(name), list(shape), dtype))
            B = {}
            B["gTm"] = [sb2("gTm%d" % l, [128, 16]) for l in range(2)]
            epsc = sb2("epsm", [128, 1])
            B["psd"] = [ps2("psd%d" % i, [128, 512]) for i in range(5)]
            B["ps"] = Rot([ps2("psm%d" % i, [128, 512]) for i in range(2)])
            B["scr"] = dict(ss=Rot([sb2("ssm%d" % i, [128, 4]) for i in range(4)]),
                            junk=sb2("junkm", [128, 2048], BF16), eps=epsc,
                            xn=Rot([sb2("xnm%d" % i, [128, 2048], BF16) for i in range(2)]),
                            pT=Rot([ps2("pTm%d" % i, [128, 4, 128], BF16) for i in range(1)]))
            B["xnT"] = sb2("xnTm", [128, 16, 640], BF16)
            B["hT"] = sb2("hTm", [128, 64, 640], BF16)
            B["wblk"] = Rot([sb2("wblkm%d" % i, [128, 16, 128], BF16) for i in range(3)])
            B["wbig"] = Rot([sb2("wbigm%d" % i, [128, 16, 512], BF16) for i in range(2)])
            B["rl"] = Rot([sb2("rlm%d" % i, [128, 640], BF16) for i in range(2)])
            B["xt"] = [sb2("xtm%d" % i, [128, 2048]) for i in range(5)]
            for l in range(2):
                S.dma("sp", B["gTm"][l][:], mlp_norm[l].rearrange("(c p) -> p c", p=128), slow=True)
            S.memset("dve", epsc[:], EPS)
            return B

        if "E" in phases:
            with contextlib.ExitStack() as es2:
                B = mlp_bufs(es2)
                groups = [[0, 1, 2, 3, 4], [5, 6, 7, 8], [9, 10, 11, 12], [13, 14, 15, 16]]
                for grp in groups:
                    nt_ = len(grp)
                    T = 128 * nt_
                    c0 = 128 * grp[0]
                    xts = [B["xt"][i][:] for i in range(nt_)]
                    for i, qt in enumerate(grp):
                        S.dma("sp", B["xt"][i][:], x_loc[(HALO + qt) * 128:(HALO + qt) * 128 + 128, :])
                    mT = B["hT"]
                    S.dma("sp", mT[:, 0:32, :T], mT_s[:, c0:c0 + T].rearrange("(c p) t -> p c t", p=128))
                    for nb in range(4):
                        ncol = slice(512 * nb, 512 * nb + 512)
                        for kh in range(2):
                            wd = B["wbig"].next()
                            S.dma("sp", wd[:], Wo[2048 * kh:2048 * kh + 2048, ncol].rearrange("(c p) n -> p c n", p=128))
                            for i in range(nt_):
                                for c in range(16):
                                    S.mm(B["psd"][i][:], mT[:, 16 * kh + c, 128 * i:128 * i + 128], wd[:, c, :],
                                         start=(kh == 0 and c == 0), stop=(kh == 1 and c == 15))
                        for i in range(nt_):
                            S.tt("dve", xts[i][:, ncol], xts[i][:, ncol], B["psd"][i][:], ALU.add)
                    mlp_block(0, xts, B)
                    for i, qt in enumerate(grp):
                        S.dma("sp", x2_s[qt * 128:qt * 128 + 128, :], xts[i])
                S.run()

        x3_s = scratch("x3_s", [2048, D], F32)
        if "F" in phases:
            with contextlib.ExitStack() as es2:
                sb2 = lambda name, shape, dtype=F32: es2.enter_context(nc.sbuf_tensor(uniq(name), list(shape), dtype))
                ps2 = lambda name, shape, dtype=F32: es2.enter_context(nc.psum_tensor(uniq(name), list(shape), dtype))
                gp_b = sb2("gp_b", [128, 2048]); pb_b = sb2("pb_b", [128, 2048]); psc_b = sb2("psc_b", [128, 2048])
                PP = sb2("PP", [128, 16, 128]); epsc = sb2("epsf", [128, 1])
                Wp_sb = sb2("Wp_sb", [128, 16, 512], BF16)
                xt = Rot([sb2("xtf%d" % i, [128, 2048]) for i in range(4)])
                hn = Rot([sb2("hnf%d" % i, [128, 2048]) for i in range(4)])
                junk = sb2("junkf", [128, 2048], BF16)
                ss = Rot([sb2("ssf%d" % i, [128, 4]) for i in range(3)])
                dT = Rot([sb2("dTf%d" % i, [128, 16, 128], BF16) for i in range(2)])
                tt_ = Rot([sb2("ttf%d" % i, [128, 512]) for i in range(3)])
                d_ps = Rot([ps2("d_ps%d" % i, [128, 4, 128]) for i in range(3)])
                y_ps = Rot([ps2("yf_ps%d" % i, [128, 512]) for i in range(2)])
                S.dma("sp", gp_b[:], pool_norm.partition_broadcast(128))
                S.dma("sp", pb_b[:], pool_b.partition_broadcast(128))
                S.dma("sp", psc_b[:], pool_scale.partition_broadcast(128))
                S.dma("sp", PP[:], poolP.rearrange("f g k p t -> p (f g k) t"))
                S.dma("sp", Wp_sb[:], Wp.rearrange("(c p) n -> p c n", p=128))
                S.memset("dve", epsc[:], EPS)

                def pnorm(row0):
                    x_ = xt.next()
                    S.dma("sp", x_[:], x2_s[row0:row0 + 128, :])
                    s_ = ss.next()
                    S.act(junk[:], x_[:], AF.Square, accum_out=s_[:, 0:1])
                    S.act(s_[:, 1:2], s_[:, 0:1], AF.Ln, bias=epsc[:], scale=1.0 / D)
                    S.act(s_[:, 2:3], s_[:, 1:2], AF.Exp, scale=-0.5)
                    h_ = hn.next()
                    S.stt("dve", h_[:], x_[:], s_[:, 2:3], gp_b[:], ALU.mult, ALU.mult)
                    return x_, h_

                _, hprev = pnorm(0)
                nxt_pn = pnorm(128)
                for i in range(16):
                    x_, hcur = nxt_pn
                    if i + 1 < 16:
                        nxt_pn = pnorm(128 * (i + 2))
                    fi = 0 if i == 0 else 1
                    d_ = dT.next()
                    for g in range(4):
                        dp = d_ps.next()
                        for j in range(4):
                            cs = slice(512 * g + 128 * j, 512 * g + 128 * j + 128)
                            S.mm(dp[:, j, :], hprev[:, cs], PP[:, (fi * 4 + g) * 2 + 0, :], start=True, stop=False)
                            S.mm(dp[:, j, :], hcur[:, cs], PP[:, (fi * 4 + g) * 2 + 1, :], start=False, stop=True)
                        S.copy("act", d_[:, 4 * g:4 * g + 4, :], dp[:])
                    for g in range(4):
                        yp = y_ps.next()
                        for kc in range(4):
                            S.mm(yp[:], d_[:, 4 * g + kc, :], Wp_sb[:, 4 * g + kc, :], start=(kc == 0), stop=(kc == 3))
                        ncol = slice(512 * g, 512 * g + 512)
                        t_ = tt_.next()
                        S.tt("dve", t_[:], yp[:], pb_b[:, ncol], ALU.add)
                        S.tt("dve", t_[:], t_[:], psc_b[:, ncol], ALU.mult)
                        S.tt("dve", x_[:, ncol], x_[:, ncol], t_[:], ALU.add)
                    S.dma("sp", x3_s[128 * i:128 * i + 128, :], x_[:])
                    hprev = hcur
                S.run()
            with contextlib.ExitStack() as es2:
                B = mlp_bufs(es2)
                for gi in range(4):
                    xts = [B["xt"][i][:] for i in range(4)]
                    for i in range(4):
                        S.dma("sp", B["xt"][i][:], x3_s[(4 * gi + i) * 128:(4 * gi + i) * 128 + 128, :])
                    mlp_block(1, xts, B)
                    for i in range(4):
                        S.dma("sp", out[(4 * gi + i) * 128:(4 * gi + i) * 128 + 128, :], xts[i])
                S.run()
    return nc


def make_consts():
    j = np.arange(128)[:, None]
    s = np.arange(128)[None, :]
    c = np.zeros((7, 128, 128), np.float32)
    c[0] = (j == s)
    c[1] = 1.0
    c[2] = -1.0 * (j >= s)
    c[3] = NEG * (j >= s)
    c[4] = 1.0 * (j <= s)
    c[5] = NEG * (j > s)
    c[6] = -1.0
    m4 = np.zeros((4, 128, 512), np.float32)
    for i in range(4):
        m4[i, :, :128 * i] = NEG
        m4[i, :, 128 * i:128 * i + 128] = c[3]
    return c, m4


def make_poolP(s):
    P = np.zeros((2, 4, 2, 128, 128), np.float32)
    tp = np.arange(128)[:, None]
    t = np.arange(128)[None, :]
    for fi in range(2):
        for gi, win in enumerate((2, 4, 8, 16)):
            start_of_seq = (fi == 0 and s == 0)
            cnt = np.minimum(t + 1, win) if start_of_seq else np.full_like(t, win)
            cur = ((tp <= t) & (t - tp < win)) / cnt.astype(np.float32) - (tp == t)
            prev = ((t - (tp - 128)) < win) / cnt.astype(np.float32)
            if start_of_seq:
                prev = np.zeros_like(prev)
            P[fi, gi, 0] = prev
            P[fi, gi, 1] = cur
    return P


_NC_CACHE = {}


def make_in_maps(inputs):
    x = np.asarray(inputs["x"], np.float32)
    consts, m4 = make_consts()
    shared = {
        "consts": consts, "mask4": m4,
        "hyb_norm": inputs["hyb_norm"][0], "hyb_w_in": inputs["hyb_w_in"][0],
        "ssd_conv_w": inputs["ssd_conv_w"][0], "ssd_conv_b": inputs["ssd_conv_b"][0],
        "ssd_dt_bias": inputs["ssd_dt_bias"][0], "ssd_a_log": inputs["ssd_a_log"][0],
        "ssd_d": inputs["ssd_d"][0], "ssd_out_norm": inputs["ssd_out_norm"][0],
        "sb_q_norm": inputs["sb_q_norm"][0], "sb_k_norm": inputs["sb_k_norm"][0],
        "hyb_w_out": inputs["hyb_w_out"][0], "pool_norm": inputs["pool_norm"][0],
        "pool_w": inputs["pool_w"][0].reshape(2048, 512), "pool_b": inputs["pool_b"][0],
        "pool_scale": inputs["pool_scale"][0], "mlp_norm": inputs["mlp_norm"],
        "mlp_w_up": inputs["mlp_w_up"], "mlp_w_down": inputs["mlp_w_down"],
    }
    shared = {k: np.ascontiguousarray(np.asarray(v, np.float32)) for k, v in shared.items()}
    in_maps = []
    for c in range(8):
        b, s = c // 2, c % 2
        xl = np.zeros((SEQ, D), np.float32)
        if s == 1:
            xl[:] = x[b]
        else:
            xl[2048:] = x[b, :2048]
        m = dict(shared)
        m["x_loc"] = xl
        m["flag"] = np.full((128, 1), float(s), np.float32)
        m["poolP"] = make_poolP(s)
        in_maps.append(m)
    return in_maps


def kernel(**inputs):
    if "nc" not in _NC_CACHE:
        _NC_CACHE["nc"] = build_nc()
    nc = _NC_CACHE["nc"]
    in_maps = make_in_maps(inputs)
    res = run_bass_kernel_spmd(nc, in_maps, core_ids=list(range(8)))
    out = np.zeros((4, SEQ, D), np.float32)
    for c in range(8):
        b, s = c // 2, c % 2
        out[b, 2048 * s:2048 * s + 2048] = res.results[c]["out"]
    return out
```
